# Optimizing a Trainium2 kernel written in Bass

```python
import jax, jax.numpy as jnp
from jax import lax
import numpy as np

D_MODEL = 2048
BATCH = 4
SEQ = 8192
DEPTH = 1
DEC_BATCH = 8
DEC_SEQ = 64
PAST_LEN = 2048

CHUNK = 64
MIX_WIDTH = D_MODEL
ATTN_WIDTH = MIX_WIDTH // 2
N_HEADS = 8
HEAD_DIM = ATTN_WIDTH // N_HEADS
IDX_HEADS = 16
IDX_DIM = 64
TOPK_MAX = 256
POOL_WIDTH = MIX_WIDTH - ATTN_WIDTH
POOL_WINDOWS = (2, 4, 8, 16)
POOL_GROUPS = len(POOL_WINDOWS)
POOL_GROUP_DIM = POOL_WIDTH // POOL_GROUPS
POOL_PAD = max(POOL_WINDOWS) - 1
D_FF = ((8 * D_MODEL + 3 * 256 - 1) // (3 * 256)) * 256
IN_SIZES = (ATTN_WIDTH, ATTN_WIDTH, ATTN_WIDTH, IDX_HEADS * IDX_DIM, IDX_DIM, IDX_HEADS, POOL_WIDTH)
IN_WIDTH = sum(IN_SIZES)
IN_SPLITS = [int(s) for s in np.cumsum(IN_SIZES)[:-1]]
Q_BLOCK = 128
NEG_INF = -1e30
EPS = 1e-6

kernel_name = "hymba_dsa_pool_stream_step"


def rmsnorm(x, g):
    x32 = x.astype(jnp.float32)
    y = x32 * lax.rsqrt(jnp.mean(x32 * x32, axis=-1, keepdims=True) + EPS) * g.astype(jnp.float32)
    return y.astype(x.dtype)


def ada_modulation(c, w_ada_l, b_ada_l):
    mod = jax.nn.silu(c) @ w_ada_l + b_ada_l
    return [m[:, None, :] for m in jnp.split(mod, 6, axis=-1)]


def split_proj(h, w_in_l):
    B, T, _ = h.shape
    p = jnp.einsum('btd,de->bte', h, w_in_l)
    q, k, v, qi, ki, wi, u = jnp.split(p, IN_SPLITS, axis=-1)
    q = q.reshape(B, T, N_HEADS, HEAD_DIM)
    k = k.reshape(B, T, N_HEADS, HEAD_DIM)
    v = v.reshape(B, T, N_HEADS, HEAD_DIM)
    qi = qi.reshape(B, T, IDX_HEADS, IDX_DIM)
    return q, k, v, qi, ki, wi, u


def dsa_attend(q, qi, wi, qpos, K, V, KI, kpos, topk):
    B, Q = q.shape[:2]
    f32 = jnp.float32
    dots = jnp.einsum('bqhd,bsd->bqhs', qi.astype(f32), KI.astype(f32)) * (IDX_DIM ** -0.5)
    score = jnp.einsum('bqh,bqhs->bqs', wi.astype(f32) * (IDX_HEADS ** -0.5), jax.nn.relu(dots))
    admissible = (kpos[None, :] // CHUNK) <= (qpos[:, None] // CHUNK)
    score = jnp.where(admissible[None], score, NEG_INF)
    top_val, top_idx = lax.top_k(score, topk)
    valid = top_val > 0.5 * NEG_INF
    Kg = jax.vmap(lambda kb, ib: kb[ib])(K, top_idx)
    Vg = jax.vmap(lambda vb, ib: vb[ib])(V, top_idx)
    logits = jnp.einsum('bqhd,bqkhd->bqhk', q.astype(f32), Kg.astype(f32)) * (HEAD_DIM ** -0.5)
    logits = jnp.where(valid[:, :, None, :], logits, NEG_INF)
    p = jax.nn.softmax(logits, axis=-1)
    o = jnp.einsum('bqhk,bqkhd->bqhd', p.astype(Vg.dtype), Vg)
    return o.reshape(B, Q, ATTN_WIDTH)


def dsa_prompt(q, k, v, qi, ki, wi, pos, topk):
    B, T = q.shape[:2]
    nb = T // Q_BLOCK

    def blockify(a):
        return jnp.moveaxis(a.reshape(a.shape[0], nb, Q_BLOCK, *a.shape[2:]), 1, 0)

    xs = (blockify(q), blockify(qi), blockify(wi), pos.reshape(nb, Q_BLOCK))
    out = lax.map(lambda blk: dsa_attend(blk[0], blk[1], blk[2], blk[3], k, v, ki, pos, topk), xs)
    return jnp.moveaxis(out, 0, 1).reshape(B, T, ATTN_WIDTH)


def pool_mixer(u, prefix, pos, w_pool_l, scale_l):
    B, T, _ = u.shape
    f32 = jnp.float32
    xp = jnp.concatenate([prefix, u], axis=1).astype(f32)
    cs = jnp.concatenate([jnp.zeros((B, 1, POOL_WIDTH), f32), jnp.cumsum(xp, axis=1)], axis=1)
    end = cs[:, POOL_PAD + 1:POOL_PAD + 1 + T]
    outs = []
    for g, w in enumerate(POOL_WINDOWS):
        sl = slice(g * POOL_GROUP_DIM, (g + 1) * POOL_GROUP_DIM)
        start = cs[:, POOL_PAD + 1 - w:POOL_PAD + 1 - w + T, sl]
        cnt = jnp.minimum(pos + 1, w).astype(f32)[None, :, None]
        outs.append((end[..., sl] - start) / cnt - u[..., sl].astype(f32))
    z = jnp.stack(outs, axis=2)
    y = jnp.einsum('btgc,gce->btge', z, w_pool_l.astype(f32)).reshape(B, T, POOL_WIDTH)
    return (y * scale_l.astype(f32)).astype(u.dtype)


def residual_update(x, attn_o, pool_o, mods, w_out_l, g2, w_gate_l, w_up_l, w_down_l):
    _, _, gate1, shift2, scale2, gate2 = mods
    mix = jnp.einsum('btm,md->btd', jnp.concatenate([attn_o, pool_o], axis=-1), w_out_l)
    x = x + gate1 * mix
    h2 = rmsnorm(x, g2) * (1.0 + scale2) + shift2
    ff = jax.nn.silu(h2 @ w_gate_l) * (h2 @ w_up_l)
    return x + gate2 * (ff @ w_down_l)


def setup_inputs(seed: int = 0) -> dict:
    key = jax.random.key(seed)
    ks = jax.random.split(key, 24)
    f32 = jnp.float32
    nrm = lambda k, shape, s: (jax.random.normal(k, shape, f32) * s)
    return {
        "x_prompt": nrm(ks[0], (BATCH, SEQ, D_MODEL), 1.0),
        "x_sample": nrm(ks[1], (DEC_BATCH, DEC_SEQ, D_MODEL), 1.0),
        "c_prompt": nrm(ks[2], (BATCH, D_MODEL), 1.0),
        "c_sample": nrm(ks[3], (DEC_BATCH, D_MODEL), 1.0),
        "cache_k": nrm(ks[4], (DEPTH, DEC_BATCH, PAST_LEN, N_HEADS, HEAD_DIM), 1.0),
        "cache_v": nrm(ks[5], (DEPTH, DEC_BATCH, PAST_LEN, N_HEADS, HEAD_DIM), 1.0),
        "cache_kidx": nrm(ks[6], (DEPTH, DEC_BATCH, PAST_LEN, IDX_DIM), 1.0),
        "state_pool": nrm(ks[7], (DEPTH, DEC_BATCH, POOL_PAD, POOL_WIDTH), 1.0),
        "w_ada": nrm(ks[8], (DEPTH, D_MODEL, 6 * D_MODEL), 0.3 * D_MODEL ** -0.5),
        "b_ada": nrm(ks[9], (DEPTH, 6 * D_MODEL), 0.02),
        "g_norm1": 1.0 + nrm(ks[10], (DEPTH, D_MODEL), 0.02),
        "w_in": nrm(ks[11], (DEPTH, D_MODEL, IN_WIDTH), D_MODEL ** -0.5),
        "w_pool": nrm(ks[12], (DEPTH, POOL_GROUPS, POOL_GROUP_DIM, POOL_GROUP_DIM), POOL_GROUP_DIM ** -0.5),
        "pool_scale": 1.0 + nrm(ks[13], (DEPTH, POOL_WIDTH), 0.02),
        "w_out": nrm(ks[14], (DEPTH, MIX_WIDTH, D_MODEL), MIX_WIDTH ** -0.5),
        "g_norm2": 1.0 + nrm(ks[15], (DEPTH, D_MODEL), 0.02),
        "w_gate": nrm(ks[16], (DEPTH, D_MODEL, D_FF), D_MODEL ** -0.5),
        "w_up": nrm(ks[17], (DEPTH, D_MODEL, D_FF), D_MODEL ** -0.5),
        "w_down": nrm(ks[18], (DEPTH, D_FF, D_MODEL), D_FF ** -0.5),
        "g_final": 1.0 + nrm(ks[19], (D_MODEL,), 0.02),
    }


def reference(x_prompt, x_sample, c_prompt, c_sample, cache_k, cache_v, cache_kidx, state_pool,
              w_ada, b_ada, g_norm1, w_in, w_pool, pool_scale, w_out, g_norm2, w_gate, w_up, w_down, g_final):
    B, T = x_prompt.shape[:2]
    Bs, Ts = x_sample.shape[:2]
    P = cache_k.shape[2]
    topk_p = min(TOPK_MAX, T // 4)
    topk_s = min(TOPK_MAX, (P + Ts) // 4)
    pos_p = jnp.arange(T, dtype=jnp.int32)
    pos_s = P + jnp.arange(Ts, dtype=jnp.int32)
    kpos_s = jnp.arange(P + Ts, dtype=jnp.int32)

    xp, xs = x_prompt, x_sample
    kp_l, vp_l, kip_l, pp_l = [], [], [], []
    ks_l, vs_l, kis_l, ps_l = [], [], [], []
    for l in range(DEPTH):
        mods = ada_modulation(c_prompt, w_ada[l], b_ada[l])
        h = rmsnorm(xp, g_norm1[l]) * (1.0 + mods[1]) + mods[0]
        q, k, v, qi, ki, wi, u = split_proj(h, w_in[l])
        attn_o = dsa_prompt(q, k, v, qi, ki, wi, pos_p, topk_p)
        prefix = jnp.zeros((B, POOL_PAD, POOL_WIDTH), u.dtype)
        pool_o = pool_mixer(u, prefix, pos_p, w_pool[l], pool_scale[l])
        xp = residual_update(xp, attn_o, pool_o, mods, w_out[l], g_norm2[l], w_gate[l], w_up[l], w_down[l])
        kp_l.append(k)
        vp_l.append(v)
        kip_l.append(ki)
        pp_l.append(jnp.concatenate([prefix, u], axis=1)[:, -POOL_PAD:])

        mods_s = ada_modulation(c_sample, w_ada[l], b_ada[l])
        hs = rmsnorm(xs, g_norm1[l]) * (1.0 + mods_s[1]) + mods_s[0]
        q_s, k_s, v_s, qi_s, ki_s, wi_s, u_s = split_proj(hs, w_in[l])
        K_all = jnp.concatenate([cache_k[l], k_s], axis=1)
        V_all = jnp.concatenate([cache_v[l], v_s], axis=1)
        KI_all = jnp.concatenate([cache_kidx[l], ki_s], axis=1)
        attn_s = dsa_attend(q_s, qi_s, wi_s, pos_s, K_all, V_all, KI_all, kpos_s, topk_s)
        pool_s = pool_mixer(u_s, state_pool[l], pos_s, w_pool[l], pool_scale[l])
        xs = residual_update(xs, attn_s, pool_s, mods_s, w_out[l], g_norm2[l], w_gate[l], w_up[l], w_down[l])
        ks_l.append(k_s)
        vs_l.append(v_s)
        kis_l.append(ki_s)
        ps_l.append(jnp.concatenate([state_pool[l], u_s], axis=1)[:, -POOL_PAD:])

    y_prompt = rmsnorm(xp, g_final)
    y_sample = rmsnorm(xs, g_final)
    return (y_prompt, y_sample,
            jnp.stack(kp_l), jnp.stack(vp_l), jnp.stack(kip_l), jnp.stack(pp_l),
            jnp.stack(ks_l), jnp.stack(vs_l), jnp.stack(kis_l), jnp.stack(ps_l))
```

```python
from contextlib import ExitStack
import numpy as np
import concourse.bass as bass
import concourse.mybir as mybir
from concourse.bass_utils import run_bass_kernel_spmd

dt = mybir.dt
F32 = dt.float32
BF16 = dt.bfloat16
U8 = dt.uint8
AF = mybir.ActivationFunctionType
ALU = mybir.AluOpType
AX = mybir.AxisListType

D = 2048
DC = 16
SEQ = 8192
NT = 256
NSUP = 16
DFF = 5632
FC = 44
NKS = 2112
VW = 136
EPS = 1e-6
NBIS = 26
ENGS = ("pe", "act", "dve", "pool", "sp")


class _Op:
    __slots__ = ("eng", "fn", "reads", "writes", "dma", "deps", "sig", "ticket", "dsem", "dval", "dprev")

    def __init__(self, eng, fn, reads, writes, dma):
        self.eng = eng
        self.fn = fn
        self.reads = reads
        self.writes = writes
        self.dma = dma
        self.deps = ()
        self.sig = False
        self.ticket = 0
        self.dsem = None
        self.dval = 0
        self.dprev = None


class Sched:
    def __init__(self, nc):
        self.nc = nc
        self.ops = []
        self.dma_slots = {"sp": 16, "act": 8, "pool": 4}
        self.regions = {}

    def overlaps(self, k, cache={}):
        r = self.regions.get(k)
        if r is None:
            return (k,)
        c = self._ovc.get(k)
        if c is None:
            c = tuple(k2 for k2, r2 in self.regions.items() if r2[0] == r[0] and r2[1] < r[2] and r[1] < r2[2])
            self._ovc[k] = c
        return c

    def op(self, eng, fn, r=(), w=()):
        self.ops.append(_Op(eng, fn, tuple(r), tuple(w), False))

    def dma(self, q, out, in_, r=(), w=(), **kw):
        def fn(e, out=out, in_=in_, kw=kw):
            return e.dma_start(out=out, in_=in_, **kw)
        self.ops.append(_Op(q, fn, tuple(r), tuple(w), True))

    def finalize(self, stack):
        nc = self.nc
        ops = self.ops
        last_w = {}
        readers = {}
        self._ovc = {}
        for i, op in enumerate(ops):
            deps = {}
            for k0 in op.reads:
                for k in self.overlaps(k0):
                    j = last_w.get(k)
                    if j is not None:
                        deps[j] = "RAW"
                    if self.regions.get(k, ("",))[0] == "P":
                        for j in readers.get(k, ()):
                            if ops[j].eng != op.eng and j not in deps:
                                deps[j] = "RAR"
            for k0 in op.writes:
                for k in self.overlaps(k0):
                    j = last_w.get(k)
                    if j is not None and j not in deps:
                        deps[j] = "WAW"
                    for j in readers.get(k, ()):
                        if j not in deps:
                            deps[j] = "WAR"
            keep = []
            for j, kind in deps.items():
                pj = ops[j]
                if pj.dma:
                    keep.append(j)
                elif (not op.dma) and pj.eng == op.eng:
                    if op.eng == "pe":
                        continue
                    keep.append(j)
                    pj.sig = True
                else:
                    keep.append(j)
                    pj.sig = True
            op.deps = keep
            for k in op.reads:
                lst = readers.get(k)
                if lst is None:
                    readers[k] = [i]
                else:
                    if not op.dma:
                        lst[:] = [x for x in lst if ops[x].dma or ops[x].eng != op.eng]
                    lst.append(i)
            for k in op.writes:
                last_w[k] = i
                readers[k] = []
        csem = {e: stack.enter_context(nc.semaphore("c_" + e)) for e in ("pe", "act", "dve", "pool")}
        dsems = {q: [stack.enter_context(nc.semaphore("d_%s%d" % (q, s))) for s in range(n)]
                 for q, n in self.dma_slots.items()}
        tick = {e: 0 for e in ENGS}
        dcount = {q: 0 for q in self.dma_slots}
        slot_last = {}
        for op in ops:
            if op.dma:
                n = dcount[op.eng]
                K = self.dma_slots[op.eng]
                op.dsem = dsems[op.eng][n % K]
                op.dval = 16 * (n // K + 1)
                op.dprev = slot_last.get((op.eng, n % K))
                slot_last[(op.eng, n % K)] = op
                dcount[op.eng] = n + 1
            elif op.sig:
                tick[op.eng] += 1
                op.ticket = tick[op.eng]
        self.stats = {"n_ops": len(ops), "ticks": dict(tick), "dmas": dict(dcount)}
        final_waits = [(o.dsem, o.dval) for o in slot_last.values()]

        def emit(name, e):
            waited = {}

            def wait(sem, val):
                key = id(sem)
                if waited.get(key, 0) < val:
                    e.wait_ge(sem, val)
                    waited[key] = val
            for op in ops:
                if op.eng != name:
                    continue
                need = {}
                for j in op.deps:
                    pj = ops[j]
                    sem, val = (pj.dsem, pj.dval) if pj.dma else (csem[pj.eng], pj.ticket)
                    if need.get(id(sem), (None, 0))[1] < val:
                        need[id(sem)] = (sem, val)
                if op.dma and op.dprev is not None:
                    sem, val = op.dprev.dsem, op.dprev.dval
                    if need.get(id(sem), (None, 0))[1] < val:
                        need[id(sem)] = (sem, val)
                for sem, val in need.values():
                    wait(sem, val)
                if op.dma:
                    op.fn(e).then_inc(op.dsem, 16)
                else:
                    ins = op.fn(e)
                    if op.sig:
                        ins.then_inc(csem[name], 1)
            if name == "sp":
                for sem, val in final_waits:
                    wait(sem, val)

        with nc.Block() as block:
            @block.tensor
            def _(e):
                emit("pe", e)

            @block.scalar
            def _(e):
                emit("act", e)

            @block.vector
            def _(e):
                emit("dve", e)

            @block.gpsimd
            def _(e):
                emit("pool", e)

            @block.sync
            def _(e):
                emit("sp", e)


class Arena:
    def __init__(self, t, size, sched):
        self.t = t
        self.size = size
        self.off = 0
        self.sched = sched

    def alloc(self, key, nbytes, dtype, pattern=None, **kw):
        off = (self.off + 63) // 64 * 64
        assert off + nbytes <= self.size, ("SBUF arena overflow", key, off, nbytes, self.size)
        assert key not in self.sched.regions, key
        self.sched.regions[key] = ("S", off, off + nbytes)
        self.off = off + nbytes
        ap = self.t[:, off:off + nbytes].bitcast(dtype)
        if pattern:
            ap = ap.rearrange(pattern, **kw)
        return ap


class Builder:
    def __init__(self, phases=("ada", "k", "q")):
        self.phases = phases
        self.nc = bass.Bass("TRN2", target_bir_lowering=False)
        self.S = Sched(self.nc)
        self.uid = 0

    def mm(self, out, lhsT, rhs, start, stop, r, w):
        self.S.op("pe", lambda e, o=out, l=lhsT, rh=rhs, s=start, t=stop: e.matmul(o, l, rh, start=s, stop=t), r, w)

    def tr(self, out, in_, ident, r, w):
        self.S.op("pe", lambda e, o=out, i=in_, d=ident: e.transpose(o, i, d), r, w)

    def act(self, out, in_, func, r, w, **kw):
        self.S.op("act", lambda e, o=out, i=in_, f=func, kw=kw: e.activation(o, i, f, **kw), r, w)

    def ts(self, eng, out, in0, s1, s2, op0, op1, r, w, accum_out=None):
        def fn(e, o=out, i=in0, s1=s1, s2=s2, op0=op0, op1=op1, a=accum_out):
            kw = {}
            if op1 is not None:
                kw["op1"] = op1
            if a is not None:
                kw["accum_out"] = a
            return e.tensor_scalar(o, i, s1, s2, op0, **kw)
        self.S.op(eng, fn, r, w)

    def tt(self, eng, out, in0, in1, op, r, w):
        self.S.op(eng, lambda e, o=out, a=in0, b=in1, p=op: e.tensor_tensor(o, a, b, p), r, w)

    def stt(self, out, in0, scalar, in1, op0, op1, r, w):
        self.S.op("dve", lambda e, o=out, a=in0, s=scalar, b=in1, p0=op0, p1=op1:
                  e.scalar_tensor_tensor(o, a, s, b, p0, p1), r, w)

    def cp(self, eng, out, in_, r, w):
        if eng == "act":
            self.S.op("act", lambda e, o=out, i=in_: e.copy(o, i), r, w)
        elif getattr(self, "cp_ts", False):
            self.ts(eng, out, in_, 1.0, None, ALU.mult, None, r, w)
        else:
            self.S.op(eng, lambda e, o=out, i=in_: e.tensor_copy(o, i), r, w)

    def ps(self, bank, cols=512):
        return self.psum[:, bank * 512: bank * 512 + cols]

    def psb(self, bank):
        return self.psum[:, bank * 512:(bank + 1) * 512].bitcast(BF16)

    def build(self):
        nc = self.nc
        S = self.S

        def din(name, shape, d=F32):
            return nc.dram_tensor(name, list(shape), d, kind="ExternalInput").ap()

        def dout(name, shape):
            return nc.dram_tensor(name, list(shape), F32, kind="ExternalOutput").ap()

        def dscr(name, shape, d):
            return nc.dram_tensor(name, list(shape), d, kind="Internal").ap()

        I = self.I = {}
        for name, shape in [
            ("xk", (SEQ, D)), ("xq", (NSUP * NT, D)), ("xh", (NSUP * 16, D)), ("xs", (64, D)),
            ("c2", (128, 32)), ("ck", (2048, 1024)), ("cv", (2048, 1024)), ("cki", (2048, 64)),
            ("spool", (16, 1024)), ("w_ada", (D, 6 * D)), ("b_ada", (1, 6 * D)), ("b_fm", (128, 96)),
            ("g1", (128, 16)), ("g2", (128, 16)), ("w_in", (D, 5200)), ("w_pool", (1024, 256)),
            ("pscale", (128, 8)), ("w_out", (D, D)), ("w_gate", (D, DFF)), ("w_up", (D, DFF)),
            ("w_down", (DFF, D)), ("gfin", (1, D)), ("negmask", (256, 512)), ("hflag", (128, 16)),
            ("invcnt", (128, NSUP * 64)), ("ident", (128, 128)),
        ]:
            I[name] = din(name, shape)
        O = self.O = {}
        for name, shape in [
            ("y", (NSUP * NT, D)), ("ys", (64, D)), ("ko", (SEQ, 1024)), ("vo", (SEQ, 1024)), ("kio", (SEQ, 64)),
            ("po", (16, 1024)), ("kso", (64, 1024)), ("vso", (64, 1024)), ("kiso", (64, 64)), ("pso", (16, 1024)),
        ]:
            O[name] = dout(name, shape)
        X = self.X = {}
        X["mods"] = dscr("mods", (2, 2 * D), F32)
        X["winq"] = dscr("winq", (12, 128, 16, 256), BF16)
        X["wout"] = dscr("wout", (4, 128, 16, 512), BF16)
        X["wg"] = dscr("wg", (22, 128, 16, 256), BF16)
        X["wu"] = dscr("wu", (22, 128, 16, 256), BF16)
        X["wd"] = dscr("wd", (4, 4, 128, 11, 512), BF16)
        X["kT"] = dscr("kT", (SEQ // 512, 128, 8, 512), BF16)
        X["v"] = dscr("v", (SEQ // 512, 128, 4, 8 * VW), BF16)
        X["kiT"] = dscr("kiT", (64, SEQ), BF16)
        X["kTs"] = dscr("kTs", (5, 128, 8, 512), BF16)
        X["vs"] = dscr("vs", (5, 128, 4, 8 * VW), BF16)
        X["kiTs"] = dscr("kiTs", (64, 2176), BF16)

        with ExitStack() as st:
            ARENA = 206 * 1024
            arena_t = st.enter_context(nc.sbuf_tensor("arena", [128, ARENA], U8))
            self.psum = st.enter_context(nc.psum_tensor("psum", [128, 4096], F32))
            A = self.A = Arena(arena_t, ARENA, S)
            for b_ in range(8):
                S.regions["ps%d" % b_] = ("P", b_ * 2048, (b_ + 1) * 2048)
            S.regions["ps0h0"] = ("P", 0, 1024)
            S.regions["ps0h1"] = ("P", 1024, 2048)
            S.regions["ps2h0"] = ("P", 4096, 5120)
            S.regions["ps2h1"] = ("P", 5120, 6144)
            self.alloc_persistent()
            self.emit_consts()
            self.emit_weight_casts()
            base = A.off
            if "ada" in self.phases:
                self.emit_ada()
            A.off = base
            if "k" in self.phases:
                self.emit_kphase()
            A.off = base
            if "nocast" not in self.phases:
                self.pop_casts(1000)
            if "q" in self.phases:
                self.emit_qphase()
            S.finalize(st)
        return nc

    def barrier(self):
        S = self.S
        keys = set()
        for op in S.ops:
            keys.update(op.reads)
            keys.update(op.writes)
        keys = sorted(keys)
        t = self.dummy
        S.op("pool", lambda e: e.memset(t[0:1, 0:1], 0.0), r=keys, w=keys + ["dummy"])

    def alloc_persistent(self):
        A = self.A
        self.ident_f = A.alloc("ident_f", 512, F32)
        self.ident_b = A.alloc("ident_b", 256, BF16)
        self.dummy = A.alloc("dummy", 64, F32)
        self.AB = A.alloc("AB", 2 * 4 * 16 * 4, F32, "p (r v c) -> p r v c", r=2, v=4)
        self.hflag = A.alloc("hflag", 64, F32)
        self.invcnt = A.alloc("invcnt", NSUP * 64 * 4, F32, "p (i g t) -> p i g t", i=NSUP, g=4)
        self.pscale = A.alloc("pscale", 32, F32)
        self.g1 = A.alloc("g1", 64, F32)
        self.g2 = A.alloc("g2", 64, F32)
        self.bfm = A.alloc("bfm", 96 * 4, F32)
        self.half = A.alloc("half", 64, F32)
        self.epsb = A.alloc("epsb", 64, F32)
        self.wwi = A.alloc("wwi", 16 * 16 * 2, BF16, "p (c n) -> p c n", c=16)
        self.wpool = A.alloc("wpool", 8 * 256 * 2, BF16, "p (g e) -> p g e", g=8)
        self.gate_b = [A.alloc("gate_b0", D * 4, F32), A.alloc("gate_b1", D * 4, F32)]
        self.gfin_b = A.alloc("gfin_b", D * 4, F32)
        self.stat = [A.alloc("stat%d" % i, 64, F32) for i in range(4)]
        self.junk = A.alloc("junk", D * 2, BF16)
        self.xn = [A.alloc("xn%d" % i, D * 2, BF16) for i in range(2)]

    def emit_consts(self):
        S = self.S
        I = self.I
        S.dma("sp", self.ident_f, I["ident"], w=["ident_f"])
        self.cp("dve", self.ident_b, self.ident_f, r=["ident_f"], w=["ident_b"])
        S.dma("sp", self.hflag[:, 0:16], I["hflag"], w=["hflag"])
        S.dma("sp", self.invcnt, I["invcnt"].rearrange("p (i g t) -> p i g t", i=NSUP, g=4), w=["invcnt"])
        S.dma("sp", self.pscale[:, 0:8], I["pscale"], w=["pscale"])
        S.dma("sp", self.g1[:, 0:16], I["g1"], w=["g1"])
        S.dma("sp", self.g2[:, 0:16], I["g2"], w=["g2"])
        S.dma("sp", self.bfm, I["b_fm"], w=["bfm"])
        S.dma("sp", self.gfin_b, I["gfin"][0:1, :].to_broadcast([128, D]), w=["gfin_b"])
        S.op("pool", lambda e: e.memset(self.half[:, 0:1], 0.5), w=["half"])
        S.op("pool", lambda e: e.memset(self.epsb[:, 0:1], EPS), w=["epsb"])
        S.dma("pool", self.wwi, I["w_in"][:, 4160:4176].rearrange("(c p) n -> p c n", p=128), w=["wwi"])
        S.dma("pool", self.wpool, I["w_pool"].rearrange("(g p) e -> p g e", p=128), w=["wpool"])

    def emit_weight_casts(self):
        I = self.I
        X = self.X
        jobs = self.cast_jobs = []

        class _J:
            @staticmethod
            def dma(q, out, in_, w):
                jobs.append((q, out, in_, w))
        S = _J
        col0 = [0, 256, 512, 768, 3072, 3328, 3584, 3840, 4176, 4432, 4688, 4944]
        for j in range(12):
            S.dma("pool", X["winq"][j], I["w_in"][:, col0[j]:col0[j] + 256].rearrange("(c p) n -> p c n", p=128),
                  w=["winq%d" % j])
        for n in range(4):
            S.dma("pool", X["wout"][n], I["w_out"][:, n * 512:(n + 1) * 512].rearrange("(c p) n -> p c n", p=128),
                  w=["wout%d" % n])
        for b in range(22):
            S.dma("pool", X["wg"][b], I["w_gate"][:, b * 256:(b + 1) * 256].rearrange("(c p) n -> p c n", p=128),
                  w=["wg%d" % b])
            S.dma("pool", X["wu"][b], I["w_up"][:, b * 256:(b + 1) * 256].rearrange("(c p) n -> p c n", p=128),
                  w=["wu%d" % b])
        for n in range(4):
            for q in range(4):
                S.dma("pool", X["wd"][n, q],
                      I["w_down"][q * 1408:(q + 1) * 1408, n * 512:(n + 1) * 512].rearrange("(k p) m -> p k m", p=128),
                      w=["wd%d_%d" % (n, q)])

    def emit_ada(self):
        S = self.S
        I = self.I
        A = self.A
        c2t = A.alloc("c2t", 128, F32)
        sc = A.alloc("sc", 128, F32)
        wa = [A.alloc("wa%d" % i, 16 * 512 * 4, F32, "p (c n) -> p c n", c=16) for i in range(2)]
        bb = [A.alloc("bb%d" % i, 2048, F32) for i in range(2)]
        mo = [A.alloc("mo%d" % i, 2048, F32) for i in range(2)]
        modfm = A.alloc("modfm", 4 * 16 * 2 * 4, F32, "p (v c r) -> p v c r", v=4, c=16)
        S.dma("sp", c2t, I["c2"], w=["c2t"])
        self.act(sc, c2t, AF.Silu, r=["c2t"], w=["sc"])
        vec_of_blk = {0: 0, 1: 0, 2: 0, 3: 0, 4: 1, 5: 1, 6: 1, 7: 1, 12: 2, 13: 2, 14: 2, 15: 2, 16: 3, 17: 3, 18: 3, 19: 3}
        gate_of_blk = {8: 0, 9: 0, 10: 0, 11: 0, 20: 1, 21: 1, 22: 1, 23: 1}
        for blk in range(24):
            b = blk % 2
            if "nocast" not in self.phases:
                self.pop_casts(2)
            S.dma("sp", wa[b], I["w_ada"][:, blk * 512:(blk + 1) * 512].rearrange("(c p) n -> p c n", p=128),
                  w=["wa%d" % b])
            if blk in gate_of_blk:
                g = gate_of_blk[blk]
                S.dma("sp", bb[b][0:2, :], I["b_ada"][0:1, blk * 512:(blk + 1) * 512].to_broadcast([2, 512]),
                      w=["bb%d" % b])
                bank = 2 + b
                for k in range(16):
                    self.mm(self.ps(bank)[0:2, :], sc[:, 2 * k:2 * k + 2], wa[b][:, k, :], k == 0, k == 15,
                            r=["sc", "wa%d" % b], w=["ps%d" % bank])
                self.tt("dve", mo[b][0:2, :], self.ps(bank)[0:2, :], bb[b][0:2, :], ALU.add,
                        r=["ps%d" % bank, "bb%d" % b], w=["mo%d" % b])
                off = g * D + (blk % 4) * 512
                S.dma("sp", self.X["mods"][0:2, off:off + 512], mo[b][0:2, :], r=["mo%d" % b], w=["mods"])
            else:
                v = vec_of_blk[blk]
                bank = 4 + b
                for m in range(4):
                    for k in range(16):
                        self.mm(self.ps(bank)[:, 2 * m:2 * m + 2], wa[b][:, k, m * 128:(m + 1) * 128],
                                sc[:, 2 * k:2 * k + 2], k == 0, k == 15,
                                r=["sc", "wa%d" % b], w=["ps%d" % bank])
                for m in range(4):
                    cc = (blk % 4) * 4 + m
                    gcol = blk * 4 + m
                    self.ts("dve", modfm[:, v, cc, :], self.ps(bank)[:, 2 * m:2 * m + 2],
                            self.bfm[:, gcol:gcol + 1], None, ALU.add, None,
                            r=["ps%d" % bank, "bfm"], w=["modfm"])
        for row in range(2):
            self.stt(self.AB[:, row, 0, :], modfm[:, 1, :, row], 1.0, self.g1[:, 0:16], ALU.add, ALU.mult,
                     r=["modfm", "g1"], w=["AB"])
            self.cp("dve", self.AB[:, row, 1, :], modfm[:, 0, :, row], r=["modfm"], w=["AB"])
            self.stt(self.AB[:, row, 2, :], modfm[:, 3, :, row], 1.0, self.g2[:, 0:16], ALU.add, ALU.mult,
                     r=["modfm", "g2"], w=["AB"])
            self.cp("dve", self.AB[:, row, 3, :], modfm[:, 2, :, row], r=["modfm"], w=["AB"])

    def pop_casts(self, n):
        if not hasattr(self, "cstage"):
            A = self.A
            o = A.off
            A.off = A.size - 2 * 16384 - 128
            self.cstage = [A.alloc("cstage%d" % i, 16384, BF16) for i in range(2)]
            A.off = o
            self.cast_i = 0
        for _ in range(n):
            if self.cast_jobs:
                q, out, in_, w = self.cast_jobs.pop(0)
                b = self.cast_i % 2
                self.cast_i += 1
                c, n_ = out.shape[1], out.shape[2]
                stg = self.cstage[b][:, 0:c * n_].rearrange("p (c n) -> p c n", c=c)
                self.S.dma("pool", stg, in_, w=["cstage%d" % b])
                prev = getattr(self, "cast_pending", None)
                if prev is not None:
                    self.S.dma("pool", prev[0], prev[1], r=[prev[2]], w=prev[3])
                self.cast_pending = (out, stg, "cstage%d" % b, w)
        if not self.cast_jobs and getattr(self, "cast_pending", None) is not None:
            prev = self.cast_pending
            self.S.dma("pool", prev[0], prev[1], r=[prev[2]], w=prev[3])
            self.cast_pending = None

    def load_gates(self, row):
        S = self.S
        for g in range(2):
            S.dma("sp", self.gate_b[g], self.X["mods"][row:row + 1, g * D:(g + 1) * D].to_broadcast([128, D]),
                  r=["mods"], w=["gate_b%d" % g])

    def norm(self, x_sb, ntok, row, which, hT_dst, xkey, hkey, sidx, banks=(1, 2)):
        self.norm_a(x_sb, ntok, xkey, sidx)
        self.norm_b(ntok, row, which, hT_dst, hkey, sidx, banks)

    def norm_a(self, x_sb, ntok, xkey, sidx):
        stt_ = self.stat[sidx]
        skey = "stat%d" % sidx
        xn = self.xn[sidx % 2]
        xnkey = "xn%d" % (sidx % 2)
        self.act(self.junk[:ntok, :], x_sb, AF.Square, r=[xkey], w=["junk", skey], accum_out=stt_[:ntok, 0:1])
        self.act(stt_[:ntok, 1:2], stt_[:ntok, 0:1], AF.Sqrt, r=[skey, "epsb"], w=[skey], scale=1.0 / D, bias=self.epsb[:ntok, 0:1])
        self.S.op("dve", lambda e, o=stt_[:ntok, 2:3], i=stt_[:ntok, 1:2]: e.reciprocal(o, i), r=[skey], w=[skey])
        self.ts("dve", xn[:ntok, :], x_sb, stt_[:ntok, 2:3], None, ALU.mult, None, r=[xkey, skey], w=[xnkey])

    def norm_b(self, ntok, row, which, hT_dst, hkey, sidx, banks=(1, 2)):
        xn = self.xn[sidx % 2]
        xnkey = "xn%d" % (sidx % 2)
        A_ = self.AB[:, row, 2 * which, :]
        B_ = self.AB[:, row, 2 * which + 1, :]
        for c in range(16):
            bank = banks[c // 8]
            tp = self.psb(bank)[:, (c % 8) * 128:(c % 8) * 128 + ntok]
            self.tr(tp, xn[:ntok, c * 128:(c + 1) * 128], self.ident_b[:ntok, :ntok], r=[xnkey, "ident_b"],
                    w=["ps%d" % bank])
            if c < 8:
                self.act(hT_dst(c), tp, AF.Identity, r=["ps%d" % bank, "AB"], w=[hkey],
                         scale=A_[:, c:c + 1], bias=B_[:, c:c + 1])
            else:
                self.ts("dve", hT_dst(c), tp, A_[:, c:c + 1], B_[:, c:c + 1], ALU.mult, ALU.add,
                        r=["ps%d" % bank, "AB"], w=[hkey])

    def emit_kphase(self):
        S = self.S
        I = self.I
        O = self.O
        X = self.X
        A = self.A
        wkv = A.alloc("wkv", 16 * 2112 * 2, BF16, "p (c n) -> p c n", c=16)
        S.dma("pool", wkv[:, :, 0:1024], I["w_in"][:, 1024:2048].rearrange("(c p) n -> p c n", p=128), w=["wkv"])
        S.dma("pool", wkv[:, :, 1024:2048], I["w_in"][:, 2048:3072].rearrange("(c p) n -> p c n", p=128), w=["wkv"])
        S.dma("pool", wkv[:, :, 2048:2112], I["w_in"][:, 4096:4160].rearrange("(c p) n -> p c n", p=128), w=["wkv"])
        xt = [A.alloc("xt%d" % i, D * 4, F32) for i in range(2)]
        hTk = [A.alloc("hTk%d" % i, 16 * 128 * 2, BF16, "p (c t) -> p c t", c=16) for i in range(2)]
        kf = [A.alloc("kf%d" % i, 1024 * 4, F32) for i in range(2)]
        vf = [A.alloc("vf%d" % i, 1024 * 4, F32) for i in range(2)]
        kif = [A.alloc("kif%d" % i, 64 * 4, F32) for i in range(2)]
        kb = [A.alloc("kb%d" % i, 1024 * 2, BF16) for i in range(2)]
        vb = [A.alloc("vb%d" % i, 8 * VW * 2, BF16, "p (h d) -> p h d", h=8) for i in range(2)]
        kib = [A.alloc("kib%d" % i, 64 * 2, BF16) for i in range(2)]
        kTt = [A.alloc("kTt%d" % i, 8 * 128 * 2, BF16, "p (h t) -> p h t", h=8) for i in range(2)]
        kiTt = [A.alloc("kiTt%d" % i, 128 * 2, BF16) for i in range(2)]
        for p in range(2):
            S.op("pool", lambda e, t=vb[p]: e.memset(t[:, :, 128:VW], 1.0), w=["vb%d" % p])

        def post(par, ntok, ksrc, vsrc, kisrc, srckeys, tok0, kT_d, v_d, kiT_d, outs, part="ab"):
            sfx = str(par)
            if part != "b":
                if ksrc is not None:
                    for blk in range(2):
                        self.cp("act", kf[par][:ntok, blk * 512:(blk + 1) * 512], ksrc(blk), r=["ps%d" % (3 + blk)],
                                w=["kf" + sfx])
                        self.cp("act", vf[par][:ntok, blk * 512:(blk + 1) * 512], vsrc(blk), r=["ps%d" % (5 + blk)],
                                w=["vf" + sfx])
                    self.cp("act", kif[par][:ntok, 0:64], kisrc, r=["ps7"], w=["kif" + sfx])
                if outs is not None:
                    S.dma("sp", outs[0], kf[par][:ntok, :], r=["kf" + sfx])
                    S.dma("sp", outs[1], vf[par][:ntok, :], r=["vf" + sfx])
                    S.dma("sp", outs[2], kif[par][:ntok, 0:64], r=["kif" + sfx])
                self.cp("dve", kb[par][:ntok, :], kf[par][:ntok, :], r=["kf" + sfx], w=["kb" + sfx])
                self.cp("pool", vb[par][:ntok, :, 0:128], vf[par][:ntok, :].rearrange("p (h d) -> p h d", h=8),
                        r=["vf" + sfx], w=["vb" + sfx])
                self.cp("dve", kib[par][:ntok, 0:64], kif[par][:ntok, 0:64], r=["kif" + sfx], w=["kib" + sfx])
                S.dma("sp", v_d[tok0 // 512][:ntok, (tok0 % 512) // 128, :], vb[par][:ntok].rearrange("p h d -> p (h d)"),
                      r=["vb" + sfx], w=["vscr"])
            if part == "a":
                return
            tpk = self.psb(0)
            for h in range(8):
                self.tr(tpk[:, h * 128:h * 128 + ntok], kb[par][:ntok, h * 128:(h + 1) * 128],
                        self.ident_b[:ntok, :ntok], r=["kb" + sfx, "ident_b"], w=["ps0"])
            self.cp("act", kTt[par][:, :, :ntok], tpk.rearrange("p (h t) -> p h t", h=8)[:, :, :ntok], r=["ps0"],
                    w=["kTt" + sfx])
            S.dma("sp", kT_d[tok0 // 512][:, :, tok0 % 512:tok0 % 512 + ntok], kTt[par][:, :, :ntok], r=["kTt" + sfx],
                  w=["kTscr"])
            tpi = self.psb(7)[0:64, 512:512 + ntok]
            self.tr(tpi, kib[par][:ntok, 0:64], self.ident_b[:ntok, :ntok], r=["kib" + sfx, "ident_b"], w=["ps7"])
            self.cp("dve", kiTt[par][0:64, :ntok], tpi, r=["ps7"], w=["kiTt" + sfx])
            S.dma("sp", kiT_d[:, tok0:tok0 + ntok], kiTt[par][0:64, :ntok], r=["kiTt" + sfx], w=["kiTscr"])

        def k_norm(t, x_rows, ntok, row, part="ab"):
            par = t % 2
            sfx = str(par)
            if x_rows is not None:
                S.dma("sp", xt[par][:ntok, :], x_rows, w=["xt" + sfx])
            if part != "b":
                self.norm_a(xt[par][:ntok, :], ntok, "xt" + sfx, par)
            if part != "a":
                self.norm_b(ntok, row, 0, lambda c: hTk[par][:, c, :ntok], "hTk" + sfx, par)

        def k_mm(t, ntok):
            par = t % 2
            sfx = str(par)
            for blk in range(5):
                ncols = 512 if blk < 4 else 64
                bank = 3 + blk
                for c in range(16):
                    self.mm(self.ps(bank)[:ntok, :ncols], hTk[par][:, c, :ntok], wkv[:, c, blk * 512:blk * 512 + ncols],
                            c == 0, c == 15, r=["hTk" + sfx, "wkv"], w=["ps%d" % bank])

        def k_post(t, ntok, tok0, kT_d, v_d, kiT_d, outs, part):
            post(t % 2, ntok, lambda blk: self.ps(3 + blk)[:ntok, :], lambda blk: self.ps(5 + blk)[:ntok, :],
                 self.ps(7)[:ntok, 0:64], None, tok0, kT_d, v_d, kiT_d, outs, part)

        nkt = self.n_ktiles if hasattr(self, "n_ktiles") else SEQ // 128

        def kouts(t):
            r0 = t * 128
            return (O["ko"][r0:r0 + 128, :], O["vo"][r0:r0 + 128, :], O["kio"][r0:r0 + 128, :])
        S.dma("sp", xt[0], I["xk"][0:128, :], w=["xt0"])
        if nkt > 1:
            S.dma("sp", xt[1], I["xk"][128:256, :], w=["xt1"])
        k_norm(0, None, 128, 0)
        for t in range(nkt):
            if "nocast" not in self.phases:
                self.pop_casts(2)
            if t + 1 < nkt:
                k_norm(t + 1, None, 128, 0, "a")
            if t + 2 < nkt:
                S.dma("sp", xt[t % 2], I["xk"][(t + 2) * 128:(t + 3) * 128, :], w=["xt%d" % (t % 2)])
            k_mm(t, 128)
            k_post(t, 128, t * 128, X["kT"], X["v"], X["kiT"], kouts(t), "a")
            if t + 1 < nkt:
                k_norm(t + 1, None, 128, 0, "b")
            if t >= 1:
                k_post(t - 1, 128, (t - 1) * 128, X["kT"], X["v"], X["kiT"], None, "b")
        k_post(nkt - 1, 128, (nkt - 1) * 128, X["kT"], X["v"], X["kiT"], None, "b")
        ncache = 0 if getattr(self, "skip_cache", False) else 16

        def cload(t):
            par = t % 2
            r0 = t * 128
            S.dma("sp", kf[par], I["ck"][r0:r0 + 128, :], w=["kf%d" % par])
            S.dma("sp", vf[par], I["cv"][r0:r0 + 128, :], w=["vf%d" % par])
            S.dma("sp", kif[par][:, 0:64], I["cki"][r0:r0 + 128, :], w=["kif%d" % par])
        if ncache:
            cload(0)
        for t in range(ncache):
            par = t % 2
            r0 = t * 128
            if t + 1 < ncache:
                cload(t + 1)
            post(par, 128, None, None, None, None, r0, X["kTs"], X["vs"], X["kiTs"], None)
        if not getattr(self, "skip_stile", False):
            k_norm(0, I["xs"][0:64, :], 64, 1)
            k_mm(0, 64)
            k_post(0, 64, 2048, X["kTs"], X["vs"], X["kiTs"],
                   (O["kso"][0:64, :], O["vso"][0:64, :], O["kiso"][0:64, :]), "ab")

    def emit_qphase(self):
        S = self.S
        I = self.I
        O = self.O
        X = self.X
        A = self.A
        NST = NT // 128
        xres = [A.alloc("xres%d" % i, D * 4, F32) for i in range(NST)]
        concatT = A.alloc("concatT", 16 * NT * 2, BF16, "p (c t) -> p c t", c=16)
        yt = [A.alloc("yt0", D * 4, F32)]
        _o = A.off
        A.off = _o - D * 4
        xh_t = A.alloc("xh_t", D * 4, F32)
        A.off = _o - D * 4
        oacc = A.alloc("oacc", 8 * 129 * 4, F32)
        A.off = _o
        nm = A.alloc("nm", NST * 512 * 4, F32, "p (s n) -> p s n", s=NST)
        for s_ in range(NST):
            S.dma("sp", nm[:, s_, :], I["negmask"][s_ * 128:(s_ + 1) * 128, :], w=["nm"])
        xbase = A.off
        hT = A.alloc("hT", 16 * NT * 2, BF16, "p (c t) -> p c t", c=16)
        hTh = A.alloc("hTh", 16 * 16 * 2, BF16, "p (c t) -> p c t", c=16)
        wq = [A.alloc("wq%d" % i, 16 * 256 * 2, BF16, "p (c n) -> p c n", c=16) for i in range(4)]
        xend = A.off
        A.off = xbase
        Ssb = A.alloc("Ssb", SEQ * 4, F32)
        junkA = A.alloc("junkA", SEQ // 2, U8)
        junkB = A.alloc("junkB", SEQ // 2, U8)
        xend = max(xend, A.off)
        A.off = xend
        ybase = A.off
        qT = A.alloc("qT", 8 * NT * 2, BF16, "p (h t) -> p h t", h=8)
        qiT = A.alloc("qiT", 16 * NT * 2, BF16, "p (h t) -> p h t", h=16)
        uT = A.alloc("uT", 8 * (NT + 16) * 4, F32, "p (c t) -> p c t", c=8)
        wi_sb = A.alloc("wi", NST * 16 * 4, F32, "p (s h) -> p s h", s=NST)
        dg = A.alloc("dg", 16 * 128 * 2, BF16, "p (h q) -> p h q", h=16)
        _d1 = A.off
        A.off = _d1 - 16 * 128 * 2
        pTm = [A.alloc("pTX%d" % i, 1024, BF16, "p (b q) -> p b q", b=4) for i in range(4)]
        A.off = _d1
        kic = [A.alloc("kic%d" % i, 1024, BF16) for i in range(2)]
        _k0 = A.off
        kTc = [A.alloc("kTc%d" % i, 8 * 512 * 2, BF16, "p (h k) -> p h k", h=8) for i in range(2)]
        Vc = [A.alloc("Vc%d" % i, 4 * 8 * VW * 2, BF16, "p (b n) -> p b n", b=4) for i in range(2)]
        _k1 = A.off
        A.off = _k0
        sA = A.alloc("sA", 8 * (NT + 16) * 4, F32, "p (c t) -> p c t", c=8)
        sB = A.alloc("sB", 8 * (NT + 16) * 4, F32, "p (c t) -> p c t", c=8)
        zT = A.alloc("zT", 8 * NT * 2, BF16, "p (c t) -> p c t", c=8)
        spl = A.alloc("spl", 1024 * 4, F32)
        ulast = A.alloc("ulast", 1024 * 4, F32)
        assert A.off <= _k1
        A.off = _k1
        mk = A.alloc("mk", 1024, BF16)
        mkT = [A.alloc("mkT%d" % i, 1024, BF16, "p (b q) -> p b q", b=4) for i in range(2)]
        Rr = [A.alloc("pTR%d" % i, 1024, BF16) for i in range(4)]
        pT = [Rr[i].rearrange("p (b q) -> p b q", b=4) for i in range(4)]
        ao = A.alloc("ao", 1024 * 2, BF16)
        bs = A.alloc("bs", 128, F32)
        bmid = A.alloc("bmid", 64, F32)
        bcA = A.alloc("bcA", 64, F32)
        bsB = A.alloc("bsB", 64, F32)
        bsel = A.alloc("bsel", 64, F32)
        top8 = A.alloc("top8", 64, F32)
        rec = A.alloc("rec", 64, F32)
        yend = A.off
        A.off = ybase
        actT = A.alloc("actT", FC * NT * 2, BF16, "p (f t) -> p f t", f=FC)
        wd = [A.alloc("wd%d" % i, 11 * 512 * 2, BF16, "p (k m) -> p k m", k=11) for i in range(2)]
        wo = [A.alloc("wo%d" % i, 16 * 512 * 2, BF16, "p (c n) -> p c n", c=16) for i in range(2)]
        tmpf = [A.alloc("tmpf%d" % i, 512 * 4, F32) for i in range(2)]
        sg = [A.alloc("sg%d" % i, NT * 4, F32) for i in range(2)]
        yend = max(yend, A.off)
        A.off = yend
        kY = ""

        def qtile(i, x_rows, ntok_all, row, halo, nk, kT_d, v_d, kiT_d, use_mask, y_rows, pool_out):
            nst = (ntok_all + 127) // 128
            nts = [min(128, ntok_all - s_ * 128) for s_ in range(nst)]
            self.load_gates(row) if (i == 0 or row == 1) else None
            for s_ in range(nst):
                nt_ = nts[s_]
                S.dma("sp", xres[s_][:nt_, :], x_rows[s_ * 128:s_ * 128 + nt_, :], w=["xres%d" % s_])
                self.norm(xres[s_][:nt_, :], nt_, row, 0, lambda c, s_=s_, nt_=nt_: hT[:, c, s_ * 128:s_ * 128 + nt_],
                          "xres%d" % s_, "hT", 2 + s_ % 2)
            if halo == "x":
                S.dma("sp", xh_t[0:16, :], I["xh"][i * 16:(i + 1) * 16, :], w=["xh_t"])
                self.norm(xh_t[0:16, :], 16, row, 0, lambda c: hTh[:, c, 0:16], "xh_t", "hTh", 0)
            sc_q = 128 ** -0.5
            sc_qi = (64 ** -0.5) * (16 ** -0.5)
            for j in range(12):
                b = j % 4
                S.dma("sp", wq[b], X["winq"][j], r=["winq%d" % j], w=["wq%d" % b])
                if j < 4:
                    for m in range(2):
                        bank = 3 + (2 * j + m) % 4
                        for c in range(16):
                            self.mm(self.ps(bank)[:, :ntok_all], wq[b][:, c, m * 128:(m + 1) * 128], hT[:, c, :ntok_all],
                                    c == 0, c == 15, r=["wq%d" % b, "hT"], w=["ps%d" % bank])
                        self.act(qT[:, 2 * j + m, :ntok_all], self.ps(bank)[:, :ntok_all], AF.Copy,
                                 r=["ps%d" % bank], w=[kY + "qT"], scale=sc_q)
                elif j < 8:
                    for m in range(4):
                        h = (j - 4) * 4 + m
                        bank = 3 + h % 4
                        for c in range(16):
                            self.mm(self.ps(bank)[0:64, :ntok_all], wq[b][:, c, m * 64:(m + 1) * 64], hT[:, c, :ntok_all],
                                    c == 0, c == 15, r=["wq%d" % b, "hT"], w=["ps%d" % bank])
                        self.ts("dve", qiT[0:64, h, :ntok_all], self.ps(bank)[0:64, :ntok_all], sc_qi, None, ALU.mult, None,
                                r=["ps%d" % bank], w=[kY + "qiT"])
                else:
                    for m in range(2):
                        ch = (j - 8) * 2 + m
                        bank = 3 + ch % 4
                        for c in range(16):
                            self.mm(self.ps(bank)[:, :ntok_all], wq[b][:, c, m * 128:(m + 1) * 128], hT[:, c, :ntok_all],
                                    c == 0, c == 15, r=["wq%d" % b, "hT"], w=["ps%d" % bank])
                        self.cp("act", uT[:, ch, 16:16 + ntok_all], self.ps(bank)[:, :ntok_all], r=["ps%d" % bank],
                                w=[kY + "uT"])
                        if halo == "x":
                            for c in range(16):
                                self.mm(self.ps(7)[:, 0:16], wq[b][:, c, m * 128:(m + 1) * 128], hTh[:, c, 0:16],
                                        c == 0, c == 15, r=["wq%d" % b, "hTh"], w=["ps7"])
                            self.ts("dve", uT[:, ch, 0:16], self.ps(7)[:, 0:16], self.hflag[:, i:i + 1], None, ALU.mult,
                                    None, r=["ps7", "hflag"], w=[kY + "uT"])
            if halo == "state":
                S.dma("sp", spl[0:16, :], I["spool"], w=[kY + "spl"])
                for ch in range(8):
                    self.tr(self.ps(7)[:, ch * 16:(ch + 1) * 16], spl[0:16, ch * 128:(ch + 1) * 128],
                            self.ident_f[0:16, 0:16], r=[kY + "spl", "ident_f"], w=["ps7"])
                self.cp("dve", uT[:, :, 0:16], self.ps(7)[:, 0:128].rearrange("p (c t) -> p c t", c=8), r=["ps7"],
                        w=[kY + "uT"])
            for s_ in range(nst):
                nt_ = nts[s_]
                for c in range(16):
                    self.mm(self.ps(7)[:nt_, 256:272], hT[:, c, s_ * 128:s_ * 128 + nt_], self.wwi[:, c, :],
                            c == 0, c == 15, r=["hT", "wwi"], w=["ps7"])
                self.cp("dve", wi_sb[:nt_, s_, :], self.ps(7)[:nt_, 256:272], r=["ps7"], w=[kY + "wi"])
            for s_ in range(nst):
                nq = nts[s_]
                q0 = s_ * 128
                for h in range(16):
                    self.ts("pool", dg[:nq, h, :nq], self.ident_b[:nq, :nq], wi_sb[:nq, s_, h:h + 1], None, ALU.mult, None,
                            r=["ident_b", kY + "wi"], w=[kY + "dg"])
                nblk = (nk + 511) // 512
                steps = []
                for kbk in range(nblk):
                    k0 = kbk * 512
                    wb = min(512, nk - k0)
                    halves = [(0, min(256, wb))] + ([(256, wb - 256)] if wb > 256 else [])
                    for hi_, (ho, hw) in enumerate(halves):
                        for hp in range(8):
                            steps.append((kbk, k0, wb, ho, hw, hp, hi_ == len(halves) - 1 and hp == 7))
                dbanks = [0, 2, 4, 6]
                LA = getattr(self, "idx_la", 2)
                NS = len(steps)
                for n in range(NS + LA):
                    if n < NS:
                        kbk, k0, wb, ho, hw, hp, _ = steps[n]
                        kb_ = kbk % 2
                        if ho == 0 and hp == 0:
                            S.dma("sp", kic[kb_][0:64, :wb], kiT_d[:, k0:k0 + wb], r=["kiTscr"], w=["kic%d" % kb_])
                        sl = n % 4
                        dbank = dbanks[sl]
                        dkey = "ps%d" % dbank
                        for j in range(2):
                            self.mm(self.ps(dbank)[:nq, j * 256:j * 256 + hw], qiT[0:64, 2 * hp + j, q0:q0 + nq],
                                    kic[kb_][0:64, ho:ho + hw], True, True, r=["qiT", "kic%d" % kb_], w=[dkey])
                        dps = self.ps(dbank)[:nq, :].rearrange("p (j k) -> p j k", j=2)[:, :, :hw]
                        rdst = Rr[sl][:nq, :].rearrange("p (j k) -> p j k", j=2)[:, :, :hw]
                        if sl < 2:
                            self.act(rdst, dps, AF.Relu, r=[dkey], w=["pTR%d" % sl])
                        else:
                            self.ts("dve", rdst, dps, 0.0, None, ALU.max, None, r=[dkey], w=["pTR%d" % sl])
                    m = n - LA
                    if m >= 0:
                        kbk, k0, wb, ho, hw, hp, lastb = steps[m]
                        sl = m % 4
                        sbank = 1 if kbk % 2 == 0 else 3
                        for j in range(2):
                            h = 2 * hp + j
                            self.mm(self.ps(sbank)[:nq, ho:ho + hw], dg[:nq, h, :nq], Rr[sl][:nq, j * 256:j * 256 + hw],
                                    h == 0, h == 15, r=["dg", "pTR%d" % sl], w=["ps%d" % sbank])
                        if lastb:
                            self.cp("dve", Ssb[:nq, k0:k0 + wb], self.ps(sbank)[:nq, :wb], r=["ps%d" % sbank], w=["Ssb"])
                bk = ["bs"]
                self.S.op("dve", lambda e, o=bs[:nq, 0:1], i_=Ssb[:nq, :nk]: e.tensor_reduce(o, i_, AX.X, ALU.min),
                          r=["Ssb"], w=bk)
                if use_mask:
                    self.tt("dve", Ssb[:nq, nk - 512:nk], Ssb[:nq, nk - 512:nk], nm[:nq, s_, :], ALU.add,
                            r=["Ssb", "nm"], w=["Ssb"])
                self.S.op("dve", lambda e, o=top8[:nq, 0:8], i_=Ssb[:nq, :nk]: e.max(o, i_), r=["Ssb"], w=["top8"])
                self.tt("dve", bs[:nq, 1:2], top8[:nq, 0:1], bs[:nq, 0:1], ALU.subtract, r=["top8"] + bk, w=bk)
                h1 = nk // 2 if nk >= 1024 else nk
                n2 = nk - h1
                for it in range(NBIS):
                    cfac = 0.5 ** (it + 1)
                    self.stt(bmid[:nq, 0:1], bs[:nq, 1:2], cfac, bs[:nq, 0:1], ALU.mult, ALU.add, r=bk, w=["bmid"])
                    self.ts("dve", junkA[:nq, :h1], Ssb[:nq, :h1], bmid[:nq, 0:1], 0.0, ALU.is_ge, ALU.add,
                            r=["bmid", "Ssb"], w=["bcA", "junkA"], accum_out=bcA[:nq, 0:1])
                    if n2 > 0:
                        self.act(junkB[:nq, :n2], Ssb[:nq, h1:nk], AF.Sign, r=["bmid", "Ssb"], w=["bsB", "junkB"],
                                 scale=-1.0, bias=bmid[:nq, 0:1], accum_out=bsB[:nq, 0:1])
                        self.stt(bsel[:nq, 0:1], bcA[:nq, 0:1], 2.0, bsB[:nq, 0:1], ALU.mult, ALU.subtract,
                                 r=["bcA", "bsB"], w=["bsel"])
                        self.ts("dve", bsel[:nq, 0:1], bsel[:nq, 0:1], 511.0 - n2, cfac, ALU.is_ge, ALU.mult,
                                r=["bsel"], w=["bsel"])
                    else:
                        self.ts("dve", bsel[:nq, 0:1], bcA[:nq, 0:1], 255.5, cfac, ALU.is_ge, ALU.mult, r=["bcA"], w=["bsel"])
                    self.stt(bs[:nq, 0:1], bsel[:nq, 0:1], bs[:nq, 1:2], bs[:nq, 0:1], ALU.mult, ALU.add,
                             r=bk + ["bsel"], w=bk)
                ngrp = (nk + 511) // 512
                pairs = [(g, h) for g in range(ngrp) for h in range(8)]
                NP = len(pairs)
                ginfo = {}
                for g in range(ngrp):
                    k0 = g * 512
                    wg_ = min(512, nk - k0)
                    nb = (wg_ + 127) // 128
                    ginfo[g] = (k0, wg_, nb, [min(128, wg_ - b_ * 128) for b_ in range(nb)])
                LAA = 3
                qbanks = [3, 4, 0, 1]
                for n in range(NP + LAA):
                    if n < NP:
                        g, h = pairs[n]
                        k0, wg_, nb, wbs = ginfo[g]
                        gb = g % 2
                        full = all(w_ == 128 for w_ in wbs)
                        if h == 0:
                            S.dma("sp", kTc[gb][:, :, :wg_], kT_d[g][:, :, :wg_], r=["kTscr"], w=["kTc%d" % gb])
                            if full:
                                S.dma("sp", Vc[gb][:, :nb, :], v_d[g][:, :nb, :], r=["vscr"], w=["Vc%d" % gb])
                            else:
                                for b_ in range(nb):
                                    S.dma("sp", Vc[gb][:wbs[b_], b_, :], v_d[g][:wbs[b_], b_, :],
                                          r=["vscr"], w=["Vc%d" % gb])
                            self.ts("dve", mk[:nq, :wg_], Ssb[:nq, k0:k0 + wg_], bs[:nq, 0:1], None, ALU.is_ge, None,
                                    r=["Ssb", "bs"], w=["mk"])
                            tpm = self.psb(2)
                            for b_ in range(nb):
                                self.tr(tpm[:wbs[b_], b_ * 128:b_ * 128 + nq], mk[:nq, b_ * 128:b_ * 128 + wbs[b_]],
                                        self.ident_b[:nq, :nq], r=["mk", "ident_b"], w=["ps2"])
                            if full:
                                self.cp("act", mkT[gb][:, :nb, :nq],
                                        tpm[:, :nb * 128].rearrange("p (b q) -> p b q", b=nb)[:, :, :nq], r=["ps2"],
                                        w=["mkT%d" % gb])
                            else:
                                for b_ in range(nb):
                                    self.cp("act", mkT[gb][:wbs[b_], b_, :nq], tpm[:wbs[b_], b_ * 128:b_ * 128 + nq],
                                            r=["ps2"], w=["mkT%d" % gb])
                        qb = qbanks[n % 4]
                        pb = n % 4
                        for b_ in range(nb):
                            self.mm(self.ps(qb)[:wbs[b_], b_ * 128:b_ * 128 + nq], kTc[gb][:, h, b_ * 128:b_ * 128 + wbs[b_]],
                                    qT[:, h, q0:q0 + nq], True, True, r=["kTc%d" % gb, "qT"], w=["ps%d" % qb])
                        if full:
                            self.act(pT[pb][:, :nb, :nq],
                                     self.ps(qb)[:, :nb * 128].rearrange("p (b q) -> p b q", b=nb)[:, :, :nq], AF.Exp,
                                     r=["ps%d" % qb], w=["pTR%d" % pb])
                            self.tt("pool" if h % 2 else "dve", pTm[pb][:, :nb, :nq], pT[pb][:, :nb, :nq], mkT[gb][:, :nb, :nq],
                                    ALU.mult, r=["pTR%d" % pb, "mkT%d" % gb], w=["pTX%d" % pb])
                        else:
                            for b_ in range(nb):
                                self.act(pT[pb][:wbs[b_], b_, :nq], self.ps(qb)[:wbs[b_], b_ * 128:b_ * 128 + nq], AF.Exp,
                                         r=["ps%d" % qb], w=["pTR%d" % pb])
                            for b_ in range(nb):
                                self.tt("dve", pTm[pb][:wbs[b_], b_, :nq], pT[pb][:wbs[b_], b_, :nq], mkT[gb][:wbs[b_], b_, :nq],
                                        ALU.mult, r=["pTR%d" % pb, "mkT%d" % gb], w=["pTX%d" % pb])
                    m = n - LAA
                    if m >= 0:
                        g, h = pairs[m]
                        k0, wg_, nb, wbs = ginfo[g]
                        gb = g % 2
                        pb = m % 4
                        ob = 5 + h // 3
                        oc = (h % 3) * 129
                        for b_ in range(nb):
                            self.mm(self.ps(ob)[:nq, oc:oc + 129], pTm[pb][:wbs[b_], b_, :nq],
                                    Vc[gb][:wbs[b_], b_, h * VW:h * VW + 129], b_ == 0, b_ == nb - 1,
                                    r=["pTX%d" % pb, "Vc%d" % gb], w=["ps%d" % ob])
                        if h == 7:
                            for ob in (5, 6, 7):
                                nh = 3 if ob < 7 else 2
                                c0 = (ob - 5) * 387
                                if g == 0:
                                    self.cp("dve", oacc[:nq, c0:c0 + nh * 129], self.ps(ob)[:nq, 0:nh * 129],
                                            r=["ps%d" % ob], w=["oacc"])
                                else:
                                    self.tt("dve", oacc[:nq, c0:c0 + nh * 129], oacc[:nq, c0:c0 + nh * 129],
                                            self.ps(ob)[:nq, 0:nh * 129], ALU.add, r=["ps%d" % ob, "oacc"], w=["oacc"])
                for h in range(8):
                    oc = h * 129
                    self.S.op("dve", lambda e, o=rec[:nq, h:h + 1], i_=oacc[:nq, oc + 128:oc + 129]: e.reciprocal(o, i_),
                              r=["oacc"], w=[kY + "rec"])
                    self.act(ao[:nq, h * 128:(h + 1) * 128], oacc[:nq, oc:oc + 128], AF.Copy,
                             r=["oacc", kY + "rec"], w=[kY + "ao"], scale=rec[:nq, h:h + 1])
                tpo = self.psb(2)
                for h in range(8):
                    self.tr(tpo[:, h * 128:h * 128 + nq], ao[:nq, h * 128:(h + 1) * 128], self.ident_b[:nq, :nq],
                            r=[kY + "ao", "ident_b"], w=["ps2"])
                self.cp("dve", concatT[:, 0:8, q0:q0 + nq], tpo.rearrange("p (h t) -> p h t", h=8)[:, :, :nq], r=["ps2"],
                        w=["concatT"])
            W = ntok_all + 16
            uk = kY + "uT"
            self.tt("pool", sA[:, :, 1:W], uT[:, :, 1:W], uT[:, :, 0:W - 1], ALU.add, r=[uk], w=[kY + "sA"])
            self.tt("pool", sB[:, 2:8, 3:W], sA[:, 2:8, 3:W], sA[:, 2:8, 1:W - 2], ALU.add, r=[kY + "sA"], w=[kY + "sB"])
            self.tt("pool", sA[:, 4:8, 7:W], sB[:, 4:8, 7:W], sB[:, 4:8, 3:W - 4], ALU.add, r=[kY + "sB"], w=[kY + "sA"])
            self.tt("pool", sB[:, 6:8, 15:W], sA[:, 6:8, 15:W], sA[:, 6:8, 7:W - 8], ALU.add, r=[kY + "sA"], w=[kY + "sB"])
            srcs = [sA, sB, sA, sB]
            for g in range(4):
                wdw = (2, 4, 8, 16)[g]
                self.stt(zT[:, 2 * g:2 * g + 2, :ntok_all], srcs[g][:, 2 * g:2 * g + 2, 16:W], 1.0 / wdw,
                         uT[:, 2 * g:2 * g + 2, 16:W], ALU.mult, ALU.subtract, r=[kY + "sA", kY + "sB", uk], w=[kY + "zT"])
                if halo == "x":
                    for cc in range(2):
                        ch = 2 * g + cc
                        self.tt("dve", sg[0][:, 0:16], srcs[g][:, ch, 16:32], self.invcnt[:, i, g, :], ALU.mult,
                                r=[kY + "sA", kY + "sB", "invcnt"], w=[kY + "sg0"])
                        self.tt("dve", zT[:, ch, 0:16], sg[0][:, 0:16], uT[:, ch, 16:32], ALU.subtract,
                                r=[kY + "sg0", uk], w=[kY + "zT"])
            for g in range(4):
                for ec in range(2):
                    bank = 3 + (2 * g + ec) % 4
                    for cc in range(2):
                        self.mm(self.ps(bank)[:, :ntok_all], self.wpool[:, 2 * g + cc, ec * 128:(ec + 1) * 128],
                                zT[:, 2 * g + cc, :ntok_all], cc == 0, cc == 1, r=["wpool", kY + "zT"], w=["ps%d" % bank])
                    self.ts("dve", concatT[:, 8 + 2 * g + ec, :ntok_all], self.ps(bank)[:, :ntok_all],
                            self.pscale[:, 2 * g + ec:2 * g + ec + 1], None, ALU.mult, None,
                            r=["ps%d" % bank, "pscale"], w=["concatT"])
            if pool_out is not None:
                for ch in range(8):
                    bk_ = 6 + ch // 4
                    self.tr(self.ps(bk_)[0:16, (ch % 4) * 128:(ch % 4 + 1) * 128], uT[:, ch, ntok_all:ntok_all + 16],
                            self.ident_f, r=[uk, "ident_f"], w=["ps%d" % bk_])
                for hh in range(2):
                    self.cp("dve", ulast[0:16, hh * 512:(hh + 1) * 512], self.ps(6 + hh)[0:16, :], r=["ps%d" % (6 + hh)],
                            w=[kY + "ulast"])
                S.dma("pool", pool_out, ulast[0:16, :], r=[kY + "ulast"])
            for n in range(4):
                b = n % 2
                S.dma("sp", wo[b], X["wout"][n], r=["wout%d" % n], w=[kY + "wo%d" % b])
                for s_ in range(nst):
                    nt_ = nts[s_]
                    bank = 3 + (n * nst + s_) % 4
                    tb = (n * nst + s_) % 2
                    for c in range(16):
                        self.mm(self.ps(bank)[:nt_, :], concatT[:, c, s_ * 128:s_ * 128 + nt_], wo[b][:, c, :],
                                c == 0, c == 15, r=["concatT", kY + "wo%d" % b], w=["ps%d" % bank])
                    self.tt("dve", tmpf[tb][:nt_, :], self.ps(bank)[:nt_, :], self.gate_b[0][:nt_, n * 512:(n + 1) * 512],
                            ALU.mult, r=["ps%d" % bank, "gate_b0"], w=[kY + "tmpf%d" % tb])
                    self.tt("pool", xres[s_][:nt_, n * 512:(n + 1) * 512], xres[s_][:nt_, n * 512:(n + 1) * 512],
                            tmpf[tb][:nt_, :], ALU.add, r=["xres%d" % s_, kY + "tmpf%d" % tb], w=["xres%d" % s_])
            for s_ in range(nst):
                nt_ = nts[s_]
                self.norm(xres[s_][:nt_, :], nt_, row, 1, lambda c, s_=s_, nt_=nt_: hT[:, c, s_ * 128:s_ * 128 + nt_],
                          "xres%d" % s_, "hT", 2 + s_ % 2)
            for fb in range(22):
                b = fb % 2
                S.dma("sp", wq[b], X["wg"][fb], r=["wg%d" % fb], w=["wq%d" % b])
                S.dma("sp", wq[2 + b], X["wu"][fb], r=["wu%d" % fb], w=["wq%d" % (2 + b)])
                for m in range(2):
                    f = fb * 2 + m
                    bg = 3 + (f % 2) * 2
                    bu = bg + 1
                    for c in range(16):
                        self.mm(self.ps(bg)[:, :ntok_all], wq[b][:, c, m * 128:(m + 1) * 128], hT[:, c, :ntok_all],
                                c == 0, c == 15, r=["wq%d" % b, "hT"], w=["ps%d" % bg])
                    for c in range(16):
                        self.mm(self.ps(bu)[:, :ntok_all], wq[2 + b][:, c, m * 128:(m + 1) * 128], hT[:, c, :ntok_all],
                                c == 0, c == 15, r=["wq%d" % (2 + b), "hT"], w=["ps%d" % bu])
                    self.act(sg[f % 2][:, :ntok_all], self.ps(bg)[:, :ntok_all], AF.Silu, r=["ps%d" % bg],
                             w=[kY + "sg%d" % (f % 2)])
                    self.tt("dve", actT[:, f, :ntok_all], sg[f % 2][:, :ntok_all], self.ps(bu)[:, :ntok_all], ALU.mult,
                            r=[kY + "sg%d" % (f % 2), "ps%d" % bu], w=[kY + "actT"])
            for n in range(4):
                for q in range(4):
                    b = (n * 4 + q) % 2
                    S.dma("sp", wd[b], X["wd"][n, q], r=["wd%d_%d" % (n, q)], w=[kY + "wd%d" % b])
                    for k in range(11):
                        kk = q * 11 + k
                        for s_ in range(nst):
                            nt_ = nts[s_]
                            bank = 3 + (n % 2) * 2 + s_
                            self.mm(self.ps(bank)[:nt_, :], actT[:, kk, s_ * 128:s_ * 128 + nt_], wd[b][:, k, :],
                                    kk == 0, kk == FC - 1, r=[kY + "actT", kY + "wd%d" % b], w=["ps%d" % bank])
                for s_ in range(nst):
                    nt_ = nts[s_]
                    bank = 3 + (n % 2) * 2 + s_
                    tb = s_ % 2
                    self.tt("dve", tmpf[tb][:nt_, :], self.ps(bank)[:nt_, :], self.gate_b[1][:nt_, n * 512:(n + 1) * 512],
                            ALU.mult, r=["ps%d" % bank, "gate_b1"], w=[kY + "tmpf%d" % tb])
                    self.tt("pool", xres[s_][:nt_, n * 512:(n + 1) * 512], xres[s_][:nt_, n * 512:(n + 1) * 512],
                            tmpf[tb][:nt_, :], ALU.add, r=["xres%d" % s_, kY + "tmpf%d" % tb], w=["xres%d" % s_])
            for s_ in range(nst):
                nt_ = nts[s_]
                stt_ = self.stat[s_ % 2]
                skey = "stat%d" % (s_ % 2)
                self.act(self.junk[:nt_, :], xres[s_][:nt_, :], AF.Square, r=["xres%d" % s_], w=["junk", skey],
                         accum_out=stt_[:nt_, 0:1])
                self.act(stt_[:nt_, 1:2], stt_[:nt_, 0:1], AF.Sqrt, r=[skey, "epsb"], w=[skey], scale=1.0 / D,
                         bias=self.epsb[:nt_, 0:1])
                self.S.op("dve", lambda e, o=stt_[:nt_, 2:3], i_=stt_[:nt_, 1:2]: e.reciprocal(o, i_), r=[skey], w=[skey])
                self.stt(yt[0][:nt_, :], xres[s_][:nt_, :], stt_[:nt_, 2:3], self.gfin_b[:nt_, :], ALU.mult, ALU.mult,
                         r=["xres%d" % s_, skey, "gfin_b"], w=["yt0"])
                S.dma("pool", y_rows[s_ * 128:s_ * 128 + nt_, :], yt[0][:nt_, :], r=["yt0"])

        nsup = self.n_sup if hasattr(self, "n_sup") else NSUP
        for i in range(nsup):
            nk = (2 * i + 2) * NT
            qtile(i, I["xq"][i * NT:(i + 1) * NT, :], NT, 0, "x", nk, X["kT"], X["v"], X["kiT"], True,
                  O["y"][i * NT:(i + 1) * NT, :], O["po"] if i == NSUP - 1 else None)
        if not hasattr(self, "skip_sample"):
            qtile(0, I["xs"], 64, 1, "state", NKS, X["kTs"], X["vs"], X["kiTs"], False, O["ys"], O["pso"])


_CACHE = {}


def _prep_inputs(inp):
    f = lambda a: np.ascontiguousarray(a, dtype=np.float32)
    x_prompt = inp["x_prompt"]
    w_ada = f(inp["w_ada"][0])
    b_ada = f(inp["b_ada"][0])
    shared = {
        "w_ada": w_ada, "b_ada": b_ada[None, :], "b_fm": f(b_ada.reshape(96, 128).T),
        "g1": f(inp["g_norm1"][0].reshape(16, 128).T), "g2": f(inp["g_norm2"][0].reshape(16, 128).T),
        "w_in": f(inp["w_in"][0]), "w_pool": f(inp["w_pool"][0].reshape(1024, 256)),
        "pscale": f(inp["pool_scale"][0].reshape(8, 128).T), "w_out": f(inp["w_out"][0]),
        "w_gate": f(inp["w_gate"][0]), "w_up": f(inp["w_up"][0]), "w_down": f(inp["w_down"][0]),
        "gfin": f(inp["g_final"][None, :]), "ident": np.eye(128, dtype=np.float32),
    }
    maps = []
    for c in range(8):
        b, hf = c // 2, c % 2
        xb = x_prompt[b]
        Js = [2 * i + hf for i in range(NSUP)]
        xq = np.concatenate([xb[J * NT:(J + 1) * NT] for J in Js], axis=0)
        xh = np.zeros((NSUP * 16, D), np.float32)
        hflag = np.zeros((128, 16), np.float32)
        invcnt = np.zeros((128, NSUP, 4, 16), np.float32)
        for i, J in enumerate(Js):
            if J > 0:
                xh[i * 16:(i + 1) * 16] = xb[J * NT - 16:J * NT]
                hflag[:, i] = 1.0
            for g, w in enumerate((2, 4, 8, 16)):
                pos = J * NT + np.arange(16)
                invcnt[:, i, g, :] = 1.0 / np.minimum(pos + 1, w)
        qch = (hf * NT + np.arange(NT)) // 64
        kch = np.arange(512) // 64
        negmask = np.where(kch[None, :] <= qch[:, None], 0.0, -1e30).astype(np.float32)
        c2 = np.stack([inp["c_prompt"][b].reshape(16, 128).T, inp["c_sample"][c].reshape(16, 128).T], axis=-1)
        spool = np.zeros((16, 1024), np.float32)
        spool[1:] = inp["state_pool"][0, c]
        m = dict(shared)
        m.update({
            "xk": f(xb), "xq": f(xq), "xh": xh, "xs": f(inp["x_sample"][c]), "c2": f(c2.reshape(128, 32)),
            "ck": f(inp["cache_k"][0, c].reshape(2048, 1024)), "cv": f(inp["cache_v"][0, c].reshape(2048, 1024)),
            "cki": f(inp["cache_kidx"][0, c]), "spool": spool, "negmask": negmask, "hflag": hflag,
            "invcnt": f(invcnt.reshape(128, NSUP * 64)),
        })
        maps.append(m)
    return maps


def _assemble(results):
    y = np.zeros((4, SEQ, D), np.float32)
    ys = np.zeros((8, 64, D), np.float32)
    kp = np.zeros((1, 4, SEQ, 8, 128), np.float32)
    vp = np.zeros((1, 4, SEQ, 8, 128), np.float32)
    kip = np.zeros((1, 4, SEQ, 64), np.float32)
    pp = np.zeros((1, 4, 15, 1024), np.float32)
    ksm = np.zeros((1, 8, 64, 8, 128), np.float32)
    vsm = np.zeros((1, 8, 64, 8, 128), np.float32)
    kis = np.zeros((1, 8, 64, 64), np.float32)
    pss = np.zeros((1, 8, 15, 1024), np.float32)
    for c in range(8):
        r = results[c]
        b, hf = c // 2, c % 2
        for i in range(NSUP):
            J = 2 * i + hf
            y[b, J * NT:(J + 1) * NT] = r["y"][i * NT:(i + 1) * NT]
        ys[c] = r["ys"]
        half = slice(hf * (SEQ // 2), (hf + 1) * (SEQ // 2))
        kp[0, b, half] = r["ko"][half].reshape(-1, 8, 128)
        vp[0, b, half] = r["vo"][half].reshape(-1, 8, 128)
        kip[0, b, half] = r["kio"][half]
        if hf == 1:
            pp[0, b] = r["po"][1:16]
        ksm[0, c] = r["kso"].reshape(64, 8, 128)
        vsm[0, c] = r["vso"].reshape(64, 8, 128)
        kis[0, c] = r["kiso"]
        pss[0, c] = r["pso"][1:16]
    return (y, ys, kp, vp, kip, pp, ksm, vsm, kis, pss)


def kernel(**inputs):
    inp = {k: np.asarray(v) for k, v in inputs.items()}
    if "nc" not in _CACHE:
        _CACHE["nc"] = Builder().build()
    nc = _CACHE["nc"]
    maps = _prep_inputs(inp)
    res = run_bass_kernel_spmd(nc, maps, core_ids=list(range(8)))
    return _assemble(res.results)
```

```python
from contextlib import ExitStack
import numpy as np
import concourse.bass as bass
import concourse.mybir as mybir
from concourse.bass_utils import run_bass_kernel_spmd

dt = mybir.dt
F32 = dt.float32
BF16 = dt.bfloat16
U8 = dt.uint8
AF = mybir.ActivationFunctionType
ALU = mybir.AluOpType
AX = mybir.AxisListType

D = 2048
DC = 16
SEQ = 8192
NT = 256
NSUP = 16
DFF = 5632
FC = 44
NKS = 2112
VW = 136
EPS = 1e-6
NBIS = 26
ENGS = ("pe", "act", "dve", "pool", "sp")


class _Op:
    __slots__ = ("eng", "fn", "reads", "writes", "dma", "deps", "sig", "ticket", "dsem", "dval", "dprev")

    def __init__(self, eng, fn, reads, writes, dma):
        self.eng = eng
        self.fn = fn
        self.reads = reads
        self.writes = writes
        self.dma = dma
        self.deps = ()
        self.sig = False
        self.ticket = 0
        self.dsem = None
        self.dval = 0
        self.dprev = None


class Sched:
    def __init__(self, nc):
        self.nc = nc
        self.ops = []
        self.dma_slots = {"sp": 16, "act": 8, "pool": 4}
        self.regions = {}

    def overlaps(self, k, cache={}):
        r = self.regions.get(k)
        if r is None:
            return (k,)
        c = self._ovc.get(k)
        if c is None:
            c = tuple(k2 for k2, r2 in self.regions.items() if r2[0] == r[0] and r2[1] < r[2] and r[1] < r2[2])
            self._ovc[k] = c
        return c

    def op(self, eng, fn, r=(), w=()):
        self.ops.append(_Op(eng, fn, tuple(r), tuple(w), False))

    def dma(self, q, out, in_, r=(), w=(), **kw):
        def fn(e, out=out, in_=in_, kw=kw):
            return e.dma_start(out=out, in_=in_, **kw)
        self.ops.append(_Op(q, fn, tuple(r), tuple(w), True))

    def finalize(self, stack):
        nc = self.nc
        ops = self.ops
        last_w = {}
        readers = {}
        self._ovc = {}
        for i, op in enumerate(ops):
            deps = {}
            for k0 in op.reads:
                for k in self.overlaps(k0):
                    j = last_w.get(k)
                    if j is not None:
                        deps[j] = "RAW"
                    if self.regions.get(k, ("",))[0] == "P":
                        for j in readers.get(k, ()):
                            if ops[j].eng != op.eng and j not in deps:
                                deps[j] = "RAR"
            for k0 in op.writes:
                for k in self.overlaps(k0):
                    j = last_w.get(k)
                    if j is not None and j not in deps:
                        deps[j] = "WAW"
                    for j in readers.get(k, ()):
                        if j not in deps:
                            deps[j] = "WAR"
            keep = []
            for j, kind in deps.items():
                pj = ops[j]
                if pj.dma:
                    keep.append(j)
                elif (not op.dma) and pj.eng == op.eng:
                    if op.eng == "pe":
                        continue
                    keep.append(j)
                    pj.sig = True
                else:
                    keep.append(j)
                    pj.sig = True
            op.deps = keep
            for k in op.reads:
                lst = readers.get(k)
                if lst is None:
                    readers[k] = [i]
                else:
                    if not op.dma:
                        lst[:] = [x for x in lst if ops[x].dma or ops[x].eng != op.eng]
                    lst.append(i)
            for k in op.writes:
                last_w[k] = i
                readers[k] = []
        csem = {e: stack.enter_context(nc.semaphore("c_" + e)) for e in ("pe", "act", "dve", "pool")}
        dsems = {q: [stack.enter_context(nc.semaphore("d_%s%d" % (q, s))) for s in range(n)]
                 for q, n in self.dma_slots.items()}
        tick = {e: 0 for e in ENGS}
        dcount = {q: 0 for q in self.dma_slots}
        slot_last = {}
        for op in ops:
            if op.dma:
                n = dcount[op.eng]
                K = self.dma_slots[op.eng]
                op.dsem = dsems[op.eng][n % K]
                op.dval = 16 * (n // K + 1)
                op.dprev = slot_last.get((op.eng, n % K))
                slot_last[(op.eng, n % K)] = op
                dcount[op.eng] = n + 1
            elif op.sig:
                tick[op.eng] += 1
                op.ticket = tick[op.eng]
        self.stats = {"n_ops": len(ops), "ticks": dict(tick), "dmas": dict(dcount)}
        final_waits = [(o.dsem, o.dval) for o in slot_last.values()]

        def emit(name, e):
            waited = {}

            def wait(sem, val):
                key = id(sem)
                if waited.get(key, 0) < val:
                    e.wait_ge(sem, val)
                    waited[key] = val
            for op in ops:
                if op.eng != name:
                    continue
                need = {}
                for j in op.deps:
                    pj = ops[j]
                    sem, val = (pj.dsem, pj.dval) if pj.dma else (csem[pj.eng], pj.ticket)
                    if need.get(id(sem), (None, 0))[1] < val:
                        need[id(sem)] = (sem, val)
                if op.dma and op.dprev is not None:
                    sem, val = op.dprev.dsem, op.dprev.dval
                    if need.get(id(sem), (None, 0))[1] < val:
                        need[id(sem)] = (sem, val)
                for sem, val in need.values():
                    wait(sem, val)
                if op.dma:
                    op.fn(e).then_inc(op.dsem, 16)
                else:
                    ins = op.fn(e)
                    if op.sig:
                        ins.then_inc(csem[name], 1)
            if name == "sp":
                for sem, val in final_waits:
                    wait(sem, val)

        with nc.Block() as block:
            @block.tensor
            def _(e):
                emit("pe", e)

            @block.scalar
            def _(e):
                emit("act", e)

            @block.vector
            def _(e):
                emit("dve", e)

            @block.gpsimd
            def _(e):
                emit("pool", e)

            @block.sync
            def _(e):
                emit("sp", e)


class Arena:
    def __init__(self, t, size, sched):
        self.t = t
        self.size = size
        self.off = 0
        self.sched = sched

    def alloc(self, key, nbytes, dtype, pattern=None, **kw):
        off = (self.off + 63) // 64 * 64
        assert off + nbytes <= self.size, ("SBUF arena overflow", key, off, nbytes, self.size)
        assert key not in self.sched.regions, key
        self.sched.regions[key] = ("S", off, off + nbytes)
        self.off = off + nbytes
        ap = self.t[:, off:off + nbytes].bitcast(dtype)
        if pattern:
            ap = ap.rearrange(pattern, **kw)
        return ap


class Builder:
    def __init__(self, phases=("ada", "k", "q")):
        self.phases = phases
        self.nc = bass.Bass("TRN2", target_bir_lowering=False)
        self.S = Sched(self.nc)
        self.uid = 0

    def mm(self, out, lhsT, rhs, start, stop, r, w):
        self.S.op("pe", lambda e, o=out, l=lhsT, rh=rhs, s=start, t=stop: e.matmul(o, l, rh, start=s, stop=t), r, w)

    def tr(self, out, in_, ident, r, w):
        self.S.op("pe", lambda e, o=out, i=in_, d=ident: e.transpose(o, i, d), r, w)

    def act(self, out, in_, func, r, w, **kw):
        self.S.op("act", lambda e, o=out, i=in_, f=func, kw=kw: e.activation(o, i, f, **kw), r, w)

    def ts(self, eng, out, in0, s1, s2, op0, op1, r, w, accum_out=None):
        def fn(e, o=out, i=in0, s1=s1, s2=s2, op0=op0, op1=op1, a=accum_out):
            kw = {}
            if op1 is not None:
                kw["op1"] = op1
            if a is not None:
                kw["accum_out"] = a
            return e.tensor_scalar(o, i, s1, s2, op0, **kw)
        self.S.op(eng, fn, r, w)

    def tt(self, eng, out, in0, in1, op, r, w):
        self.S.op(eng, lambda e, o=out, a=in0, b=in1, p=op: e.tensor_tensor(o, a, b, p), r, w)

    def stt(self, out, in0, scalar, in1, op0, op1, r, w):
        self.S.op("dve", lambda e, o=out, a=in0, s=scalar, b=in1, p0=op0, p1=op1:
                  e.scalar_tensor_tensor(o, a, s, b, p0, p1), r, w)

    def cp(self, eng, out, in_, r, w):
        if eng == "act":
            self.S.op("act", lambda e, o=out, i=in_: e.copy(o, i), r, w)
        elif getattr(self, "cp_ts", False):
            self.ts(eng, out, in_, 1.0, None, ALU.mult, None, r, w)
        else:
            self.S.op(eng, lambda e, o=out, i=in_: e.tensor_copy(o, i), r, w)

    def ps(self, bank, cols=512):
        return self.psum[:, bank * 512: bank * 512 + cols]

    def psb(self, bank):
        return self.psum[:, bank * 512:(bank + 1) * 512].bitcast(BF16)

    def build(self):
        nc = self.nc
        S = self.S

        def din(name, shape, d=F32):
            return nc.dram_tensor(name, list(shape), d, kind="ExternalInput").ap()

        def dout(name, shape):
            return nc.dram_tensor(name, list(shape), F32, kind="ExternalOutput").ap()

        def dscr(name, shape, d):
            return nc.dram_tensor(name, list(shape), d, kind="Internal").ap()

        I = self.I = {}
        for name, shape in [
            ("xk", (SEQ, D)), ("xq", (NSUP * NT, D)), ("xh", (NSUP * 16, D)), ("xs", (64, D)),
            ("c2", (128, 32)), ("ck", (2048, 1024)), ("cv", (2048, 1024)), ("cki", (2048, 64)),
            ("spool", (16, 1024)), ("w_ada", (D, 6 * D)), ("b_ada", (1, 6 * D)), ("b_fm", (128, 96)),
            ("g1", (128, 16)), ("g2", (128, 16)), ("w_in", (D, 5200)), ("w_pool", (1024, 256)),
            ("pscale", (128, 8)), ("w_out", (D, D)), ("w_gate", (D, DFF)), ("w_up", (D, DFF)),
            ("w_down", (DFF, D)), ("gfin", (1, D)), ("negmask", (256, 512)), ("hflag", (128, 16)),
            ("invcnt", (128, NSUP * 64)), ("ident", (128, 128)),
        ]:
            I[name] = din(name, shape)
        O = self.O = {}
        for name, shape in [
            ("y", (NSUP * NT, D)), ("ys", (64, D)), ("ko", (SEQ, 1024)), ("vo", (SEQ, 1024)), ("kio", (SEQ, 64)),
            ("po", (16, 1024)), ("kso", (64, 1024)), ("vso", (64, 1024)), ("kiso", (64, 64)), ("pso", (16, 1024)),
        ]:
            O[name] = dout(name, shape)
        X = self.X = {}
        X["mods"] = dscr("mods", (2, 2 * D), F32)
        X["winq"] = dscr("winq", (12, 128, 16, 256), BF16)
        X["wout"] = dscr("wout", (4, 128, 16, 512), BF16)
        X["wg"] = dscr("wg", (22, 128, 16, 256), BF16)
        X["wu"] = dscr("wu", (22, 128, 16, 256), BF16)
        X["wd"] = dscr("wd", (4, 4, 128, 11, 512), BF16)
        X["kT"] = dscr("kT", (SEQ // 512, 128, 8, 512), BF16)
        X["v"] = dscr("v", (SEQ // 512, 128, 4, 8 * VW), BF16)
        X["kiT"] = dscr("kiT", (64, SEQ), BF16)
        X["kTs"] = dscr("kTs", (5, 128, 8, 512), BF16)
        X["vs"] = dscr("vs", (5, 128, 4, 8 * VW), BF16)
        X["kiTs"] = dscr("kiTs", (64, 2176), BF16)

        with ExitStack() as st:
            ARENA = 206 * 1024
            arena_t = st.enter_context(nc.sbuf_tensor("arena", [128, ARENA], U8))
            self.psum = st.enter_context(nc.psum_tensor("psum", [128, 4096], F32))
            A = self.A = Arena(arena_t, ARENA, S)
            for b_ in range(8):
                S.regions["ps%d" % b_] = ("P", b_ * 2048, (b_ + 1) * 2048)
            S.regions["ps0h0"] = ("P", 0, 1024)
            S.regions["ps0h1"] = ("P", 1024, 2048)
            S.regions["ps2h0"] = ("P", 4096, 5120)
            S.regions["ps2h1"] = ("P", 5120, 6144)
            self.alloc_persistent()
            self.emit_consts()
            self.emit_weight_casts()
            base = A.off
            if "ada" in self.phases:
                self.emit_ada()
            A.off = base
            if "k" in self.phases:
                self.emit_kphase()
            A.off = base
            if "nocast" not in self.phases:
                self.pop_casts(1000)
            if "q" in self.phases:
                self.emit_qphase()
            S.finalize(st)
        return nc

    def barrier(self):
        S = self.S
        keys = set()
        for op in S.ops:
            keys.update(op.reads)
            keys.update(op.writes)
        keys = sorted(keys)
        t = self.dummy
        S.op("pool", lambda e: e.memset(t[0:1, 0:1], 0.0), r=keys, w=keys + ["dummy"])

    def alloc_persistent(self):
        A = self.A
        self.ident_f = A.alloc("ident_f", 512, F32)
        self.ident_b = A.alloc("ident_b", 256, BF16)
        self.dummy = A.alloc("dummy", 64, F32)
        self.AB = A.alloc("AB", 2 * 4 * 16 * 4, F32, "p (r v c) -> p r v c", r=2, v=4)
        self.hflag = A.alloc("hflag", 64, F32)
        self.invcnt = A.alloc("invcnt", NSUP * 64 * 4, F32, "p (i g t) -> p i g t", i=NSUP, g=4)
        self.pscale = A.alloc("pscale", 32, F32)
        self.g1 = A.alloc("g1", 64, F32)
        self.g2 = A.alloc("g2", 64, F32)
        self.bfm = A.alloc("bfm", 96 * 4, F32)
        self.half = A.alloc("half", 64, F32)
        self.epsb = A.alloc("epsb", 64, F32)
        self.wwi = A.alloc("wwi", 16 * 16 * 2, BF16, "p (c n) -> p c n", c=16)
        self.wpool = A.alloc("wpool", 8 * 256 * 2, BF16, "p (g e) -> p g e", g=8)
        self.gate_b = [A.alloc("gate_b0", D * 4, F32), A.alloc("gate_b1", D * 4, F32)]
        self.gfin_b = A.alloc("gfin_b", D * 4, F32)
        self.stat = [A.alloc("stat%d" % i, 64, F32) for i in range(4)]
        self.junk = A.alloc("junk", D * 2, BF16)
        self.xn = [A.alloc("xn%d" % i, D * 2, BF16) for i in range(2)]

    def emit_consts(self):
        S = self.S
        I = self.I
        S.dma("sp", self.ident_f, I["ident"], w=["ident_f"])
        self.cp("dve", self.ident_b, self.ident_f, r=["ident_f"], w=["ident_b"])
        S.dma("sp", self.hflag[:, 0:16], I["hflag"], w=["hflag"])
        S.dma("sp", self.invcnt, I["invcnt"].rearrange("p (i g t) -> p i g t", i=NSUP, g=4), w=["invcnt"])
        S.dma("sp", self.pscale[:, 0:8], I["pscale"], w=["pscale"])
        S.dma("sp", self.g1[:, 0:16], I["g1"], w=["g1"])
        S.dma("sp", self.g2[:, 0:16], I["g2"], w=["g2"])
        S.dma("sp", self.bfm, I["b_fm"], w=["bfm"])
        S.dma("sp", self.gfin_b, I["gfin"][0:1, :].to_broadcast([128, D]), w=["gfin_b"])
        S.op("pool", lambda e: e.memset(self.half[:, 0:1], 0.5), w=["half"])
        S.op("pool", lambda e: e.memset(self.epsb[:, 0:1], EPS), w=["epsb"])
        S.dma("pool", self.wwi, I["w_in"][:, 4160:4176].rearrange("(c p) n -> p c n", p=128), w=["wwi"])
        S.dma("pool", self.wpool, I["w_pool"].rearrange("(g p) e -> p g e", p=128), w=["wpool"])

    def emit_weight_casts(self):
        I = self.I
        X = self.X
        jobs = self.cast_jobs = []

        class _J:
            @staticmethod
            def dma(q, out, in_, w):
                jobs.append((q, out, in_, w))
        S = _J
        col0 = [0, 256, 512, 768, 3072, 3328, 3584, 3840, 4176, 4432, 4688, 4944]
        for j in range(12):
            S.dma("pool", X["winq"][j], I["w_in"][:, col0[j]:col0[j] + 256].rearrange("(c p) n -> p c n", p=128),
                  w=["winq%d" % j])
        for n in range(4):
            S.dma("pool", X["wout"][n], I["w_out"][:, n * 512:(n + 1) * 512].rearrange("(c p) n -> p c n", p=128),
                  w=["wout%d" % n])
        for b in range(22):
            S.dma("pool", X["wg"][b], I["w_gate"][:, b * 256:(b + 1) * 256].rearrange("(c p) n -> p c n", p=128),
                  w=["wg%d" % b])
            S.dma("pool", X["wu"][b], I["w_up"][:, b * 256:(b + 1) * 256].rearrange("(c p) n -> p c n", p=128),
                  w=["wu%d" % b])
        for n in range(4):
            for q in range(4):
                S.dma("pool", X["wd"][n, q],
                      I["w_down"][q * 1408:(q + 1) * 1408, n * 512:(n + 1) * 512].rearrange("(k p) m -> p k m", p=128),
                      w=["wd%d_%d" % (n, q)])

    def emit_ada(self):
        S = self.S
        I = self.I
        A = self.A
        c2t = A.alloc("c2t", 128, F32)
        sc = A.alloc("sc", 128, F32)
        wa = [A.alloc("wa%d" % i, 16 * 512 * 4, F32, "p (c n) -> p c n", c=16) for i in range(2)]
        bb = [A.alloc("bb%d" % i, 2048, F32) for i in range(2)]
        mo = [A.alloc("mo%d" % i, 2048, F32) for i in range(2)]
        modfm = A.alloc("modfm", 4 * 16 * 2 * 4, F32, "p (v c r) -> p v c r", v=4, c=16)
        S.dma("sp", c2t, I["c2"], w=["c2t"])
        self.act(sc, c2t, AF.Silu, r=["c2t"], w=["sc"])
        vec_of_blk = {0: 0, 1: 0, 2: 0, 3: 0, 4: 1, 5: 1, 6: 1, 7: 1, 12: 2, 13: 2, 14: 2, 15: 2, 16: 3, 17: 3, 18: 3, 19: 3}
        gate_of_blk = {8: 0, 9: 0, 10: 0, 11: 0, 20: 1, 21: 1, 22: 1, 23: 1}
        for blk in range(24):
            b = blk % 2
            if "nocast" not in self.phases:
                self.pop_casts(2)
            S.dma("sp", wa[b], I["w_ada"][:, blk * 512:(blk + 1) * 512].rearrange("(c p) n -> p c n", p=128),
                  w=["wa%d" % b])
            if blk in gate_of_blk:
                g = gate_of_blk[blk]
                S.dma("sp", bb[b][0:2, :], I["b_ada"][0:1, blk * 512:(blk + 1) * 512].to_broadcast([2, 512]),
                      w=["bb%d" % b])
                bank = 2 + b
                for k in range(16):
                    self.mm(self.ps(bank)[0:2, :], sc[:, 2 * k:2 * k + 2], wa[b][:, k, :], k == 0, k == 15,
                            r=["sc", "wa%d" % b], w=["ps%d" % bank])
                self.tt("dve", mo[b][0:2, :], self.ps(bank)[0:2, :], bb[b][0:2, :], ALU.add,
                        r=["ps%d" % bank, "bb%d" % b], w=["mo%d" % b])
                off = g * D + (blk % 4) * 512
                S.dma("sp", self.X["mods"][0:2, off:off + 512], mo[b][0:2, :], r=["mo%d" % b], w=["mods"])
            else:
                v = vec_of_blk[blk]
                bank = 4 + b
                for m in range(4):
                    for k in range(16):
                        self.mm(self.ps(bank)[:, 2 * m:2 * m + 2], wa[b][:, k, m * 128:(m + 1) * 128],
                                sc[:, 2 * k:2 * k + 2], k == 0, k == 15,
                                r=["sc", "wa%d" % b], w=["ps%d" % bank])
                for m in range(4):
                    cc = (blk % 4) * 4 + m
                    gcol = blk * 4 + m
                    self.ts("dve", modfm[:, v, cc, :], self.ps(bank)[:, 2 * m:2 * m + 2],
                            self.bfm[:, gcol:gcol + 1], None, ALU.add, None,
                            r=["ps%d" % bank, "bfm"], w=["modfm"])
        for row in range(2):
            self.stt(self.AB[:, row, 0, :], modfm[:, 1, :, row], 1.0, self.g1[:, 0:16], ALU.add, ALU.mult,
                     r=["modfm", "g1"], w=["AB"])
            self.cp("dve", self.AB[:, row, 1, :], modfm[:, 0, :, row], r=["modfm"], w=["AB"])
            self.stt(self.AB[:, row, 2, :], modfm[:, 3, :, row], 1.0, self.g2[:, 0:16], ALU.add, ALU.mult,
                     r=["modfm", "g2"], w=["AB"])
            self.cp("dve", self.AB[:, row, 3, :], modfm[:, 2, :, row], r=["modfm"], w=["AB"])

    def pop_casts(self, n):
        if not hasattr(self, "cstage"):
            A = self.A
            o = A.off
            A.off = A.size - 2 * 16384 - 128
            self.cstage = [A.alloc("cstage%d" % i, 16384, BF16) for i in range(2)]
            A.off = o
            self.cast_i = 0
        for _ in range(n):
            if self.cast_jobs:
                q, out, in_, w = self.cast_jobs.pop(0)
                b = self.cast_i % 2
                self.cast_i += 1
                c, n_ = out.shape[1], out.shape[2]
                stg = self.cstage[b][:, 0:c * n_].rearrange("p (c n) -> p c n", c=c)
                self.S.dma("pool", stg, in_, w=["cstage%d" % b])
                prev = getattr(self, "cast_pending", None)
                if prev is not None:
                    self.S.dma("pool", prev[0], prev[1], r=[prev[2]], w=prev[3])
                self.cast_pending = (out, stg, "cstage%d" % b, w)
        if not self.cast_jobs and getattr(self, "cast_pending", None) is not None:
            prev = self.cast_pending
            self.S.dma("pool", prev[0], prev[1], r=[prev[2]], w=prev[3])
            self.cast_pending = None

    def load_gates(self, row):
        S = self.S
        for g in range(2):
            S.dma("sp", self.gate_b[g], self.X["mods"][row:row + 1, g * D:(g + 1) * D].to_broadcast([128, D]),
                  r=["mods"], w=["gate_b%d" % g])

    def norm(self, x_sb, ntok, row, which, hT_dst, xkey, hkey, sidx, banks=(1, 2)):
        self.norm_a(x_sb, ntok, xkey, sidx)
        self.norm_b(ntok, row, which, hT_dst, hkey, sidx, banks)

    def norm_a(self, x_sb, ntok, xkey, sidx):
        stt_ = self.stat[sidx]
        skey = "stat%d" % sidx
        xn = self.xn[sidx % 2]
        xnkey = "xn%d" % (sidx % 2)
        self.act(self.junk[:ntok, :], x_sb, AF.Square, r=[xkey], w=["junk", skey], accum_out=stt_[:ntok, 0:1])
        self.act(stt_[:ntok, 1:2], stt_[:ntok, 0:1], AF.Sqrt, r=[skey, "epsb"], w=[skey], scale=1.0 / D, bias=self.epsb[:ntok, 0:1])
        self.S.op("dve", lambda e, o=stt_[:ntok, 2:3], i=stt_[:ntok, 1:2]: e.reciprocal(o, i), r=[skey], w=[skey])
        self.ts("dve", xn[:ntok, :], x_sb, stt_[:ntok, 2:3], None, ALU.mult, None, r=[xkey, skey], w=[xnkey])

    def norm_b(self, ntok, row, which, hT_dst, hkey, sidx, banks=(1, 2)):
        xn = self.xn[sidx % 2]
        xnkey = "xn%d" % (sidx % 2)
        A_ = self.AB[:, row, 2 * which, :]
        B_ = self.AB[:, row, 2 * which + 1, :]
        for c in range(16):
            bank = banks[c // 8]
            tp = self.psb(bank)[:, (c % 8) * 128:(c % 8) * 128 + ntok]
            self.tr(tp, xn[:ntok, c * 128:(c + 1) * 128], self.ident_b[:ntok, :ntok], r=[xnkey, "ident_b"],
                    w=["ps%d" % bank])
            if c < 8:
                self.act(hT_dst(c), tp, AF.Identity, r=["ps%d" % bank, "AB"], w=[hkey],
                         scale=A_[:, c:c + 1], bias=B_[:, c:c + 1])
            else:
                self.ts("dve", hT_dst(c), tp, A_[:, c:c + 1], B_[:, c:c + 1], ALU.mult, ALU.add,
                        r=["ps%d" % bank, "AB"], w=[hkey])

    def emit_kphase(self):
        S = self.S
        I = self.I
        O = self.O
        X = self.X
        A = self.A
        wkv = A.alloc("wkv", 16 * 2112 * 2, BF16, "p (c n) -> p c n", c=16)
        S.dma("pool", wkv[:, :, 0:1024], I["w_in"][:, 1024:2048].rearrange("(c p) n -> p c n", p=128), w=["wkv"])
        S.dma("pool", wkv[:, :, 1024:2048], I["w_in"][:, 2048:3072].rearrange("(c p) n -> p c n", p=128), w=["wkv"])
        S.dma("pool", wkv[:, :, 2048:2112], I["w_in"][:, 4096:4160].rearrange("(c p) n -> p c n", p=128), w=["wkv"])
        xt = [A.alloc("xt%d" % i, D * 4, F32) for i in range(2)]
        hTk = [A.alloc("hTk%d" % i, 16 * 128 * 2, BF16, "p (c t) -> p c t", c=16) for i in range(2)]
        kf = [A.alloc("kf%d" % i, 1024 * 4, F32) for i in range(2)]
        vf = [A.alloc("vf%d" % i, 1024 * 4, F32) for i in range(2)]
        kif = [A.alloc("kif%d" % i, 64 * 4, F32) for i in range(2)]
        kb = [A.alloc("kb%d" % i, 1024 * 2, BF16) for i in range(2)]
        vb = [A.alloc("vb%d" % i, 8 * VW * 2, BF16, "p (h d) -> p h d", h=8) for i in range(2)]
        kib = [A.alloc("kib%d" % i, 64 * 2, BF16) for i in range(2)]
        kTt = [A.alloc("kTt%d" % i, 8 * 128 * 2, BF16, "p (h t) -> p h t", h=8) for i in range(2)]
        kiTt = [A.alloc("kiTt%d" % i, 128 * 2, BF16) for i in range(2)]
        for p in range(2):
            S.op("pool", lambda e, t=vb[p]: e.memset(t[:, :, 128:VW], 1.0), w=["vb%d" % p])

        def post(par, ntok, ksrc, vsrc, kisrc, srckeys, tok0, kT_d, v_d, kiT_d, outs, part="ab"):
            sfx = str(par)
            if part != "b":
                if ksrc is not None:
                    for blk in range(2):
                        self.cp("act", kf[par][:ntok, blk * 512:(blk + 1) * 512], ksrc(blk), r=["ps%d" % (3 + blk)],
                                w=["kf" + sfx])
                        self.cp("act", vf[par][:ntok, blk * 512:(blk + 1) * 512], vsrc(blk), r=["ps%d" % (5 + blk)],
                                w=["vf" + sfx])
                    self.cp("act", kif[par][:ntok, 0:64], kisrc, r=["ps7"], w=["kif" + sfx])
                if outs is not None:
                    S.dma("sp", outs[0], kf[par][:ntok, :], r=["kf" + sfx])
                    S.dma("sp", outs[1], vf[par][:ntok, :], r=["vf" + sfx])
                    S.dma("sp", outs[2], kif[par][:ntok, 0:64], r=["kif" + sfx])
                self.cp("dve", kb[par][:ntok, :], kf[par][:ntok, :], r=["kf" + sfx], w=["kb" + sfx])
                self.cp("pool", vb[par][:ntok, :, 0:128], vf[par][:ntok, :].rearrange("p (h d) -> p h d", h=8),
                        r=["vf" + sfx], w=["vb" + sfx])
                self.cp("dve", kib[par][:ntok, 0:64], kif[par][:ntok, 0:64], r=["kif" + sfx], w=["kib" + sfx])
                S.dma("sp", v_d[tok0 // 512][:ntok, (tok0 % 512) // 128, :], vb[par][:ntok].rearrange("p h d -> p (h d)"),
                      r=["vb" + sfx], w=["vscr"])
            if part == "a":
                return
            tpk = self.psb(0)
            for h in range(8):
                self.tr(tpk[:, h * 128:h * 128 + ntok], kb[par][:ntok, h * 128:(h + 1) * 128],
                        self.ident_b[:ntok, :ntok], r=["kb" + sfx, "ident_b"], w=["ps0"])
            self.cp("act", kTt[par][:, :, :ntok], tpk.rearrange("p (h t) -> p h t", h=8)[:, :, :ntok], r=["ps0"],
                    w=["kTt" + sfx])
            S.dma("sp", kT_d[tok0 // 512][:, :, tok0 % 512:tok0 % 512 + ntok], kTt[par][:, :, :ntok], r=["kTt" + sfx],
                  w=["kTscr"])
            tpi = self.psb(7)[0:64, 512:512 + ntok]
            self.tr(tpi, kib[par][:ntok, 0:64], self.ident_b[:ntok, :ntok], r=["kib" + sfx, "ident_b"], w=["ps7"])
            self.cp("dve", kiTt[par][0:64, :ntok], tpi, r=["ps7"], w=["kiTt" + sfx])
            S.dma("sp", kiT_d[:, tok0:tok0 + ntok], kiTt[par][0:64, :ntok], r=["kiTt" + sfx], w=["kiTscr"])

        def k_norm(t, x_rows, ntok, row, part="ab"):
            par = t % 2
            sfx = str(par)
            if x_rows is not None:
                S.dma("sp", xt[par][:ntok, :], x_rows, w=["xt" + sfx])
            if part != "b":
                self.norm_a(xt[par][:ntok, :], ntok, "xt" + sfx, par)
            if part != "a":
                self.norm_b(ntok, row, 0, lambda c: hTk[par][:, c, :ntok], "hTk" + sfx, par)

        def k_mm(t, ntok):
            par = t % 2
            sfx = str(par)
            for blk in range(5):
                ncols = 512 if blk < 4 else 64
                bank = 3 + blk
                for c in range(16):
                    self.mm(self.ps(bank)[:ntok, :ncols], hTk[par][:, c, :ntok], wkv[:, c, blk * 512:blk * 512 + ncols],
                            c == 0, c == 15, r=["hTk" + sfx, "wkv"], w=["ps%d" % bank])

        def k_post(t, ntok, tok0, kT_d, v_d, kiT_d, outs, part):
            post(t % 2, ntok, lambda blk: self.ps(3 + blk)[:ntok, :], lambda blk: self.ps(5 + blk)[:ntok, :],
                 self.ps(7)[:ntok, 0:64], None, tok0, kT_d, v_d, kiT_d, outs, part)

        nkt = self.n_ktiles if hasattr(self, "n_ktiles") else SEQ // 128

        def kouts(t):
            r0 = t * 128
            return (O["ko"][r0:r0 + 128, :], O["vo"][r0:r0 + 128, :], O["kio"][r0:r0 + 128, :])
        S.dma("sp", xt[0], I["xk"][0:128, :], w=["xt0"])
        if nkt > 1:
            S.dma("sp", xt[1], I["xk"][128:256, :], w=["xt1"])
        k_norm(0, None, 128, 0)
        for t in range(nkt):
            if "nocast" not in self.phases:
                self.pop_casts(2)
            if t + 1 < nkt:
                k_norm(t + 1, None, 128, 0, "a")
            if t + 2 < nkt:
                S.dma("sp", xt[t % 2], I["xk"][(t + 2) * 128:(t + 3) * 128, :], w=["xt%d" % (t % 2)])
            k_mm(t, 128)
            k_post(t, 128, t * 128, X["kT"], X["v"], X["kiT"], kouts(t), "a")
            if t + 1 < nkt:
                k_norm(t + 1, None, 128, 0, "b")
            if t >= 1:
                k_post(t - 1, 128, (t - 1) * 128, X["kT"], X["v"], X["kiT"], None, "b")
        k_post(nkt - 1, 128, (nkt - 1) * 128, X["kT"], X["v"], X["kiT"], None, "b")
        ncache = 0 if getattr(self, "skip_cache", False) else 16

        def cload(t):
            par = t % 2
            r0 = t * 128
            S.dma("sp", kf[par], I["ck"][r0:r0 + 128, :], w=["kf%d" % par])
            S.dma("sp", vf[par], I["cv"][r0:r0 + 128, :], w=["vf%d" % par])
            S.dma("sp", kif[par][:, 0:64], I["cki"][r0:r0 + 128, :], w=["kif%d" % par])
        if ncache:
            cload(0)
        for t in range(ncache):
            par = t % 2
            r0 = t * 128
            if t + 1 < ncache:
                cload(t + 1)
            post(par, 128, None, None, None, None, r0, X["kTs"], X["vs"], X["kiTs"], None)
        if not getattr(self, "skip_stile", False):
            k_norm(0, I["xs"][0:64, :], 64, 1)
            k_mm(0, 64)
            k_post(0, 64, 2048, X["kTs"], X["vs"], X["kiTs"],
                   (O["kso"][0:64, :], O["vso"][0:64, :], O["kiso"][0:64, :]), "ab")

    def emit_qphase(self):
        S = self.S
        I = self.I
        O = self.O
        X = self.X
        A = self.A
        NST = NT // 128
        xres = [A.alloc("xres%d" % i, D * 4, F32) for i in range(NST)]
        concatT = A.alloc("concatT", 16 * NT * 2, BF16, "p (c t) -> p c t", c=16)
        yt = [A.alloc("yt0", D * 4, F32)]
        _o = A.off
        A.off = _o - D * 4
        xh_t = A.alloc("xh_t", D * 4, F32)
        A.off = _o - D * 4
        oacc = A.alloc("oacc", 8 * 129 * 4, F32)
        A.off = _o
        nm = A.alloc("nm", NST * 512 * 4, F32, "p (s n) -> p s n", s=NST)
        for s_ in range(NST):
            S.dma("sp", nm[:, s_, :], I["negmask"][s_ * 128:(s_ + 1) * 128, :], w=["nm"])
        xbase = A.off
        hT = A.alloc("hT", 16 * NT * 2, BF16, "p (c t) -> p c t", c=16)
        hTh = A.alloc("hTh", 16 * 16 * 2, BF16, "p (c t) -> p c t", c=16)
        wq = [A.alloc("wq%d" % i, 16 * 256 * 2, BF16, "p (c n) -> p c n", c=16) for i in range(4)]
        xend = A.off
        A.off = xbase
        Ssb = A.alloc("Ssb", SEQ * 4, F32)
        junkA = A.alloc("junkA", SEQ // 2, U8)
        junkB = A.alloc("junkB", SEQ // 2, U8)
        xend = max(xend, A.off)
        A.off = xend
        ybase = A.off
        qT = A.alloc("qT", 8 * NT * 2, BF16, "p (h t) -> p h t", h=8)
        qiT = A.alloc("qiT", 16 * NT * 2, BF16, "p (h t) -> p h t", h=16)
        uT = A.alloc("uT", 8 * (NT + 16) * 4, F32, "p (c t) -> p c t", c=8)
        wi_sb = A.alloc("wi", NST * 16 * 4, F32, "p (s h) -> p s h", s=NST)
        dg = A.alloc("dg", 16 * 128 * 2, BF16, "p (h q) -> p h q", h=16)
        _d1 = A.off
        A.off = _d1 - 16 * 128 * 2
        pTm = [A.alloc("pTX%d" % i, 1024, BF16, "p (b q) -> p b q", b=4) for i in range(4)]
        A.off = _d1
        kic = [A.alloc("kic%d" % i, 1024, BF16) for i in range(2)]
        _k0 = A.off
        kTc = [A.alloc("kTc%d" % i, 8 * 512 * 2, BF16, "p (h k) -> p h k", h=8) for i in range(2)]
        Vc = [A.alloc("Vc%d" % i, 4 * 8 * VW * 2, BF16, "p (b n) -> p b n", b=4) for i in range(2)]
        _k1 = A.off
        A.off = _k0
        sA = A.alloc("sA", 8 * (NT + 16) * 4, F32, "p (c t) -> p c t", c=8)
        sB = A.alloc("sB", 8 * (NT + 16) * 4, F32, "p (c t) -> p c t", c=8)
        zT = A.alloc("zT", 8 * NT * 2, BF16, "p (c t) -> p c t", c=8)
        spl = A.alloc("spl", 1024 * 4, F32)
        ulast = A.alloc("ulast", 1024 * 4, F32)
        assert A.off <= _k1
        A.off = _k1
        mk = A.alloc("mk", 1024, BF16)
        mkT = [A.alloc("mkT%d" % i, 1024, BF16, "p (b q) -> p b q", b=4) for i in range(2)]
        Rr = [A.alloc("pTR%d" % i, 1024, BF16) for i in range(4)]
        pT = [Rr[i].rearrange("p (b q) -> p b q", b=4) for i in range(4)]
        ao = A.alloc("ao", 1024 * 2, BF16)
        bs = A.alloc("bs", 128, F32)
        bmid = A.alloc("bmid", 64, F32)
        bcA = A.alloc("bcA", 64, F32)
        bsB = A.alloc("bsB", 64, F32)
        bsel = A.alloc("bsel", 64, F32)
        top8 = A.alloc("top8", 64, F32)
        rec = A.alloc("rec", 64, F32)
        yend = A.off
        A.off = ybase
        actT = A.alloc("actT", FC * NT * 2, BF16, "p (f t) -> p f t", f=FC)
        wd = [A.alloc("wd%d" % i, 11 * 512 * 2, BF16, "p (k m) -> p k m", k=11) for i in range(2)]
        wo = [A.alloc("wo%d" % i, 16 * 512 * 2, BF16, "p (c n) -> p c n", c=16) for i in range(2)]
        tmpf = [A.alloc("tmpf%d" % i, 512 * 4, F32) for i in range(2)]
        sg = [A.alloc("sg%d" % i, NT * 4, F32) for i in range(2)]
        yend = max(yend, A.off)
        A.off = yend
        kY = ""

        def qtile(i, x_rows, ntok_all, row, halo, nk, kT_d, v_d, kiT_d, use_mask, y_rows, pool_out):
            nst = (ntok_all + 127) // 128
            nts = [min(128, ntok_all - s_ * 128) for s_ in range(nst)]
            self.load_gates(row) if (i == 0 or row == 1) else None
            for s_ in range(nst):
                nt_ = nts[s_]
                S.dma("sp", xres[s_][:nt_, :], x_rows[s_ * 128:s_ * 128 + nt_, :], w=["xres%d" % s_])
            for s_ in range(nst):
                self.norm_a(xres[s_][:nts[s_], :], nts[s_], "xres%d" % s_, 2 + s_ % 2)
            for s_ in range(nst):
                nt_ = nts[s_]
                self.norm_b(nt_, row, 0, lambda c, s_=s_, nt_=nt_: hT[:, c, s_ * 128:s_ * 128 + nt_], "hT", 2 + s_ % 2)
            if halo == "x":
                S.dma("sp", xh_t[0:16, :], I["xh"][i * 16:(i + 1) * 16, :], w=["xh_t"])
                self.norm(xh_t[0:16, :], 16, row, 0, lambda c: hTh[:, c, 0:16], "xh_t", "hTh", 0)
            sc_q = 128 ** -0.5
            sc_qi = (64 ** -0.5) * (16 ** -0.5)
            for j in range(12):
                b = j % 4
                S.dma("sp", wq[b], X["winq"][j], r=["winq%d" % j], w=["wq%d" % b])
                if j < 4:
                    for m in range(2):
                        bank = 3 + (2 * j + m) % 4
                        for c in range(16):
                            self.mm(self.ps(bank)[:, :ntok_all], wq[b][:, c, m * 128:(m + 1) * 128], hT[:, c, :ntok_all],
                                    c == 0, c == 15, r=["wq%d" % b, "hT"], w=["ps%d" % bank])
                        self.act(qT[:, 2 * j + m, :ntok_all], self.ps(bank)[:, :ntok_all], AF.Copy,
                                 r=["ps%d" % bank], w=[kY + "qT"], scale=sc_q)
                elif j < 8:
                    for m in range(4):
                        h = (j - 4) * 4 + m
                        bank = 3 + h % 4
                        for c in range(16):
                            self.mm(self.ps(bank)[0:64, :ntok_all], wq[b][:, c, m * 64:(m + 1) * 64], hT[:, c, :ntok_all],
                                    c == 0, c == 15, r=["wq%d" % b, "hT"], w=["ps%d" % bank])
                        self.ts("dve", qiT[0:64, h, :ntok_all], self.ps(bank)[0:64, :ntok_all], sc_qi, None, ALU.mult, None,
                                r=["ps%d" % bank], w=[kY + "qiT"])
                else:
                    for m in range(2):
                        ch = (j - 8) * 2 + m
                        bank = 3 + ch % 4
                        for c in range(16):
                            self.mm(self.ps(bank)[:, :ntok_all], wq[b][:, c, m * 128:(m + 1) * 128], hT[:, c, :ntok_all],
                                    c == 0, c == 15, r=["wq%d" % b, "hT"], w=["ps%d" % bank])
                        self.cp("act", uT[:, ch, 16:16 + ntok_all], self.ps(bank)[:, :ntok_all], r=["ps%d" % bank],
                                w=[kY + "uT"])
                        if halo == "x":
                            for c in range(16):
                                self.mm(self.ps(7)[:, 0:16], wq[b][:, c, m * 128:(m + 1) * 128], hTh[:, c, 0:16],
                                        c == 0, c == 15, r=["wq%d" % b, "hTh"], w=["ps7"])
                            self.ts("dve", uT[:, ch, 0:16], self.ps(7)[:, 0:16], self.hflag[:, i:i + 1], None, ALU.mult,
                                    None, r=["ps7", "hflag"], w=[kY + "uT"])
            if halo == "state":
                S.dma("sp", spl[0:16, :], I["spool"], w=[kY + "spl"])
                for ch in range(8):
                    self.tr(self.ps(7)[:, ch * 16:(ch + 1) * 16], spl[0:16, ch * 128:(ch + 1) * 128],
                            self.ident_f[0:16, 0:16], r=[kY + "spl", "ident_f"], w=["ps7"])
                self.cp("dve", uT[:, :, 0:16], self.ps(7)[:, 0:128].rearrange("p (c t) -> p c t", c=8), r=["ps7"],
                        w=[kY + "uT"])
            for s_ in range(nst):
                nt_ = nts[s_]
                for c in range(16):
                    self.mm(self.ps(7)[:nt_, 256:272], hT[:, c, s_ * 128:s_ * 128 + nt_], self.wwi[:, c, :],
                            c == 0, c == 15, r=["hT", "wwi"], w=["ps7"])
                self.cp("dve", wi_sb[:nt_, s_, :], self.ps(7)[:nt_, 256:272], r=["ps7"], w=[kY + "wi"])
            def pool_mixer():
                W = ntok_all + 16
                uk = kY + "uT"
                self.tt("pool", sA[:, :, 1:W], uT[:, :, 1:W], uT[:, :, 0:W - 1], ALU.add, r=[uk], w=[kY + "sA"])
                self.tt("pool", sB[:, 2:8, 3:W], sA[:, 2:8, 3:W], sA[:, 2:8, 1:W - 2], ALU.add, r=[kY + "sA"], w=[kY + "sB"])
                self.tt("pool", sA[:, 4:8, 7:W], sB[:, 4:8, 7:W], sB[:, 4:8, 3:W - 4], ALU.add, r=[kY + "sB"], w=[kY + "sA"])
                self.tt("pool", sB[:, 6:8, 15:W], sA[:, 6:8, 15:W], sA[:, 6:8, 7:W - 8], ALU.add, r=[kY + "sA"], w=[kY + "sB"])
                srcs = [sA, sB, sA, sB]
                for g in range(4):
                    wdw = (2, 4, 8, 16)[g]
                    self.stt(zT[:, 2 * g:2 * g + 2, :ntok_all], srcs[g][:, 2 * g:2 * g + 2, 16:W], 1.0 / wdw,
                             uT[:, 2 * g:2 * g + 2, 16:W], ALU.mult, ALU.subtract, r=[kY + "sA", kY + "sB", uk], w=[kY + "zT"])
                    if halo == "x":
                        for cc in range(2):
                            ch = 2 * g + cc
                            self.tt("dve", sg[0][:, 0:16], srcs[g][:, ch, 16:32], self.invcnt[:, i, g, :], ALU.mult,
                                    r=[kY + "sA", kY + "sB", "invcnt"], w=[kY + "sg0"])
                            self.tt("dve", zT[:, ch, 0:16], sg[0][:, 0:16], uT[:, ch, 16:32], ALU.subtract,
                                    r=[kY + "sg0", uk], w=[kY + "zT"])
                for g in range(4):
                    for ec in range(2):
                        bank = 3 + (2 * g + ec) % 4
                        for cc in range(2):
                            self.mm(self.ps(bank)[:, :ntok_all], self.wpool[:, 2 * g + cc, ec * 128:(ec + 1) * 128],
                                    zT[:, 2 * g + cc, :ntok_all], cc == 0, cc == 1, r=["wpool", kY + "zT"], w=["ps%d" % bank])
                        self.ts("dve", concatT[:, 8 + 2 * g + ec, :ntok_all], self.ps(bank)[:, :ntok_all],
                                self.pscale[:, 2 * g + ec:2 * g + ec + 1], None, ALU.mult, None,
                                r=["ps%d" % bank, "pscale"], w=["concatT"])
                if pool_out is not None:
                    for ch in range(8):
                        bk_ = 6 + ch // 4
                        self.tr(self.ps(bk_)[0:16, (ch % 4) * 128:(ch % 4 + 1) * 128], uT[:, ch, ntok_all:ntok_all + 16],
                                self.ident_f, r=[uk, "ident_f"], w=["ps%d" % bk_])
                    for hh in range(2):
                        self.cp("dve", ulast[0:16, hh * 512:(hh + 1) * 512], self.ps(6 + hh)[0:16, :], r=["ps%d" % (6 + hh)],
                                w=[kY + "ulast"])
                    S.dma("pool", pool_out, ulast[0:16, :], r=[kY + "ulast"])
            for s_ in range(nst):
                nq = nts[s_]
                q0 = s_ * 128
                for h in range(16):
                    self.ts("pool", dg[:nq, h, :nq], self.ident_b[:nq, :nq], wi_sb[:nq, s_, h:h + 1], None, ALU.mult, None,
                            r=["ident_b", kY + "wi"], w=[kY + "dg"])
                nblk = (nk + 511) // 512
                steps = []
                for kbk in range(nblk):
                    k0 = kbk * 512
                    wb = min(512, nk - k0)
                    halves = [(0, min(256, wb))] + ([(256, wb - 256)] if wb > 256 else [])
                    for hi_, (ho, hw) in enumerate(halves):
                        for hp in range(8):
                            steps.append((kbk, k0, wb, ho, hw, hp, hi_ == len(halves) - 1 and hp == 7))
                dbanks = [0, 2, 4, 6]
                LA = getattr(self, "idx_la", 2)
                NS = len(steps)
                for n in range(NS + LA):
                    if n < NS:
                        kbk, k0, wb, ho, hw, hp, _ = steps[n]
                        kb_ = kbk % 2
                        if ho == 0 and hp == 0:
                            S.dma("sp", kic[kb_][0:64, :wb], kiT_d[:, k0:k0 + wb], r=["kiTscr"], w=["kic%d" % kb_])
                        sl = n % 4
                        dbank = dbanks[sl]
                        dkey = "ps%d" % dbank
                        for j in range(2):
                            self.mm(self.ps(dbank)[:nq, j * 256:j * 256 + hw], qiT[0:64, 2 * hp + j, q0:q0 + nq],
                                    kic[kb_][0:64, ho:ho + hw], True, True, r=["qiT", "kic%d" % kb_], w=[dkey])
                        dps = self.ps(dbank)[:nq, :].rearrange("p (j k) -> p j k", j=2)[:, :, :hw]
                        rdst = Rr[sl][:nq, :].rearrange("p (j k) -> p j k", j=2)[:, :, :hw]
                        if sl < 2:
                            self.act(rdst, dps, AF.Relu, r=[dkey], w=["pTR%d" % sl])
                        else:
                            self.ts("dve", rdst, dps, 0.0, None, ALU.max, None, r=[dkey], w=["pTR%d" % sl])
                    m = n - LA
                    if m >= 0:
                        kbk, k0, wb, ho, hw, hp, lastb = steps[m]
                        sl = m % 4
                        sbank = 1 if kbk % 2 == 0 else 3
                        for j in range(2):
                            h = 2 * hp + j
                            self.mm(self.ps(sbank)[:nq, ho:ho + hw], dg[:nq, h, :nq], Rr[sl][:nq, j * 256:j * 256 + hw],
                                    h == 0, h == 15, r=["dg", "pTR%d" % sl], w=["ps%d" % sbank])
                        if lastb:
                            self.cp("dve", Ssb[:nq, k0:k0 + wb], self.ps(sbank)[:nq, :wb], r=["ps%d" % sbank], w=["Ssb"])
                if s_ == 0:
                    pool_mixer()
                bk = ["bs"]
                self.S.op("dve", lambda e, o=bs[:nq, 0:1], i_=Ssb[:nq, :nk]: e.tensor_reduce(o, i_, AX.X, ALU.min),
                          r=["Ssb"], w=bk)
                if use_mask:
                    self.tt("dve", Ssb[:nq, nk - 512:nk], Ssb[:nq, nk - 512:nk], nm[:nq, s_, :], ALU.add,
                            r=["Ssb", "nm"], w=["Ssb"])
                self.S.op("dve", lambda e, o=top8[:nq, 0:8], i_=Ssb[:nq, :nk]: e.max(o, i_), r=["Ssb"], w=["top8"])
                self.tt("dve", bs[:nq, 1:2], top8[:nq, 0:1], bs[:nq, 0:1], ALU.subtract, r=["top8"] + bk, w=bk)
                h1 = nk // 2 if nk >= 1024 else nk
                n2 = nk - h1
                for it in range(NBIS):
                    cfac = 0.5 ** (it + 1)
                    self.stt(bmid[:nq, 0:1], bs[:nq, 1:2], cfac, bs[:nq, 0:1], ALU.mult, ALU.add, r=bk, w=["bmid"])
                    self.ts("dve", junkA[:nq, :h1], Ssb[:nq, :h1], bmid[:nq, 0:1], 0.0, ALU.is_ge, ALU.add,
                            r=["bmid", "Ssb"], w=["bcA", "junkA"], accum_out=bcA[:nq, 0:1])
                    if n2 > 0:
                        self.act(junkB[:nq, :n2], Ssb[:nq, h1:nk], AF.Sign, r=["bmid", "Ssb"], w=["bsB", "junkB"],
                                 scale=-1.0, bias=bmid[:nq, 0:1], accum_out=bsB[:nq, 0:1])
                        self.stt(bsel[:nq, 0:1], bcA[:nq, 0:1], 2.0, bsB[:nq, 0:1], ALU.mult, ALU.subtract,
                                 r=["bcA", "bsB"], w=["bsel"])
                        self.ts("dve", bsel[:nq, 0:1], bsel[:nq, 0:1], 511.0 - n2, cfac, ALU.is_ge, ALU.mult,
                                r=["bsel"], w=["bsel"])
                    else:
                        self.ts("dve", bsel[:nq, 0:1], bcA[:nq, 0:1], 255.5, cfac, ALU.is_ge, ALU.mult, r=["bcA"], w=["bsel"])
                    self.stt(bs[:nq, 0:1], bsel[:nq, 0:1], bs[:nq, 1:2], bs[:nq, 0:1], ALU.mult, ALU.add,
                             r=bk + ["bsel"], w=bk)
                ngrp = (nk + 511) // 512
                pairs = [(g, h) for g in range(ngrp) for h in range(8)]
                NP = len(pairs)
                ginfo = {}
                for g in range(ngrp):
                    k0 = g * 512
                    wg_ = min(512, nk - k0)
                    nb = (wg_ + 127) // 128
                    ginfo[g] = (k0, wg_, nb, [min(128, wg_ - b_ * 128) for b_ in range(nb)])
                LAA = 3
                qbanks = [3, 4, 0, 1]
                for n in range(NP + LAA):
                    if n < NP:
                        g, h = pairs[n]
                        k0, wg_, nb, wbs = ginfo[g]
                        gb = g % 2
                        full = all(w_ == 128 for w_ in wbs)
                        if h == 0:
                            S.dma("sp", kTc[gb][:, :, :wg_], kT_d[g][:, :, :wg_], r=["kTscr"], w=["kTc%d" % gb])
                            if full:
                                S.dma("sp", Vc[gb][:, :nb, :], v_d[g][:, :nb, :], r=["vscr"], w=["Vc%d" % gb])
                            else:
                                for b_ in range(nb):
                                    S.dma("sp", Vc[gb][:wbs[b_], b_, :], v_d[g][:wbs[b_], b_, :],
                                          r=["vscr"], w=["Vc%d" % gb])
                            self.ts("dve", mk[:nq, :wg_], Ssb[:nq, k0:k0 + wg_], bs[:nq, 0:1], None, ALU.is_ge, None,
                                    r=["Ssb", "bs"], w=["mk"])
                            tpm = self.psb(2)
                            for b_ in range(nb):
                                self.tr(tpm[:wbs[b_], b_ * 128:b_ * 128 + nq], mk[:nq, b_ * 128:b_ * 128 + wbs[b_]],
                                        self.ident_b[:nq, :nq], r=["mk", "ident_b"], w=["ps2"])
                            if full:
                                self.cp("act", mkT[gb][:, :nb, :nq],
                                        tpm[:, :nb * 128].rearrange("p (b q) -> p b q", b=nb)[:, :, :nq], r=["ps2"],
                                        w=["mkT%d" % gb])
                            else:
                                for b_ in range(nb):
                                    self.cp("act", mkT[gb][:wbs[b_], b_, :nq], tpm[:wbs[b_], b_ * 128:b_ * 128 + nq],
                                            r=["ps2"], w=["mkT%d" % gb])
                        qb = qbanks[n % 4]
                        pb = n % 4
                        for b_ in range(nb):
                            self.mm(self.ps(qb)[:wbs[b_], b_ * 128:b_ * 128 + nq], kTc[gb][:, h, b_ * 128:b_ * 128 + wbs[b_]],
                                    qT[:, h, q0:q0 + nq], True, True, r=["kTc%d" % gb, "qT"], w=["ps%d" % qb])
                        if full:
                            self.act(pT[pb][:, :nb, :nq],
                                     self.ps(qb)[:, :nb * 128].rearrange("p (b q) -> p b q", b=nb)[:, :, :nq], AF.Exp,
                                     r=["ps%d" % qb], w=["pTR%d" % pb])
                            self.tt("pool" if h % 2 else "dve", pTm[pb][:, :nb, :nq], pT[pb][:, :nb, :nq], mkT[gb][:, :nb, :nq],
                                    ALU.mult, r=["pTR%d" % pb, "mkT%d" % gb], w=["pTX%d" % pb])
                        else:
                            for b_ in range(nb):
                                self.act(pT[pb][:wbs[b_], b_, :nq], self.ps(qb)[:wbs[b_], b_ * 128:b_ * 128 + nq], AF.Exp,
                                         r=["ps%d" % qb], w=["pTR%d" % pb])
                            for b_ in range(nb):
                                self.tt("dve", pTm[pb][:wbs[b_], b_, :nq], pT[pb][:wbs[b_], b_, :nq], mkT[gb][:wbs[b_], b_, :nq],
                                        ALU.mult, r=["pTR%d" % pb, "mkT%d" % gb], w=["pTX%d" % pb])
                    m = n - LAA
                    if m >= 0:
                        g, h = pairs[m]
                        k0, wg_, nb, wbs = ginfo[g]
                        gb = g % 2
                        pb = m % 4
                        ob = 5 + h // 3
                        oc = (h % 3) * 129
                        for b_ in range(nb):
                            self.mm(self.ps(ob)[:nq, oc:oc + 129], pTm[pb][:wbs[b_], b_, :nq],
                                    Vc[gb][:wbs[b_], b_, h * VW:h * VW + 129], b_ == 0, b_ == nb - 1,
                                    r=["pTX%d" % pb, "Vc%d" % gb], w=["ps%d" % ob])
                        if h == 7:
                            for ob in (5, 6, 7):
                                nh = 3 if ob < 7 else 2
                                c0 = (ob - 5) * 387
                                if g == 0:
                                    self.cp("dve", oacc[:nq, c0:c0 + nh * 129], self.ps(ob)[:nq, 0:nh * 129],
                                            r=["ps%d" % ob], w=["oacc"])
                                else:
                                    self.tt("dve", oacc[:nq, c0:c0 + nh * 129], oacc[:nq, c0:c0 + nh * 129],
                                            self.ps(ob)[:nq, 0:nh * 129], ALU.add, r=["ps%d" % ob, "oacc"], w=["oacc"])
                for h in range(8):
                    oc = h * 129
                    self.S.op("dve", lambda e, o=rec[:nq, h:h + 1], i_=oacc[:nq, oc + 128:oc + 129]: e.reciprocal(o, i_),
                              r=["oacc"], w=[kY + "rec"])
                    self.act(ao[:nq, h * 128:(h + 1) * 128], oacc[:nq, oc:oc + 128], AF.Copy,
                             r=["oacc", kY + "rec"], w=[kY + "ao"], scale=rec[:nq, h:h + 1])
                tpo = self.psb(2)
                for h in range(8):
                    self.tr(tpo[:, h * 128:h * 128 + nq], ao[:nq, h * 128:(h + 1) * 128], self.ident_b[:nq, :nq],
                            r=[kY + "ao", "ident_b"], w=["ps2"])
                self.cp("dve", concatT[:, 0:8, q0:q0 + nq], tpo.rearrange("p (h t) -> p h t", h=8)[:, :, :nq], r=["ps2"],
                        w=["concatT"])
            for n in range(4):
                b = n % 2
                S.dma("sp", wo[b], X["wout"][n], r=["wout%d" % n], w=[kY + "wo%d" % b])
                for s_ in range(nst):
                    nt_ = nts[s_]
                    bank = 3 + (n * nst + s_) % 4
                    tb = (n * nst + s_) % 2
                    for c in range(16):
                        self.mm(self.ps(bank)[:nt_, :], concatT[:, c, s_ * 128:s_ * 128 + nt_], wo[b][:, c, :],
                                c == 0, c == 15, r=["concatT", kY + "wo%d" % b], w=["ps%d" % bank])
                    self.tt("dve", tmpf[tb][:nt_, :], self.ps(bank)[:nt_, :], self.gate_b[0][:nt_, n * 512:(n + 1) * 512],
                            ALU.mult, r=["ps%d" % bank, "gate_b0"], w=[kY + "tmpf%d" % tb])
                    self.tt("pool", xres[s_][:nt_, n * 512:(n + 1) * 512], xres[s_][:nt_, n * 512:(n + 1) * 512],
                            tmpf[tb][:nt_, :], ALU.add, r=["xres%d" % s_, kY + "tmpf%d" % tb], w=["xres%d" % s_])
            for s_ in range(nst):
                nt_ = nts[s_]
                self.norm(xres[s_][:nt_, :], nt_, row, 1, lambda c, s_=s_, nt_=nt_: hT[:, c, s_ * 128:s_ * 128 + nt_],
                          "xres%d" % s_, "hT", 2 + s_ % 2)
            for fb in range(22):
                b = fb % 2
                S.dma("sp", wq[b], X["wg"][fb], r=["wg%d" % fb], w=["wq%d" % b])
                S.dma("sp", wq[2 + b], X["wu"][fb], r=["wu%d" % fb], w=["wq%d" % (2 + b)])
                for m in range(2):
                    f = fb * 2 + m
                    bg = 3 + (f % 2) * 2
                    bu = bg + 1
                    for c in range(16):
                        self.mm(self.ps(bg)[:, :ntok_all], wq[b][:, c, m * 128:(m + 1) * 128], hT[:, c, :ntok_all],
                                c == 0, c == 15, r=["wq%d" % b, "hT"], w=["ps%d" % bg])
                    for c in range(16):
                        self.mm(self.ps(bu)[:, :ntok_all], wq[2 + b][:, c, m * 128:(m + 1) * 128], hT[:, c, :ntok_all],
                                c == 0, c == 15, r=["wq%d" % (2 + b), "hT"], w=["ps%d" % bu])
                    self.act(sg[f % 2][:, :ntok_all], self.ps(bg)[:, :ntok_all], AF.Silu, r=["ps%d" % bg],
                             w=[kY + "sg%d" % (f % 2)])
                    self.tt("dve", actT[:, f, :ntok_all], sg[f % 2][:, :ntok_all], self.ps(bu)[:, :ntok_all], ALU.mult,
                            r=[kY + "sg%d" % (f % 2), "ps%d" % bu], w=[kY + "actT"])
            for n in range(4):
                for q in range(4):
                    b = (n * 4 + q) % 2
                    S.dma("sp", wd[b], X["wd"][n, q], r=["wd%d_%d" % (n, q)], w=[kY + "wd%d" % b])
                    for k in range(11):
                        kk = q * 11 + k
                        for s_ in range(nst):
                            nt_ = nts[s_]
                            bank = 3 + (n % 2) * 2 + s_
                            self.mm(self.ps(bank)[:nt_, :], actT[:, kk, s_ * 128:s_ * 128 + nt_], wd[b][:, k, :],
                                    kk == 0, kk == FC - 1, r=[kY + "actT", kY + "wd%d" % b], w=["ps%d" % bank])
                for s_ in range(nst):
                    nt_ = nts[s_]
                    bank = 3 + (n % 2) * 2 + s_
                    tb = s_ % 2
                    self.tt("dve", tmpf[tb][:nt_, :], self.ps(bank)[:nt_, :], self.gate_b[1][:nt_, n * 512:(n + 1) * 512],
                            ALU.mult, r=["ps%d" % bank, "gate_b1"], w=[kY + "tmpf%d" % tb])
                    self.tt("pool", xres[s_][:nt_, n * 512:(n + 1) * 512], xres[s_][:nt_, n * 512:(n + 1) * 512],
                            tmpf[tb][:nt_, :], ALU.add, r=["xres%d" % s_, kY + "tmpf%d" % tb], w=["xres%d" % s_])
            for s_ in range(nst):
                nt_ = nts[s_]
                stt_ = self.stat[s_ % 2]
                skey = "stat%d" % (s_ % 2)
                self.act(self.junk[:nt_, :], xres[s_][:nt_, :], AF.Square, r=["xres%d" % s_], w=["junk", skey],
                         accum_out=stt_[:nt_, 0:1])
                self.act(stt_[:nt_, 1:2], stt_[:nt_, 0:1], AF.Sqrt, r=[skey, "epsb"], w=[skey], scale=1.0 / D,
                         bias=self.epsb[:nt_, 0:1])
                self.S.op("dve", lambda e, o=stt_[:nt_, 2:3], i_=stt_[:nt_, 1:2]: e.reciprocal(o, i_), r=[skey], w=[skey])
                self.stt(yt[0][:nt_, :], xres[s_][:nt_, :], stt_[:nt_, 2:3], self.gfin_b[:nt_, :], ALU.mult, ALU.mult,
                         r=["xres%d" % s_, skey, "gfin_b"], w=["yt0"])
                S.dma("pool", y_rows[s_ * 128:s_ * 128 + nt_, :], yt[0][:nt_, :], r=["yt0"])

        nsup = self.n_sup if hasattr(self, "n_sup") else NSUP
        for i in range(nsup):
            nk = (2 * i + 2) * NT
            qtile(i, I["xq"][i * NT:(i + 1) * NT, :], NT, 0, "x", nk, X["kT"], X["v"], X["kiT"], True,
                  O["y"][i * NT:(i + 1) * NT, :], O["po"] if i == NSUP - 1 else None)
        if not hasattr(self, "skip_sample"):
            qtile(0, I["xs"], 64, 1, "state", NKS, X["kTs"], X["vs"], X["kiTs"], False, O["ys"], O["pso"])


_CACHE = {}


def _prep_inputs(inp):
    f = lambda a: np.ascontiguousarray(a, dtype=np.float32)
    x_prompt = inp["x_prompt"]
    w_ada = f(inp["w_ada"][0])
    b_ada = f(inp["b_ada"][0])
    shared = {
        "w_ada": w_ada, "b_ada": b_ada[None, :], "b_fm": f(b_ada.reshape(96, 128).T),
        "g1": f(inp["g_norm1"][0].reshape(16, 128).T), "g2": f(inp["g_norm2"][0].reshape(16, 128).T),
        "w_in": f(inp["w_in"][0]), "w_pool": f(inp["w_pool"][0].reshape(1024, 256)),
        "pscale": f(inp["pool_scale"][0].reshape(8, 128).T), "w_out": f(inp["w_out"][0]),
        "w_gate": f(inp["w_gate"][0]), "w_up": f(inp["w_up"][0]), "w_down": f(inp["w_down"][0]),
        "gfin": f(inp["g_final"][None, :]), "ident": np.eye(128, dtype=np.float32),
    }
    maps = []
    for c in range(8):
        b, hf = c // 2, c % 2
        xb = x_prompt[b]
        Js = [2 * i + hf for i in range(NSUP)]
        xq = np.concatenate([xb[J * NT:(J + 1) * NT] for J in Js], axis=0)
        xh = np.zeros((NSUP * 16, D), np.float32)
        hflag = np.zeros((128, 16), np.float32)
        invcnt = np.zeros((128, NSUP, 4, 16), np.float32)
        for i, J in enumerate(Js):
            if J > 0:
                xh[i * 16:(i + 1) * 16] = xb[J * NT - 16:J * NT]
                hflag[:, i] = 1.0
            for g, w in enumerate((2, 4, 8, 16)):
                pos = J * NT + np.arange(16)
                invcnt[:, i, g, :] = 1.0 / np.minimum(pos + 1, w)
        qch = (hf * NT + np.arange(NT)) // 64
        kch = np.arange(512) // 64
        negmask = np.where(kch[None, :] <= qch[:, None], 0.0, -1e30).astype(np.float32)
        c2 = np.stack([inp["c_prompt"][b].reshape(16, 128).T, inp["c_sample"][c].reshape(16, 128).T], axis=-1)
        spool = np.zeros((16, 1024), np.float32)
        spool[1:] = inp["state_pool"][0, c]
        m = dict(shared)
        m.update({
            "xk": f(xb), "xq": f(xq), "xh": xh, "xs": f(inp["x_sample"][c]), "c2": f(c2.reshape(128, 32)),
            "ck": f(inp["cache_k"][0, c].reshape(2048, 1024)), "cv": f(inp["cache_v"][0, c].reshape(2048, 1024)),
            "cki": f(inp["cache_kidx"][0, c]), "spool": spool, "negmask": negmask, "hflag": hflag,
            "invcnt": f(invcnt.reshape(128, NSUP * 64)),
        })
        maps.append(m)
    return maps


def _assemble(results):
    y = np.zeros((4, SEQ, D), np.float32)
    ys = np.zeros((8, 64, D), np.float32)
    kp = np.zeros((1, 4, SEQ, 8, 128), np.float32)
    vp = np.zeros((1, 4, SEQ, 8, 128), np.float32)
    kip = np.zeros((1, 4, SEQ, 64), np.float32)
    pp = np.zeros((1, 4, 15, 1024), np.float32)
    ksm = np.zeros((1, 8, 64, 8, 128), np.float32)
    vsm = np.zeros((1, 8, 64, 8, 128), np.float32)
    kis = np.zeros((1, 8, 64, 64), np.float32)
    pss = np.zeros((1, 8, 15, 1024), np.float32)
    for c in range(8):
        r = results[c]
        b, hf = c // 2, c % 2
        for i in range(NSUP):
            J = 2 * i + hf
            y[b, J * NT:(J + 1) * NT] = r["y"][i * NT:(i + 1) * NT]
        ys[c] = r["ys"]
        half = slice(hf * (SEQ // 2), (hf + 1) * (SEQ // 2))
        kp[0, b, half] = r["ko"][half].reshape(-1, 8, 128)
        vp[0, b, half] = r["vo"][half].reshape(-1, 8, 128)
        kip[0, b, half] = r["kio"][half]
        if hf == 1:
            pp[0, b] = r["po"][1:16]
        ksm[0, c] = r["kso"].reshape(64, 8, 128)
        vsm[0, c] = r["vso"].reshape(64, 8, 128)
        kis[0, c] = r["kiso"]
        pss[0, c] = r["pso"][1:16]
    return (y, ys, kp, vp, kip, pp, ksm, vsm, kis, pss)


def kernel(**inputs):
    inp = {k: np.asarray(v) for k, v in inputs.items()}
    if "nc" not in _CACHE:
        _CACHE["nc"] = Builder().build()
    nc = _CACHE["nc"]
    maps = _prep_inputs(inp)
    res = run_bass_kernel_spmd(nc, maps, core_ids=list(range(8)))
    return _assemble(res.results)
```

```python
from contextlib import ExitStack
import numpy as np
import concourse.bass as bass
import concourse.mybir as mybir
from concourse.bass_utils import run_bass_kernel_spmd

dt = mybir.dt
F32 = dt.float32
BF16 = dt.bfloat16
U8 = dt.uint8
AF = mybir.ActivationFunctionType
ALU = mybir.AluOpType
AX = mybir.AxisListType

D = 2048
DC = 16
SEQ = 8192
NT = 256
NSUP = 16
DFF = 5632
FC = 44
NKS = 2112
VW = 130
EPS = 1e-6
NBIS = 26
ENGS = ("pe", "act", "dve", "pool", "sp")


class _Op:
    __slots__ = ("eng", "fn", "reads", "writes", "dma", "deps", "sig", "ticket", "dsem", "dval", "dprev")

    def __init__(self, eng, fn, reads, writes, dma):
        self.eng = eng
        self.fn = fn
        self.reads = reads
        self.writes = writes
        self.dma = dma
        self.deps = ()
        self.sig = False
        self.ticket = 0
        self.dsem = None
        self.dval = 0
        self.dprev = None


class Sched:
    def __init__(self, nc):
        self.nc = nc
        self.ops = []
        self.dma_slots = {"sp": 16, "act": 8, "pool": 4}
        self.regions = {}

    def overlaps(self, k, cache={}):
        r = self.regions.get(k)
        if r is None:
            return (k,)
        c = self._ovc.get(k)
        if c is None:
            c = tuple(k2 for k2, r2 in self.regions.items() if r2[0] == r[0] and r2[1] < r[2] and r[1] < r2[2])
            self._ovc[k] = c
        return c

    def op(self, eng, fn, r=(), w=()):
        self.ops.append(_Op(eng, fn, tuple(r), tuple(w), False))

    def dma(self, q, out, in_, r=(), w=(), **kw):
        def fn(e, out=out, in_=in_, kw=kw):
            return e.dma_start(out=out, in_=in_, **kw)
        self.ops.append(_Op(q, fn, tuple(r), tuple(w), True))

    def finalize(self, stack):
        nc = self.nc
        ops = self.ops
        last_w = {}
        readers = {}
        self._ovc = {}
        for i, op in enumerate(ops):
            deps = {}
            for k0 in op.reads:
                for k in self.overlaps(k0):
                    j = last_w.get(k)
                    if j is not None:
                        deps[j] = "RAW"
                    if self.regions.get(k, ("",))[0] == "P":
                        for j in readers.get(k, ()):
                            if ops[j].eng != op.eng and j not in deps:
                                deps[j] = "RAR"
            for k0 in op.writes:
                for k in self.overlaps(k0):
                    j = last_w.get(k)
                    if j is not None and j not in deps:
                        deps[j] = "WAW"
                    for j in readers.get(k, ()):
                        if j not in deps:
                            deps[j] = "WAR"
            keep = []
            for j, kind in deps.items():
                pj = ops[j]
                if pj.dma:
                    keep.append(j)
                elif (not op.dma) and pj.eng == op.eng:
                    if op.eng == "pe":
                        continue
                    keep.append(j)
                    pj.sig = True
                else:
                    keep.append(j)
                    pj.sig = True
            op.deps = keep
            for k in op.reads:
                lst = readers.get(k)
                if lst is None:
                    readers[k] = [i]
                else:
                    if not op.dma:
                        lst[:] = [x for x in lst if ops[x].dma or ops[x].eng != op.eng]
                    lst.append(i)
            for k in op.writes:
                last_w[k] = i
                readers[k] = []
        csem = {e: stack.enter_context(nc.semaphore("c_" + e)) for e in ("pe", "act", "dve", "pool")}
        dsems = {q: [stack.enter_context(nc.semaphore("d_%s%d" % (q, s))) for s in range(n)]
                 for q, n in self.dma_slots.items()}
        tick = {e: 0 for e in ENGS}
        dcount = {q: 0 for q in self.dma_slots}
        slot_last = {}
        for op in ops:
            if op.dma:
                n = dcount[op.eng]
                K = self.dma_slots[op.eng]
                op.dsem = dsems[op.eng][n % K]
                op.dval = 16 * (n // K + 1)
                op.dprev = slot_last.get((op.eng, n % K))
                slot_last[(op.eng, n % K)] = op
                dcount[op.eng] = n + 1
            elif op.sig:
                tick[op.eng] += 1
                op.ticket = tick[op.eng]
        self.stats = {"n_ops": len(ops), "ticks": dict(tick), "dmas": dict(dcount)}
        final_waits = [(o.dsem, o.dval) for o in slot_last.values()]

        def emit(name, e):
            waited = {}

            def wait(sem, val):
                key = id(sem)
                if waited.get(key, 0) < val:
                    e.wait_ge(sem, val)
                    waited[key] = val
            for op in ops:
                if op.eng != name:
                    continue
                need = {}
                for j in op.deps:
                    pj = ops[j]
                    sem, val = (pj.dsem, pj.dval) if pj.dma else (csem[pj.eng], pj.ticket)
                    if need.get(id(sem), (None, 0))[1] < val:
                        need[id(sem)] = (sem, val)
                if op.dma and op.dprev is not None:
                    sem, val = op.dprev.dsem, op.dprev.dval
                    if need.get(id(sem), (None, 0))[1] < val:
                        need[id(sem)] = (sem, val)
                for sem, val in need.values():
                    wait(sem, val)
                if op.dma:
                    op.fn(e).then_inc(op.dsem, 16)
                else:
                    ins = op.fn(e)
                    if op.sig:
                        ins.then_inc(csem[name], 1)
            if name == "sp":
                for sem, val in final_waits:
                    wait(sem, val)

        with nc.Block() as block:
            @block.tensor
            def _(e):
                emit("pe", e)

            @block.scalar
            def _(e):
                emit("act", e)

            @block.vector
            def _(e):
                emit("dve", e)

            @block.gpsimd
            def _(e):
                emit("pool", e)

            @block.sync
            def _(e):
                emit("sp", e)


class Arena:
    def __init__(self, t, size, sched):
        self.t = t
        self.size = size
        self.off = 0
        self.sched = sched

    def alloc(self, key, nbytes, dtype, pattern=None, **kw):
        off = (self.off + 63) // 64 * 64
        assert off + nbytes <= self.size, ("SBUF arena overflow", key, off, nbytes, self.size)
        assert key not in self.sched.regions, key
        self.sched.regions[key] = ("S", off, off + nbytes)
        self.off = off + nbytes
        ap = self.t[:, off:off + nbytes].bitcast(dtype)
        if pattern:
            ap = ap.rearrange(pattern, **kw)
        return ap


class Builder:
    def __init__(self, phases=("ada", "k", "q")):
        self.phases = phases
        self.nc = bass.Bass("TRN2", target_bir_lowering=False)
        self.S = Sched(self.nc)
        self.uid = 0

    def mm(self, out, lhsT, rhs, start, stop, r, w):
        self.S.op("pe", lambda e, o=out, l=lhsT, rh=rhs, s=start, t=stop: e.matmul(o, l, rh, start=s, stop=t), r, w)

    def tr(self, out, in_, ident, r, w):
        self.S.op("pe", lambda e, o=out, i=in_, d=ident: e.transpose(o, i, d), r, w)

    def act(self, out, in_, func, r, w, **kw):
        self.S.op("act", lambda e, o=out, i=in_, f=func, kw=kw: e.activation(o, i, f, **kw), r, w)

    def ts(self, eng, out, in0, s1, s2, op0, op1, r, w, accum_out=None):
        def fn(e, o=out, i=in0, s1=s1, s2=s2, op0=op0, op1=op1, a=accum_out):
            kw = {}
            if op1 is not None:
                kw["op1"] = op1
            if a is not None:
                kw["accum_out"] = a
            return e.tensor_scalar(o, i, s1, s2, op0, **kw)
        self.S.op(eng, fn, r, w)

    def tt(self, eng, out, in0, in1, op, r, w):
        self.S.op(eng, lambda e, o=out, a=in0, b=in1, p=op: e.tensor_tensor(o, a, b, p), r, w)

    def stt(self, out, in0, scalar, in1, op0, op1, r, w):
        self.S.op("dve", lambda e, o=out, a=in0, s=scalar, b=in1, p0=op0, p1=op1:
                  e.scalar_tensor_tensor(o, a, s, b, p0, p1), r, w)

    def cp(self, eng, out, in_, r, w):
        if eng == "act":
            self.S.op("act", lambda e, o=out, i=in_: e.copy(o, i), r, w)
        elif getattr(self, "cp_ts", False):
            self.ts(eng, out, in_, 1.0, None, ALU.mult, None, r, w)
        else:
            self.S.op(eng, lambda e, o=out, i=in_: e.tensor_copy(o, i), r, w)

    def ps(self, bank, cols=512):
        return self.psum[:, bank * 512: bank * 512 + cols]

    def psb(self, bank):
        return self.psum[:, bank * 512:(bank + 1) * 512].bitcast(BF16)

    def build(self):
        nc = self.nc
        S = self.S

        def din(name, shape, d=F32):
            return nc.dram_tensor(name, list(shape), d, kind="ExternalInput").ap()

        def dout(name, shape):
            return nc.dram_tensor(name, list(shape), F32, kind="ExternalOutput").ap()

        def dscr(name, shape, d):
            return nc.dram_tensor(name, list(shape), d, kind="Internal").ap()

        I = self.I = {}
        for name, shape in [
            ("xk", (SEQ, D)), ("xq", (NSUP * NT, D)), ("xh", (NSUP * 16, D)), ("xs", (64, D)),
            ("c2", (128, 32)), ("ck", (2048, 1024)), ("cv", (2048, 1024)), ("cki", (2048, 64)),
            ("spool", (16, 1024)), ("w_ada", (D, 6 * D)), ("b_ada", (1, 6 * D)), ("b_fm", (128, 96)),
            ("g1", (128, 16)), ("g2", (128, 16)), ("w_in", (D, 5200)), ("w_pool", (1024, 256)),
            ("pscale", (128, 8)), ("w_out", (D, D)), ("w_gate", (D, DFF)), ("w_up", (D, DFF)),
            ("w_down", (DFF, D)), ("gfin", (1, D)), ("negmask", (256, 512)), ("hflag", (128, 16)),
            ("invcnt", (128, NSUP * 64)), ("ident", (128, 128)),
        ]:
            I[name] = din(name, shape)
        O = self.O = {}
        for name, shape in [
            ("y", (NSUP * NT, D)), ("ys", (64, D)), ("ko", (SEQ, 1024)), ("vo", (SEQ, 1024)), ("kio", (SEQ, 64)),
            ("po", (16, 1024)), ("kso", (64, 1024)), ("vso", (64, 1024)), ("kiso", (64, 64)), ("pso", (16, 1024)),
        ]:
            O[name] = dout(name, shape)
        X = self.X = {}
        X["mods"] = dscr("mods", (2, 2 * D), F32)
        X["winq"] = dscr("winq", (12, 128, 16, 256), BF16)
        X["wout"] = dscr("wout", (4, 128, 16, 512), BF16)
        X["wg"] = dscr("wg", (22, 128, 16, 256), BF16)
        X["wu"] = dscr("wu", (22, 128, 16, 256), BF16)
        X["wd"] = dscr("wd", (4, 4, 128, 11, 512), BF16)
        X["kT"] = dscr("kT", (SEQ // 512, 128, 8, 512), BF16)
        X["v"] = dscr("v", (SEQ // 512, 128, 4, 8 * VW), BF16)
        X["kiT"] = dscr("kiT", (64, SEQ), BF16)
        X["kTs"] = dscr("kTs", (5, 128, 8, 512), BF16)
        X["vs"] = dscr("vs", (5, 128, 4, 8 * VW), BF16)
        X["kiTs"] = dscr("kiTs", (64, 2176), BF16)

        with ExitStack() as st:
            ARENA = 206 * 1024
            arena_t = st.enter_context(nc.sbuf_tensor("arena", [128, ARENA], U8))
            self.psum = st.enter_context(nc.psum_tensor("psum", [128, 4096], F32))
            A = self.A = Arena(arena_t, ARENA, S)
            for b_ in range(8):
                S.regions["ps%d" % b_] = ("P", b_ * 2048, (b_ + 1) * 2048)
            S.regions["ps0h0"] = ("P", 0, 1024)
            S.regions["ps0h1"] = ("P", 1024, 2048)
            S.regions["ps2h0"] = ("P", 4096, 5120)
            S.regions["ps2h1"] = ("P", 5120, 6144)
            self.alloc_persistent()
            self.emit_consts()
            self.emit_weight_casts()
            base = A.off
            if "ada" in self.phases:
                self.emit_ada()
            A.off = base
            if "k" in self.phases:
                self.emit_kphase()
            A.off = base
            if "nocast" not in self.phases:
                self.pop_casts(1000)
            if "q" in self.phases:
                self.emit_qphase()
            S.finalize(st)
        return nc

    def barrier(self):
        S = self.S
        keys = set()
        for op in S.ops:
            keys.update(op.reads)
            keys.update(op.writes)
        keys = sorted(keys)
        t = self.dummy
        S.op("pool", lambda e: e.memset(t[0:1, 0:1], 0.0), r=keys, w=keys + ["dummy"])

    def alloc_persistent(self):
        A = self.A
        self.ident_f = A.alloc("ident_f", 512, F32)
        self.ident_b = A.alloc("ident_b", 256, BF16)
        self.dummy = A.alloc("dummy", 64, F32)
        self.AB = A.alloc("AB", 2 * 4 * 16 * 4, F32, "p (r v c) -> p r v c", r=2, v=4)
        self.hflag = A.alloc("hflag", 64, F32)
        self.invcnt = A.alloc("invcnt", NSUP * 64 * 4, F32, "p (i g t) -> p i g t", i=NSUP, g=4)
        self.pscale = A.alloc("pscale", 32, F32)
        self.g1 = A.alloc("g1", 64, F32)
        self.g2 = A.alloc("g2", 64, F32)
        self.bfm = A.alloc("bfm", 96 * 4, F32)
        self.half = A.alloc("half", 64, F32)
        self.epsb = A.alloc("epsb", 64, F32)
        self.wwi = A.alloc("wwi", 16 * 16 * 2, BF16, "p (c n) -> p c n", c=16)
        self.wpool = A.alloc("wpool", 8 * 256 * 2, BF16, "p (g e) -> p g e", g=8)
        self.gate_b = [A.alloc("gate_b0", D * 4, F32), A.alloc("gate_b1", D * 4, F32)]
        self.gfin_b = A.alloc("gfin_b", D * 4, F32)
        self.stat = [A.alloc("stat%d" % i, 64, F32) for i in range(4)]
        self.junk = A.alloc("junk", D * 2, BF16)
        self.xn = [A.alloc("xn%d" % i, D * 2, BF16) for i in range(2)]

    def emit_consts(self):
        S = self.S
        I = self.I
        S.dma("sp", self.ident_f, I["ident"], w=["ident_f"])
        self.cp("dve", self.ident_b, self.ident_f, r=["ident_f"], w=["ident_b"])
        S.dma("sp", self.hflag[:, 0:16], I["hflag"], w=["hflag"])
        S.dma("sp", self.invcnt, I["invcnt"].rearrange("p (i g t) -> p i g t", i=NSUP, g=4), w=["invcnt"])
        S.dma("sp", self.pscale[:, 0:8], I["pscale"], w=["pscale"])
        S.dma("sp", self.g1[:, 0:16], I["g1"], w=["g1"])
        S.dma("sp", self.g2[:, 0:16], I["g2"], w=["g2"])
        S.dma("sp", self.bfm, I["b_fm"], w=["bfm"])
        S.dma("sp", self.gfin_b, I["gfin"][0:1, :].to_broadcast([128, D]), w=["gfin_b"])
        S.op("pool", lambda e: e.memset(self.half[:, 0:1], 0.5), w=["half"])
        S.op("pool", lambda e: e.memset(self.epsb[:, 0:1], EPS), w=["epsb"])
        S.dma("pool", self.wwi, I["w_in"][:, 4160:4176].rearrange("(c p) n -> p c n", p=128), w=["wwi"])
        S.dma("pool", self.wpool, I["w_pool"].rearrange("(g p) e -> p g e", p=128), w=["wpool"])

    def emit_weight_casts(self):
        I = self.I
        X = self.X
        jobs = self.cast_jobs = []

        class _J:
            @staticmethod
            def dma(q, out, in_, w):
                jobs.append((q, out, in_, w))
        S = _J
        col0 = [0, 256, 512, 768, 3072, 3328, 3584, 3840, 4176, 4432, 4688, 4944]
        for j in range(12):
            S.dma("pool", X["winq"][j], I["w_in"][:, col0[j]:col0[j] + 256].rearrange("(c p) n -> p c n", p=128),
                  w=["winq%d" % j])
        for n in range(4):
            S.dma("pool", X["wout"][n], I["w_out"][:, n * 512:(n + 1) * 512].rearrange("(c p) n -> p c n", p=128),
                  w=["wout%d" % n])
        for b in range(22):
            S.dma("pool", X["wg"][b], I["w_gate"][:, b * 256:(b + 1) * 256].rearrange("(c p) n -> p c n", p=128),
                  w=["wg%d" % b])
            S.dma("pool", X["wu"][b], I["w_up"][:, b * 256:(b + 1) * 256].rearrange("(c p) n -> p c n", p=128),
                  w=["wu%d" % b])
        for n in range(4):
            for q in range(4):
                S.dma("pool", X["wd"][n, q],
                      I["w_down"][q * 1408:(q + 1) * 1408, n * 512:(n + 1) * 512].rearrange("(k p) m -> p k m", p=128),
                      w=["wd%d_%d" % (n, q)])

    def emit_ada(self):
        S = self.S
        I = self.I
        A = self.A
        c2t = A.alloc("c2t", 128, F32)
        sc = A.alloc("sc", 128, F32)
        wa = [A.alloc("wa%d" % i, 16 * 512 * 4, F32, "p (c n) -> p c n", c=16) for i in range(2)]
        bb = [A.alloc("bb%d" % i, 2048, F32) for i in range(2)]
        mo = [A.alloc("mo%d" % i, 2048, F32) for i in range(2)]
        modfm = A.alloc("modfm", 4 * 16 * 2 * 4, F32, "p (v c r) -> p v c r", v=4, c=16)
        S.dma("sp", c2t, I["c2"], w=["c2t"])
        self.act(sc, c2t, AF.Silu, r=["c2t"], w=["sc"])
        vec_of_blk = {0: 0, 1: 0, 2: 0, 3: 0, 4: 1, 5: 1, 6: 1, 7: 1, 12: 2, 13: 2, 14: 2, 15: 2, 16: 3, 17: 3, 18: 3, 19: 3}
        gate_of_blk = {8: 0, 9: 0, 10: 0, 11: 0, 20: 1, 21: 1, 22: 1, 23: 1}
        for blk in range(24):
            b = blk % 2
            if "nocast" not in self.phases:
                self.pop_casts(2)
            S.dma("sp", wa[b], I["w_ada"][:, blk * 512:(blk + 1) * 512].rearrange("(c p) n -> p c n", p=128),
                  w=["wa%d" % b])
            if blk in gate_of_blk:
                g = gate_of_blk[blk]
                S.dma("sp", bb[b][0:2, :], I["b_ada"][0:1, blk * 512:(blk + 1) * 512].to_broadcast([2, 512]),
                      w=["bb%d" % b])
                bank = 2 + b
                for k in range(16):
                    self.mm(self.ps(bank)[0:2, :], sc[:, 2 * k:2 * k + 2], wa[b][:, k, :], k == 0, k == 15,
                            r=["sc", "wa%d" % b], w=["ps%d" % bank])
                self.tt("dve", mo[b][0:2, :], self.ps(bank)[0:2, :], bb[b][0:2, :], ALU.add,
                        r=["ps%d" % bank, "bb%d" % b], w=["mo%d" % b])
                off = g * D + (blk % 4) * 512
                S.dma("sp", self.X["mods"][0:2, off:off + 512], mo[b][0:2, :], r=["mo%d" % b], w=["mods"])
            else:
                v = vec_of_blk[blk]
                bank = 4 + b
                for m in range(4):
                    for k in range(16):
                        self.mm(self.ps(bank)[:, 2 * m:2 * m + 2], wa[b][:, k, m * 128:(m + 1) * 128],
                                sc[:, 2 * k:2 * k + 2], k == 0, k == 15,
                                r=["sc", "wa%d" % b], w=["ps%d" % bank])
                for m in range(4):
                    cc = (blk % 4) * 4 + m
                    gcol = blk * 4 + m
                    self.ts("dve", modfm[:, v, cc, :], self.ps(bank)[:, 2 * m:2 * m + 2],
                            self.bfm[:, gcol:gcol + 1], None, ALU.add, None,
                            r=["ps%d" % bank, "bfm"], w=["modfm"])
        for row in range(2):
            self.stt(self.AB[:, row, 0, :], modfm[:, 1, :, row], 1.0, self.g1[:, 0:16], ALU.add, ALU.mult,
                     r=["modfm", "g1"], w=["AB"])
            self.cp("dve", self.AB[:, row, 1, :], modfm[:, 0, :, row], r=["modfm"], w=["AB"])
            self.stt(self.AB[:, row, 2, :], modfm[:, 3, :, row], 1.0, self.g2[:, 0:16], ALU.add, ALU.mult,
                     r=["modfm", "g2"], w=["AB"])
            self.cp("dve", self.AB[:, row, 3, :], modfm[:, 2, :, row], r=["modfm"], w=["AB"])

    def pop_casts(self, n):
        if not hasattr(self, "cstage"):
            A = self.A
            o = A.off
            A.off = A.size - 2 * 16384 - 128
            self.cstage = [A.alloc("cstage%d" % i, 16384, BF16) for i in range(2)]
            A.off = o
            self.cast_i = 0
        for _ in range(n):
            if self.cast_jobs:
                q, out, in_, w = self.cast_jobs.pop(0)
                b = self.cast_i % 2
                self.cast_i += 1
                c, n_ = out.shape[1], out.shape[2]
                stg = self.cstage[b][:, 0:c * n_].rearrange("p (c n) -> p c n", c=c)
                self.S.dma("pool", stg, in_, w=["cstage%d" % b])
                prev = getattr(self, "cast_pending", None)
                if prev is not None:
                    self.S.dma("pool", prev[0], prev[1], r=[prev[2]], w=prev[3])
                self.cast_pending = (out, stg, "cstage%d" % b, w)
        if not self.cast_jobs and getattr(self, "cast_pending", None) is not None:
            prev = self.cast_pending
            self.S.dma("pool", prev[0], prev[1], r=[prev[2]], w=prev[3])
            self.cast_pending = None

    def load_gates(self, row):
        S = self.S
        for g in range(2):
            S.dma("sp", self.gate_b[g], self.X["mods"][row:row + 1, g * D:(g + 1) * D].to_broadcast([128, D]),
                  r=["mods"], w=["gate_b%d" % g])

    def norm(self, x_sb, ntok, row, which, hT_dst, xkey, hkey, sidx, banks=(1, 2)):
        self.norm_a(x_sb, ntok, xkey, sidx)
        self.norm_b(ntok, row, which, hT_dst, hkey, sidx, banks)

    def norm_a(self, x_sb, ntok, xkey, sidx):
        stt_ = self.stat[sidx]
        skey = "stat%d" % sidx
        xn = self.xn[sidx % 2]
        xnkey = "xn%d" % (sidx % 2)
        self.act(self.junk[:ntok, :], x_sb, AF.Square, r=[xkey], w=["junk", skey], accum_out=stt_[:ntok, 0:1])
        self.act(stt_[:ntok, 1:2], stt_[:ntok, 0:1], AF.Sqrt, r=[skey, "epsb"], w=[skey], scale=1.0 / D, bias=self.epsb[:ntok, 0:1])
        self.S.op("dve", lambda e, o=stt_[:ntok, 2:3], i=stt_[:ntok, 1:2]: e.reciprocal(o, i), r=[skey], w=[skey])
        self.ts("dve", xn[:ntok, :], x_sb, stt_[:ntok, 2:3], None, ALU.mult, None, r=[xkey, skey], w=[xnkey])

    def norm_b(self, ntok, row, which, hT_dst, hkey, sidx, banks=(1, 2)):
        xn = self.xn[sidx % 2]
        xnkey = "xn%d" % (sidx % 2)
        A_ = self.AB[:, row, 2 * which, :]
        B_ = self.AB[:, row, 2 * which + 1, :]
        for c in range(16):
            bank = banks[c // 8]
            tp = self.psb(bank)[:, (c % 8) * 128:(c % 8) * 128 + ntok]
            self.tr(tp, xn[:ntok, c * 128:(c + 1) * 128], self.ident_b[:ntok, :ntok], r=[xnkey, "ident_b"],
                    w=["ps%d" % bank])
            if c < 8:
                self.act(hT_dst(c), tp, AF.Identity, r=["ps%d" % bank, "AB"], w=[hkey],
                         scale=A_[:, c:c + 1], bias=B_[:, c:c + 1])
            else:
                self.ts("dve", hT_dst(c), tp, A_[:, c:c + 1], B_[:, c:c + 1], ALU.mult, ALU.add,
                        r=["ps%d" % bank, "AB"], w=[hkey])

    def emit_kphase(self):
        S = self.S
        I = self.I
        O = self.O
        X = self.X
        A = self.A
        wkv = A.alloc("wkv", 16 * 2112 * 2, BF16, "p (c n) -> p c n", c=16)
        S.dma("pool", wkv[:, :, 0:1024], I["w_in"][:, 1024:2048].rearrange("(c p) n -> p c n", p=128), w=["wkv"])
        S.dma("pool", wkv[:, :, 1024:2048], I["w_in"][:, 2048:3072].rearrange("(c p) n -> p c n", p=128), w=["wkv"])
        S.dma("pool", wkv[:, :, 2048:2112], I["w_in"][:, 4096:4160].rearrange("(c p) n -> p c n", p=128), w=["wkv"])
        xt = [A.alloc("xt%d" % i, D * 4, F32) for i in range(2)]
        hTk = [A.alloc("hTk%d" % i, 16 * 128 * 2, BF16, "p (c t) -> p c t", c=16) for i in range(2)]
        kf = [A.alloc("kf%d" % i, 1024 * 4, F32) for i in range(2)]
        vf = [A.alloc("vf%d" % i, 1024 * 4, F32) for i in range(2)]
        kif = [A.alloc("kif%d" % i, 64 * 4, F32) for i in range(2)]
        kb = [A.alloc("kb%d" % i, 1024 * 2, BF16) for i in range(2)]
        vb = [A.alloc("vb%d" % i, 8 * VW * 2, BF16, "p (h d) -> p h d", h=8) for i in range(2)]
        kib = [A.alloc("kib%d" % i, 64 * 2, BF16) for i in range(2)]
        kTt = [A.alloc("kTt%d" % i, 8 * 128 * 2, BF16, "p (h t) -> p h t", h=8) for i in range(2)]
        kiTt = [A.alloc("kiTt%d" % i, 128 * 2, BF16) for i in range(2)]
        for p in range(2):
            S.op("pool", lambda e, t=vb[p]: e.memset(t[:, :, 128:VW], 1.0), w=["vb%d" % p])

        def post(par, ntok, ksrc, vsrc, kisrc, srckeys, tok0, kT_d, v_d, kiT_d, outs, part="ab"):
            sfx = str(par)
            if part != "b":
                if ksrc is not None:
                    for blk in range(2):
                        self.cp("act", kf[par][:ntok, blk * 512:(blk + 1) * 512], ksrc(blk), r=["ps%d" % (3 + blk)],
                                w=["kf" + sfx])
                        self.cp("act", vf[par][:ntok, blk * 512:(blk + 1) * 512], vsrc(blk), r=["ps%d" % (5 + blk)],
                                w=["vf" + sfx])
                    self.cp("act", kif[par][:ntok, 0:64], kisrc, r=["ps7"], w=["kif" + sfx])
                if outs is not None:
                    S.dma("sp", outs[0], kf[par][:ntok, :], r=["kf" + sfx])
                    S.dma("sp", outs[1], vf[par][:ntok, :], r=["vf" + sfx])
                    S.dma("sp", outs[2], kif[par][:ntok, 0:64], r=["kif" + sfx])
                self.cp("dve", kb[par][:ntok, :], kf[par][:ntok, :], r=["kf" + sfx], w=["kb" + sfx])
                self.cp("pool", vb[par][:ntok, :, 0:128], vf[par][:ntok, :].rearrange("p (h d) -> p h d", h=8),
                        r=["vf" + sfx], w=["vb" + sfx])
                self.cp("dve", kib[par][:ntok, 0:64], kif[par][:ntok, 0:64], r=["kif" + sfx], w=["kib" + sfx])
                S.dma("sp", v_d[tok0 // 512][:ntok, (tok0 % 512) // 128, :], vb[par][:ntok].rearrange("p h d -> p (h d)"),
                      r=["vb" + sfx], w=["vscr"])
            if part == "a":
                return
            tpk = self.psb(0)
            for h in range(8):
                self.tr(tpk[:, h * 128:h * 128 + ntok], kb[par][:ntok, h * 128:(h + 1) * 128],
                        self.ident_b[:ntok, :ntok], r=["kb" + sfx, "ident_b"], w=["ps0"])
            self.cp("act", kTt[par][:, :, :ntok], tpk.rearrange("p (h t) -> p h t", h=8)[:, :, :ntok], r=["ps0"],
                    w=["kTt" + sfx])
            S.dma("sp", kT_d[tok0 // 512][:, :, tok0 % 512:tok0 % 512 + ntok], kTt[par][:, :, :ntok], r=["kTt" + sfx],
                  w=["kTscr"])
            tpi = self.psb(7)[0:64, 512:512 + ntok]
            self.tr(tpi, kib[par][:ntok, 0:64], self.ident_b[:ntok, :ntok], r=["kib" + sfx, "ident_b"], w=["ps7"])
            self.cp("dve", kiTt[par][0:64, :ntok], tpi, r=["ps7"], w=["kiTt" + sfx])
            S.dma("sp", kiT_d[:, tok0:tok0 + ntok], kiTt[par][0:64, :ntok], r=["kiTt" + sfx], w=["kiTscr"])

        def k_norm(t, x_rows, ntok, row, part="ab"):
            par = t % 2
            sfx = str(par)
            if x_rows is not None:
                S.dma("sp", xt[par][:ntok, :], x_rows, w=["xt" + sfx])
            if part != "b":
                self.norm_a(xt[par][:ntok, :], ntok, "xt" + sfx, par)
            if part != "a":
                self.norm_b(ntok, row, 0, lambda c: hTk[par][:, c, :ntok], "hTk" + sfx, par)

        def k_mm(t, ntok):
            par = t % 2
            sfx = str(par)
            for blk in range(5):
                ncols = 512 if blk < 4 else 64
                bank = 3 + blk
                for c in range(16):
                    self.mm(self.ps(bank)[:ntok, :ncols], hTk[par][:, c, :ntok], wkv[:, c, blk * 512:blk * 512 + ncols],
                            c == 0, c == 15, r=["hTk" + sfx, "wkv"], w=["ps%d" % bank])

        def k_post(t, ntok, tok0, kT_d, v_d, kiT_d, outs, part):
            post(t % 2, ntok, lambda blk: self.ps(3 + blk)[:ntok, :], lambda blk: self.ps(5 + blk)[:ntok, :],
                 self.ps(7)[:ntok, 0:64], None, tok0, kT_d, v_d, kiT_d, outs, part)

        nkt = self.n_ktiles if hasattr(self, "n_ktiles") else SEQ // 128

        def kouts(t):
            r0 = t * 128
            return (O["ko"][r0:r0 + 128, :], O["vo"][r0:r0 + 128, :], O["kio"][r0:r0 + 128, :])
        S.dma("sp", xt[0], I["xk"][0:128, :], w=["xt0"])
        if nkt > 1:
            S.dma("sp", xt[1], I["xk"][128:256, :], w=["xt1"])
        k_norm(0, None, 128, 0)
        for t in range(nkt):
            if "nocast" not in self.phases:
                self.pop_casts(2)
            if t + 1 < nkt:
                k_norm(t + 1, None, 128, 0, "a")
            if t + 2 < nkt:
                S.dma("sp", xt[t % 2], I["xk"][(t + 2) * 128:(t + 3) * 128, :], w=["xt%d" % (t % 2)])
            k_mm(t, 128)
            k_post(t, 128, t * 128, X["kT"], X["v"], X["kiT"], kouts(t), "a")
            if t + 1 < nkt:
                k_norm(t + 1, None, 128, 0, "b")
            if t >= 1:
                k_post(t - 1, 128, (t - 1) * 128, X["kT"], X["v"], X["kiT"], None, "b")
        k_post(nkt - 1, 128, (nkt - 1) * 128, X["kT"], X["v"], X["kiT"], None, "b")
        ncache = 0 if getattr(self, "skip_cache", False) else 16

        def cload(t):
            par = t % 2
            r0 = t * 128
            S.dma("sp", kf[par], I["ck"][r0:r0 + 128, :], w=["kf%d" % par])
            S.dma("sp", vf[par], I["cv"][r0:r0 + 128, :], w=["vf%d" % par])
            S.dma("sp", kif[par][:, 0:64], I["cki"][r0:r0 + 128, :], w=["kif%d" % par])
        if ncache:
            cload(0)
        for t in range(ncache):
            par = t % 2
            r0 = t * 128
            if t + 1 < ncache:
                cload(t + 1)
            post(par, 128, None, None, None, None, r0, X["kTs"], X["vs"], X["kiTs"], None)
        if not getattr(self, "skip_stile", False):
            k_norm(0, I["xs"][0:64, :], 64, 1)
            k_mm(0, 64)
            k_post(0, 64, 2048, X["kTs"], X["vs"], X["kiTs"],
                   (O["kso"][0:64, :], O["vso"][0:64, :], O["kiso"][0:64, :]), "ab")

    def emit_qphase(self):
        S = self.S
        I = self.I
        O = self.O
        X = self.X
        A = self.A
        NST = NT // 128
        xres = [A.alloc("xres%d" % i, D * 4, F32) for i in range(NST)]
        concatT = A.alloc("concatT", 16 * NT * 2, BF16, "p (c t) -> p c t", c=16)
        yt = [A.alloc("yt0", D * 4, F32)]
        _o = A.off
        A.off = _o - D * 4
        xh_t = A.alloc("xh_t", D * 4, F32)
        A.off = _o - D * 4
        oacc = A.alloc("oacc", 8 * 129 * 4, F32)
        A.off = _o
        nm = A.alloc("nm", NST * 512 * 4, F32, "p (s n) -> p s n", s=NST)
        for s_ in range(NST):
            S.dma("sp", nm[:, s_, :], I["negmask"][s_ * 128:(s_ + 1) * 128, :], w=["nm"])
        xbase = A.off
        hT = A.alloc("hT", 16 * NT * 2, BF16, "p (c t) -> p c t", c=16)
        hTh = A.alloc("hTh", 16 * 16 * 2, BF16, "p (c t) -> p c t", c=16)
        wq = [A.alloc("wq%d" % i, 16 * 256 * 2, BF16, "p (c n) -> p c n", c=16) for i in range(4)]
        xend = A.off
        A.off = xbase
        Ssb = A.alloc("Ssb", SEQ * 4, F32)
        junkA = A.alloc("junkA", SEQ // 2, U8)
        junkB = A.alloc("junkB", SEQ // 2, U8)
        xend = max(xend, A.off)
        A.off = xend
        ybase = A.off
        qT = A.alloc("qT", 8 * NT * 2, BF16, "p (h t) -> p h t", h=8)
        qiT = A.alloc("qiT", 16 * NT * 2, BF16, "p (h t) -> p h t", h=16)
        uT = A.alloc("uT", 8 * (NT + 16) * 4, F32, "p (c t) -> p c t", c=8)
        wi_sb = A.alloc("wi", NST * 16 * 4, F32, "p (s h) -> p s h", s=NST)
        dg = A.alloc("dg", 16 * 128 * 2, BF16, "p (h q) -> p h q", h=16)
        _d1 = A.off
        A.off = _d1 - 16 * 128 * 2
        pTm = [A.alloc("pTX%d" % i, 1024, BF16, "p (b q) -> p b q", b=4) for i in range(4)]
        A.off = _d1
        kic = [A.alloc("kic%d" % i, 1024, BF16) for i in range(2)]
        _k0 = A.off
        kTc = [A.alloc("kTc%d" % i, 8 * 512 * 2, BF16, "p (h k) -> p h k", h=8) for i in range(2)]
        Vc = [A.alloc("Vc%d" % i, 4 * 8 * VW * 2, BF16, "p (b n) -> p b n", b=4) for i in range(2)]
        _k1 = A.off
        A.off = _k0
        sA = A.alloc("sA", 8 * (NT + 16) * 4, F32, "p (c t) -> p c t", c=8)
        sB = A.alloc("sB", 8 * (NT + 16) * 4, F32, "p (c t) -> p c t", c=8)
        zT = A.alloc("zT", 8 * NT * 2, BF16, "p (c t) -> p c t", c=8)
        spl = A.alloc("spl", 1024 * 4, F32)
        ulast = A.alloc("ulast", 1024 * 4, F32)
        assert A.off <= _k1
        A.off = _k1
        mk = A.alloc("mk", 1024, BF16)
        mkT = [A.alloc("mkT%d" % i, 1024, BF16, "p (b q) -> p b q", b=4) for i in range(2)]
        Rr = [A.alloc("pTR%d" % i, 1024, BF16) for i in range(4)]
        pT = [Rr[i].rearrange("p (b q) -> p b q", b=4) for i in range(4)]
        ao = A.alloc("ao", 1024 * 2, BF16)
        bs = A.alloc("bs", 128, F32)
        bmid = A.alloc("bmid", 64, F32)
        bcA = A.alloc("bcA", 64, F32)
        bsB = A.alloc("bsB", 64, F32)
        bsel = A.alloc("bsel", 64, F32)
        top8 = A.alloc("top8", 64, F32)
        rec = A.alloc("rec", 64, F32)
        yend = A.off
        A.off = ybase
        actT = A.alloc("actT", FC * NT * 2, BF16, "p (f t) -> p f t", f=FC)
        wd = [A.alloc("wd%d" % i, 11 * 512 * 2, BF16, "p (k m) -> p k m", k=11) for i in range(2)]
        wo = [A.alloc("wo%d" % i, 16 * 512 * 2, BF16, "p (c n) -> p c n", c=16) for i in range(2)]
        tmpf = [A.alloc("tmpf%d" % i, 512 * 4, F32) for i in range(2)]
        sg = [A.alloc("sg%d" % i, NT * 4, F32) for i in range(2)]
        yend = max(yend, A.off)
        A.off = yend
        kY = ""

        def qtile(i, x_rows, ntok_all, row, halo, nk, kT_d, v_d, kiT_d, use_mask, y_rows, pool_out):
            nst = (ntok_all + 127) // 128
            nts = [min(128, ntok_all - s_ * 128) for s_ in range(nst)]
            self.load_gates(row) if (i == 0 or row == 1) else None
            for s_ in range(nst):
                nt_ = nts[s_]
                S.dma("sp", xres[s_][:nt_, :], x_rows[s_ * 128:s_ * 128 + nt_, :], w=["xres%d" % s_])
            for s_ in range(nst):
                self.norm_a(xres[s_][:nts[s_], :], nts[s_], "xres%d" % s_, 2 + s_ % 2)
            for s_ in range(nst):
                nt_ = nts[s_]
                self.norm_b(nt_, row, 0, lambda c, s_=s_, nt_=nt_: hT[:, c, s_ * 128:s_ * 128 + nt_], "hT", 2 + s_ % 2)
            if halo == "x":
                S.dma("sp", xh_t[0:16, :], I["xh"][i * 16:(i + 1) * 16, :], w=["xh_t"])
                self.norm(xh_t[0:16, :], 16, row, 0, lambda c: hTh[:, c, 0:16], "xh_t", "hTh", 0)
            sc_q = 128 ** -0.5
            sc_qi = (64 ** -0.5) * (16 ** -0.5)
            for j in range(12):
                b = j % 4
                S.dma("sp", wq[b], X["winq"][j], r=["winq%d" % j], w=["wq%d" % b])
                if j < 4:
                    for m in range(2):
                        bank = 3 + (2 * j + m) % 4
                        for c in range(16):
                            self.mm(self.ps(bank)[:, :ntok_all], wq[b][:, c, m * 128:(m + 1) * 128], hT[:, c, :ntok_all],
                                    c == 0, c == 15, r=["wq%d" % b, "hT"], w=["ps%d" % bank])
                        self.act(qT[:, 2 * j + m, :ntok_all], self.ps(bank)[:, :ntok_all], AF.Copy,
                                 r=["ps%d" % bank], w=[kY + "qT"], scale=sc_q)
                elif j < 8:
                    for m in range(4):
                        h = (j - 4) * 4 + m
                        bank = 3 + h % 4
                        for c in range(16):
                            self.mm(self.ps(bank)[0:64, :ntok_all], wq[b][:, c, m * 64:(m + 1) * 64], hT[:, c, :ntok_all],
                                    c == 0, c == 15, r=["wq%d" % b, "hT"], w=["ps%d" % bank])
                        self.ts("dve", qiT[0:64, h, :ntok_all], self.ps(bank)[0:64, :ntok_all], sc_qi, None, ALU.mult, None,
                                r=["ps%d" % bank], w=[kY + "qiT"])
                else:
                    for m in range(2):
                        ch = (j - 8) * 2 + m
                        bank = 3 + ch % 4
                        for c in range(16):
                            self.mm(self.ps(bank)[:, :ntok_all], wq[b][:, c, m * 128:(m + 1) * 128], hT[:, c, :ntok_all],
                                    c == 0, c == 15, r=["wq%d" % b, "hT"], w=["ps%d" % bank])
                        self.cp("act", uT[:, ch, 16:16 + ntok_all], self.ps(bank)[:, :ntok_all], r=["ps%d" % bank],
                                w=[kY + "uT"])
                        if halo == "x":
                            for c in range(16):
                                self.mm(self.ps(7)[:, 0:16], wq[b][:, c, m * 128:(m + 1) * 128], hTh[:, c, 0:16],
                                        c == 0, c == 15, r=["wq%d" % b, "hTh"], w=["ps7"])
                            self.ts("dve", uT[:, ch, 0:16], self.ps(7)[:, 0:16], self.hflag[:, i:i + 1], None, ALU.mult,
                                    None, r=["ps7", "hflag"], w=[kY + "uT"])
            if halo == "state":
                S.dma("sp", spl[0:16, :], I["spool"], w=[kY + "spl"])
                for ch in range(8):
                    self.tr(self.ps(7)[:, ch * 16:(ch + 1) * 16], spl[0:16, ch * 128:(ch + 1) * 128],
                            self.ident_f[0:16, 0:16], r=[kY + "spl", "ident_f"], w=["ps7"])
                self.cp("dve", uT[:, :, 0:16], self.ps(7)[:, 0:128].rearrange("p (c t) -> p c t", c=8), r=["ps7"],
                        w=[kY + "uT"])
            for s_ in range(nst):
                nt_ = nts[s_]
                for c in range(16):
                    self.mm(self.ps(7)[:nt_, 256:272], hT[:, c, s_ * 128:s_ * 128 + nt_], self.wwi[:, c, :],
                            c == 0, c == 15, r=["hT", "wwi"], w=["ps7"])
                self.cp("dve", wi_sb[:nt_, s_, :], self.ps(7)[:nt_, 256:272], r=["ps7"], w=[kY + "wi"])
            for s_ in range(nst):
                nq = nts[s_]
                q0 = s_ * 128
                for h in range(16):
                    self.ts("pool", dg[:nq, h, :nq], self.ident_b[:nq, :nq], wi_sb[:nq, s_, h:h + 1], None, ALU.mult, None,
                            r=["ident_b", kY + "wi"], w=[kY + "dg"])
                nblk = (nk + 511) // 512
                steps = []
                for kbk in range(nblk):
                    k0 = kbk * 512
                    wb = min(512, nk - k0)
                    halves = [(0, min(256, wb))] + ([(256, wb - 256)] if wb > 256 else [])
                    for hi_, (ho, hw) in enumerate(halves):
                        for hp in range(8):
                            steps.append((kbk, k0, wb, ho, hw, hp, hi_ == len(halves) - 1 and hp == 7))
                dbanks = [0, 2, 4, 6]
                LA = getattr(self, "idx_la", 2)
                NS = len(steps)
                for n in range(NS + LA):
                    if n < NS:
                        kbk, k0, wb, ho, hw, hp, _ = steps[n]
                        kb_ = kbk % 2
                        if ho == 0 and hp == 0:
                            S.dma("sp", kic[kb_][0:64, :wb], kiT_d[:, k0:k0 + wb], r=["kiTscr"], w=["kic%d" % kb_])
                        sl = n % 4
                        dbank = dbanks[sl]
                        dkey = "ps%d" % dbank
                        for j in range(2):
                            self.mm(self.ps(dbank)[:nq, j * 256:j * 256 + hw], qiT[0:64, 2 * hp + j, q0:q0 + nq],
                                    kic[kb_][0:64, ho:ho + hw], True, True, r=["qiT", "kic%d" % kb_], w=[dkey])
                        dps = self.ps(dbank)[:nq, :].rearrange("p (j k) -> p j k", j=2)[:, :, :hw]
                        rdst = Rr[sl][:nq, :].rearrange("p (j k) -> p j k", j=2)[:, :, :hw]
                        if sl < 2:
                            self.act(rdst, dps, AF.Relu, r=[dkey], w=["pTR%d" % sl])
                        else:
                            self.ts("dve", rdst, dps, 0.0, None, ALU.max, None, r=[dkey], w=["pTR%d" % sl])
                    m = n - LA
                    if m >= 0:
                        kbk, k0, wb, ho, hw, hp, lastb = steps[m]
                        sl = m % 4
                        sbank = 1 if kbk % 2 == 0 else 3
                        for j in range(2):
                            h = 2 * hp + j
                            self.mm(self.ps(sbank)[:nq, ho:ho + hw], dg[:nq, h, :nq], Rr[sl][:nq, j * 256:j * 256 + hw],
                                    h == 0, h == 15, r=["dg", "pTR%d" % sl], w=["ps%d" % sbank])
                        if lastb:
                            self.cp("dve", Ssb[:nq, k0:k0 + wb], self.ps(sbank)[:nq, :wb], r=["ps%d" % sbank], w=["Ssb"])
                bk = ["bs"]
                self.S.op("dve", lambda e, o=bs[:nq, 0:1], i_=Ssb[:nq, :nk]: e.tensor_reduce(o, i_, AX.X, ALU.min),
                          r=["Ssb"], w=bk)
                if use_mask:
                    self.tt("dve", Ssb[:nq, nk - 512:nk], Ssb[:nq, nk - 512:nk], nm[:nq, s_, :], ALU.add,
                            r=["Ssb", "nm"], w=["Ssb"])
                self.S.op("dve", lambda e, o=top8[:nq, 0:8], i_=Ssb[:nq, :nk]: e.max(o, i_), r=["Ssb"], w=["top8"])
                self.tt("dve", bs[:nq, 1:2], top8[:nq, 0:1], bs[:nq, 0:1], ALU.subtract, r=["top8"] + bk, w=bk)
                h1 = nk // 2 if nk >= 1024 else nk
                n2 = nk - h1
                for it in range(NBIS):
                    cfac = 0.5 ** (it + 1)
                    self.stt(bmid[:nq, 0:1], bs[:nq, 1:2], cfac, bs[:nq, 0:1], ALU.mult, ALU.add, r=bk, w=["bmid"])
                    self.ts("dve", junkA[:nq, :h1], Ssb[:nq, :h1], bmid[:nq, 0:1], 0.0, ALU.is_ge, ALU.add,
                            r=["bmid", "Ssb"], w=["bcA", "junkA"], accum_out=bcA[:nq, 0:1])
                    if n2 > 0:
                        self.act(junkB[:nq, :n2], Ssb[:nq, h1:nk], AF.Sign, r=["bmid", "Ssb"], w=["bsB", "junkB"],
                                 scale=-1.0, bias=bmid[:nq, 0:1], accum_out=bsB[:nq, 0:1])
                        self.stt(bsel[:nq, 0:1], bcA[:nq, 0:1], 2.0, bsB[:nq, 0:1], ALU.mult, ALU.subtract,
                                 r=["bcA", "bsB"], w=["bsel"])
                        self.ts("dve", bsel[:nq, 0:1], bsel[:nq, 0:1], 511.0 - n2, cfac, ALU.is_ge, ALU.mult,
                                r=["bsel"], w=["bsel"])
                    else:
                        self.ts("dve", bsel[:nq, 0:1], bcA[:nq, 0:1], 255.5, cfac, ALU.is_ge, ALU.mult, r=["bcA"], w=["bsel"])
                    self.stt(bs[:nq, 0:1], bsel[:nq, 0:1], bs[:nq, 1:2], bs[:nq, 0:1], ALU.mult, ALU.add,
                             r=bk + ["bsel"], w=bk)
                ngrp = (nk + 511) // 512
                pairs = [(g, h) for g in range(ngrp) for h in range(8)]
                NP = len(pairs)
                ginfo = {}
                for g in range(ngrp):
                    k0 = g * 512
                    wg_ = min(512, nk - k0)
                    nb = (wg_ + 127) // 128
                    ginfo[g] = (k0, wg_, nb, [min(128, wg_ - b_ * 128) for b_ in range(nb)])
                LAA = 3
                qbanks = [3, 4, 0, 1]
                for n in range(NP + LAA):
                    if n < NP:
                        g, h = pairs[n]
                        k0, wg_, nb, wbs = ginfo[g]
                        gb = g % 2
                        full = all(w_ == 128 for w_ in wbs)
                        if h == 0:
                            S.dma("sp", kTc[gb][:, :, :wg_], kT_d[g][:, :, :wg_], r=["kTscr"], w=["kTc%d" % gb])
                            if full:
                                S.dma("sp", Vc[gb][:, :nb, :], v_d[g][:, :nb, :], r=["vscr"], w=["Vc%d" % gb])
                            else:
                                for b_ in range(nb):
                                    S.dma("sp", Vc[gb][:wbs[b_], b_, :], v_d[g][:wbs[b_], b_, :],
                                          r=["vscr"], w=["Vc%d" % gb])
                            self.ts("dve", mk[:nq, :wg_], Ssb[:nq, k0:k0 + wg_], bs[:nq, 0:1], None, ALU.is_ge, None,
                                    r=["Ssb", "bs"], w=["mk"])
                            tpm = self.psb(2)
                            for b_ in range(nb):
                                self.tr(tpm[:wbs[b_], b_ * 128:b_ * 128 + nq], mk[:nq, b_ * 128:b_ * 128 + wbs[b_]],
                                        self.ident_b[:nq, :nq], r=["mk", "ident_b"], w=["ps2"])
                            if full:
                                self.cp("act", mkT[gb][:, :nb, :nq],
                                        tpm[:, :nb * 128].rearrange("p (b q) -> p b q", b=nb)[:, :, :nq], r=["ps2"],
                                        w=["mkT%d" % gb])
                            else:
                                for b_ in range(nb):
                                    self.cp("act", mkT[gb][:wbs[b_], b_, :nq], tpm[:wbs[b_], b_ * 128:b_ * 128 + nq],
                                            r=["ps2"], w=["mkT%d" % gb])
                        qb = qbanks[n % 4]
                        pb = n % 4
                        for b_ in range(nb):
                            self.mm(self.ps(qb)[:wbs[b_], b_ * 128:b_ * 128 + nq], kTc[gb][:, h, b_ * 128:b_ * 128 + wbs[b_]],
                                    qT[:, h, q0:q0 + nq], True, True, r=["kTc%d" % gb, "qT"], w=["ps%d" % qb])
                        if full:
                            self.act(pT[pb][:, :nb, :nq],
                                     self.ps(qb)[:, :nb * 128].rearrange("p (b q) -> p b q", b=nb)[:, :, :nq], AF.Exp,
                                     r=["ps%d" % qb], w=["pTR%d" % pb])
                            self.tt("pool" if h % 2 else "dve", pTm[pb][:, :nb, :nq], pT[pb][:, :nb, :nq], mkT[gb][:, :nb, :nq],
                                    ALU.mult, r=["pTR%d" % pb, "mkT%d" % gb], w=["pTX%d" % pb])
                        else:
                            for b_ in range(nb):
                                self.act(pT[pb][:wbs[b_], b_, :nq], self.ps(qb)[:wbs[b_], b_ * 128:b_ * 128 + nq], AF.Exp,
                                         r=["ps%d" % qb], w=["pTR%d" % pb])
                            for b_ in range(nb):
                                self.tt("dve", pTm[pb][:wbs[b_], b_, :nq], pT[pb][:wbs[b_], b_, :nq], mkT[gb][:wbs[b_], b_, :nq],
                                        ALU.mult, r=["pTR%d" % pb, "mkT%d" % gb], w=["pTX%d" % pb])
                    m = n - LAA
                    if m >= 0:
                        g, h = pairs[m]
                        k0, wg_, nb, wbs = ginfo[g]
                        gb = g % 2
                        pb = m % 4
                        ob = 5 + h // 3
                        oc = (h % 3) * 129
                        for b_ in range(nb):
                            self.mm(self.ps(ob)[:nq, oc:oc + 129], pTm[pb][:wbs[b_], b_, :nq],
                                    Vc[gb][:wbs[b_], b_, h * VW:h * VW + 129], b_ == 0, b_ == nb - 1,
                                    r=["pTX%d" % pb, "Vc%d" % gb], w=["ps%d" % ob])
                        if h == 7:
                            for ob in (5, 6, 7):
                                nh = 3 if ob < 7 else 2
                                c0 = (ob - 5) * 387
                                if g == 0:
                                    self.cp("dve", oacc[:nq, c0:c0 + nh * 129], self.ps(ob)[:nq, 0:nh * 129],
                                            r=["ps%d" % ob], w=["oacc"])
                                else:
                                    self.tt("dve", oacc[:nq, c0:c0 + nh * 129], oacc[:nq, c0:c0 + nh * 129],
                                            self.ps(ob)[:nq, 0:nh * 129], ALU.add, r=["ps%d" % ob, "oacc"], w=["oacc"])
                for h in range(8):
                    oc = h * 129
                    self.S.op("dve", lambda e, o=rec[:nq, h:h + 1], i_=oacc[:nq, oc + 128:oc + 129]: e.reciprocal(o, i_),
                              r=["oacc"], w=[kY + "rec"])
                    self.act(ao[:nq, h * 128:(h + 1) * 128], oacc[:nq, oc:oc + 128], AF.Copy,
                             r=["oacc", kY + "rec"], w=[kY + "ao"], scale=rec[:nq, h:h + 1])
                tpo = self.psb(2)
                for h in range(8):
                    self.tr(tpo[:, h * 128:h * 128 + nq], ao[:nq, h * 128:(h + 1) * 128], self.ident_b[:nq, :nq],
                            r=[kY + "ao", "ident_b"], w=["ps2"])
                self.cp("dve", concatT[:, 0:8, q0:q0 + nq], tpo.rearrange("p (h t) -> p h t", h=8)[:, :, :nq], r=["ps2"],
                        w=["concatT"])
            W = ntok_all + 16
            uk = kY + "uT"
            self.tt("pool", sA[:, :, 1:W], uT[:, :, 1:W], uT[:, :, 0:W - 1], ALU.add, r=[uk], w=[kY + "sA"])
            self.tt("pool", sB[:, 2:8, 3:W], sA[:, 2:8, 3:W], sA[:, 2:8, 1:W - 2], ALU.add, r=[kY + "sA"], w=[kY + "sB"])
            self.tt("pool", sA[:, 4:8, 7:W], sB[:, 4:8, 7:W], sB[:, 4:8, 3:W - 4], ALU.add, r=[kY + "sB"], w=[kY + "sA"])
            self.tt("pool", sB[:, 6:8, 15:W], sA[:, 6:8, 15:W], sA[:, 6:8, 7:W - 8], ALU.add, r=[kY + "sA"], w=[kY + "sB"])
            srcs = [sA, sB, sA, sB]
            for g in range(4):
                wdw = (2, 4, 8, 16)[g]
                self.stt(zT[:, 2 * g:2 * g + 2, :ntok_all], srcs[g][:, 2 * g:2 * g + 2, 16:W], 1.0 / wdw,
                         uT[:, 2 * g:2 * g + 2, 16:W], ALU.mult, ALU.subtract, r=[kY + "sA", kY + "sB", uk], w=[kY + "zT"])
                if halo == "x":
                    for cc in range(2):
                        ch = 2 * g + cc
                        self.tt("dve", sg[0][:, 0:16], srcs[g][:, ch, 16:32], self.invcnt[:, i, g, :], ALU.mult,
                                r=[kY + "sA", kY + "sB", "invcnt"], w=[kY + "sg0"])
                        self.tt("dve", zT[:, ch, 0:16], sg[0][:, 0:16], uT[:, ch, 16:32], ALU.subtract,
                                r=[kY + "sg0", uk], w=[kY + "zT"])
            for g in range(4):
                for ec in range(2):
                    bank = 3 + (2 * g + ec) % 4
                    for cc in range(2):
                        self.mm(self.ps(bank)[:, :ntok_all], self.wpool[:, 2 * g + cc, ec * 128:(ec + 1) * 128],
                                zT[:, 2 * g + cc, :ntok_all], cc == 0, cc == 1, r=["wpool", kY + "zT"], w=["ps%d" % bank])
                    self.ts("dve", concatT[:, 8 + 2 * g + ec, :ntok_all], self.ps(bank)[:, :ntok_all],
                            self.pscale[:, 2 * g + ec:2 * g + ec + 1], None, ALU.mult, None,
                            r=["ps%d" % bank, "pscale"], w=["concatT"])
            if pool_out is not None:
                for ch in range(8):
                    bk_ = 6 + ch // 4
                    self.tr(self.ps(bk_)[0:16, (ch % 4) * 128:(ch % 4 + 1) * 128], uT[:, ch, ntok_all:ntok_all + 16],
                            self.ident_f, r=[uk, "ident_f"], w=["ps%d" % bk_])
                for hh in range(2):
                    self.cp("dve", ulast[0:16, hh * 512:(hh + 1) * 512], self.ps(6 + hh)[0:16, :], r=["ps%d" % (6 + hh)],
                            w=[kY + "ulast"])
                S.dma("pool", pool_out, ulast[0:16, :], r=[kY + "ulast"])
            for n in range(4):
                b = n % 2
                S.dma("sp", wo[b], X["wout"][n], r=["wout%d" % n], w=[kY + "wo%d" % b])
                for s_ in range(nst):
                    nt_ = nts[s_]
                    bank = 3 + (n * nst + s_) % 4
                    tb = (n * nst + s_) % 2
                    for c in range(16):
                        self.mm(self.ps(bank)[:nt_, :], concatT[:, c, s_ * 128:s_ * 128 + nt_], wo[b][:, c, :],
                                c == 0, c == 15, r=["concatT", kY + "wo%d" % b], w=["ps%d" % bank])
                    self.tt("dve", tmpf[tb][:nt_, :], self.ps(bank)[:nt_, :], self.gate_b[0][:nt_, n * 512:(n + 1) * 512],
                            ALU.mult, r=["ps%d" % bank, "gate_b0"], w=[kY + "tmpf%d" % tb])
                    self.tt("pool", xres[s_][:nt_, n * 512:(n + 1) * 512], xres[s_][:nt_, n * 512:(n + 1) * 512],
                            tmpf[tb][:nt_, :], ALU.add, r=["xres%d" % s_, kY + "tmpf%d" % tb], w=["xres%d" % s_])
            for s_ in range(nst):
                nt_ = nts[s_]
                self.norm(xres[s_][:nt_, :], nt_, row, 1, lambda c, s_=s_, nt_=nt_: hT[:, c, s_ * 128:s_ * 128 + nt_],
                          "xres%d" % s_, "hT", 2 + s_ % 2)
            for fb in range(22):
                b = fb % 2
                S.dma("sp", wq[b], X["wg"][fb], r=["wg%d" % fb], w=["wq%d" % b])
                S.dma("sp", wq[2 + b], X["wu"][fb], r=["wu%d" % fb], w=["wq%d" % (2 + b)])
                for m in range(2):
                    f = fb * 2 + m
                    bg = 3 + (f % 2) * 2
                    bu = bg + 1
                    for c in range(16):
                        self.mm(self.ps(bg)[:, :ntok_all], wq[b][:, c, m * 128:(m + 1) * 128], hT[:, c, :ntok_all],
                                c == 0, c == 15, r=["wq%d" % b, "hT"], w=["ps%d" % bg])
                    for c in range(16):
                        self.mm(self.ps(bu)[:, :ntok_all], wq[2 + b][:, c, m * 128:(m + 1) * 128], hT[:, c, :ntok_all],
                                c == 0, c == 15, r=["wq%d" % (2 + b), "hT"], w=["ps%d" % bu])
                    self.act(sg[f % 2][:, :ntok_all], self.ps(bg)[:, :ntok_all], AF.Silu, r=["ps%d" % bg],
                             w=[kY + "sg%d" % (f % 2)])
                    self.tt("dve", actT[:, f, :ntok_all], sg[f % 2][:, :ntok_all], self.ps(bu)[:, :ntok_all], ALU.mult,
                            r=[kY + "sg%d" % (f % 2), "ps%d" % bu], w=[kY + "actT"])
            for n in range(4):
                for q in range(4):
                    b = (n * 4 + q) % 2
                    S.dma("sp", wd[b], X["wd"][n, q], r=["wd%d_%d" % (n, q)], w=[kY + "wd%d" % b])
                    for k in range(11):
                        kk = q * 11 + k
                        for s_ in range(nst):
                            nt_ = nts[s_]
                            bank = 3 + (n % 2) * 2 + s_
                            self.mm(self.ps(bank)[:nt_, :], actT[:, kk, s_ * 128:s_ * 128 + nt_], wd[b][:, k, :],
                                    kk == 0, kk == FC - 1, r=[kY + "actT", kY + "wd%d" % b], w=["ps%d" % bank])
                for s_ in range(nst):
                    nt_ = nts[s_]
                    bank = 3 + (n % 2) * 2 + s_
                    tb = s_ % 2
                    self.tt("dve", tmpf[tb][:nt_, :], self.ps(bank)[:nt_, :], self.gate_b[1][:nt_, n * 512:(n + 1) * 512],
                            ALU.mult, r=["ps%d" % bank, "gate_b1"], w=[kY + "tmpf%d" % tb])
                    self.tt("pool", xres[s_][:nt_, n * 512:(n + 1) * 512], xres[s_][:nt_, n * 512:(n + 1) * 512],
                            tmpf[tb][:nt_, :], ALU.add, r=["xres%d" % s_, kY + "tmpf%d" % tb], w=["xres%d" % s_])
            for s_ in range(nst):
                nt_ = nts[s_]
                stt_ = self.stat[s_ % 2]
                skey = "stat%d" % (s_ % 2)
                self.act(self.junk[:nt_, :], xres[s_][:nt_, :], AF.Square, r=["xres%d" % s_], w=["junk", skey],
                         accum_out=stt_[:nt_, 0:1])
                self.act(stt_[:nt_, 1:2], stt_[:nt_, 0:1], AF.Sqrt, r=[skey, "epsb"], w=[skey], scale=1.0 / D,
                         bias=self.epsb[:nt_, 0:1])
                self.S.op("dve", lambda e, o=stt_[:nt_, 2:3], i_=stt_[:nt_, 1:2]: e.reciprocal(o, i_), r=[skey], w=[skey])
                self.stt(yt[0][:nt_, :], xres[s_][:nt_, :], stt_[:nt_, 2:3], self.gfin_b[:nt_, :], ALU.mult, ALU.mult,
                         r=["xres%d" % s_, skey, "gfin_b"], w=["yt0"])
                S.dma("pool", y_rows[s_ * 128:s_ * 128 + nt_, :], yt[0][:nt_, :], r=["yt0"])

        nsup = self.n_sup if hasattr(self, "n_sup") else NSUP
        for i in range(nsup):
            nk = (2 * i + 2) * NT
            qtile(i, I["xq"][i * NT:(i + 1) * NT, :], NT, 0, "x", nk, X["kT"], X["v"], X["kiT"], True,
                  O["y"][i * NT:(i + 1) * NT, :], O["po"] if i == NSUP - 1 else None)
        if not hasattr(self, "skip_sample"):
            qtile(0, I["xs"], 64, 1, "state", NKS, X["kTs"], X["vs"], X["kiTs"], False, O["ys"], O["pso"])


_CACHE = {}


def _prep_inputs(inp):
    f = lambda a: np.ascontiguousarray(a, dtype=np.float32)
    x_prompt = inp["x_prompt"]
    w_ada = f(inp["w_ada"][0])
    b_ada = f(inp["b_ada"][0])
    shared = {
        "w_ada": w_ada, "b_ada": b_ada[None, :], "b_fm": f(b_ada.reshape(96, 128).T),
        "g1": f(inp["g_norm1"][0].reshape(16, 128).T), "g2": f(inp["g_norm2"][0].reshape(16, 128).T),
        "w_in": f(inp["w_in"][0]), "w_pool": f(inp["w_pool"][0].reshape(1024, 256)),
        "pscale": f(inp["pool_scale"][0].reshape(8, 128).T), "w_out": f(inp["w_out"][0]),
        "w_gate": f(inp["w_gate"][0]), "w_up": f(inp["w_up"][0]), "w_down": f(inp["w_down"][0]),
        "gfin": f(inp["g_final"][None, :]), "ident": np.eye(128, dtype=np.float32),
    }
    maps = []
    for c in range(8):
        b, hf = c // 2, c % 2
        xb = x_prompt[b]
        Js = [2 * i + hf for i in range(NSUP)]
        xq = np.concatenate([xb[J * NT:(J + 1) * NT] for J in Js], axis=0)
        xh = np.zeros((NSUP * 16, D), np.float32)
        hflag = np.zeros((128, 16), np.float32)
        invcnt = np.zeros((128, NSUP, 4, 16), np.float32)
        for i, J in enumerate(Js):
            if J > 0:
                xh[i * 16:(i + 1) * 16] = xb[J * NT - 16:J * NT]
                hflag[:, i] = 1.0
            for g, w in enumerate((2, 4, 8, 16)):
                pos = J * NT + np.arange(16)
                invcnt[:, i, g, :] = 1.0 / np.minimum(pos + 1, w)
        qch = (hf * NT + np.arange(NT)) // 64
        kch = np.arange(512) // 64
        negmask = np.where(kch[None, :] <= qch[:, None], 0.0, -1e30).astype(np.float32)
        c2 = np.stack([inp["c_prompt"][b].reshape(16, 128).T, inp["c_sample"][c].reshape(16, 128).T], axis=-1)
        spool = np.zeros((16, 1024), np.float32)
        spool[1:] = inp["state_pool"][0, c]
        m = dict(shared)
        m.update({
            "xk": f(xb), "xq": f(xq), "xh": xh, "xs": f(inp["x_sample"][c]), "c2": f(c2.reshape(128, 32)),
            "ck": f(inp["cache_k"][0, c].reshape(2048, 1024)), "cv": f(inp["cache_v"][0, c].reshape(2048, 1024)),
            "cki": f(inp["cache_kidx"][0, c]), "spool": spool, "negmask": negmask, "hflag": hflag,
            "invcnt": f(invcnt.reshape(128, NSUP * 64)),
        })
        maps.append(m)
    return maps


def _assemble(results):
    y = np.zeros((4, SEQ, D), np.float32)
    ys = np.zeros((8, 64, D), np.float32)
    kp = np.zeros((1, 4, SEQ, 8, 128), np.float32)
    vp = np.zeros((1, 4, SEQ, 8, 128), np.float32)
    kip = np.zeros((1, 4, SEQ, 64), np.float32)
    pp = np.zeros((1, 4, 15, 1024), np.float32)
    ksm = np.zeros((1, 8, 64, 8, 128), np.float32)
    vsm = np.zeros((1, 8, 64, 8, 128), np.float32)
    kis = np.zeros((1, 8, 64, 64), np.float32)
    pss = np.zeros((1, 8, 15, 1024), np.float32)
    for c in range(8):
        r = results[c]
        b, hf = c // 2, c % 2
        for i in range(NSUP):
            J = 2 * i + hf
            y[b, J * NT:(J + 1) * NT] = r["y"][i * NT:(i + 1) * NT]
        ys[c] = r["ys"]
        half = slice(hf * (SEQ // 2), (hf + 1) * (SEQ // 2))
        kp[0, b, half] = r["ko"][half].reshape(-1, 8, 128)
        vp[0, b, half] = r["vo"][half].reshape(-1, 8, 128)
        kip[0, b, half] = r["kio"][half]
        if hf == 1:
            pp[0, b] = r["po"][1:16]
        ksm[0, c] = r["kso"].reshape(64, 8, 128)
        vsm[0, c] = r["vso"].reshape(64, 8, 128)
        kis[0, c] = r["kiso"]
        pss[0, c] = r["pso"][1:16]
    return (y, ys, kp, vp, kip, pp, ksm, vsm, kis, pss)


def kernel(**inputs):
    inp = {k: np.asarray(v) for k, v in inputs.items()}
    if "nc" not in _CACHE:
        _CACHE["nc"] = Builder().build()
    nc = _CACHE["nc"]
    maps = _prep_inputs(inp)
    res = run_bass_kernel_spmd(nc, maps, core_ids=list(range(8)))
    return _assemble(res.results)
```
